# Optimizing a Trainium2 kernel written in Bass

```python
import math
import jax, jax.numpy as jnp
from jax import lax
import numpy as np

D_MODEL = 2048
BATCH = 2
SEQ = 16384
DEPTH = 1
DEC_BATCH = 8
DEC_SEQ = 16
PAST_LEN = 2048

CHUNK = 64
MIX_WIDTH = D_MODEL
DN_DK = 128
DN_DV = 128
DN_HEADS = (MIX_WIDTH // 2) // DN_DV
RET_DK = 256
RET_DV = 256
RET_HEADS = (MIX_WIDTH - DN_HEADS * DN_DV) // RET_DV
CONV_W = 4
CONV_CH = DN_HEADS * (2 * DN_DK + DN_DV)
D_FF = ((8 * D_MODEL + 3 * 256 - 1) // (3 * 256)) * 256
ROPE_BASE = 10000.0
EPS = 1e-6
N_ADA = 6
SPLIT_SIZES = (CONV_CH, DN_HEADS * DN_DV, DN_HEADS, DN_HEADS,
               RET_HEADS * RET_DK, RET_HEADS * RET_DK, RET_HEADS * RET_DV, RET_HEADS * RET_DV)
PROJ_WIDTH = CONV_CH + DN_HEADS * DN_DV + 2 * DN_HEADS + 2 * RET_HEADS * RET_DK + 2 * RET_HEADS * RET_DV

kernel_name = 'hybrid_gdn_retention_stream'

F32 = jnp.float32


def rmsnorm(x, w=None):
    xf = x.astype(F32)
    y = xf * lax.rsqrt(jnp.mean(xf * xf, axis=-1, keepdims=True) + EPS)
    if w is not None:
        y = y * w.astype(F32)
    return y


def modulate(h, shift, scale):
    return h * (1.0 + scale[:, None, :]) + shift[:, None, :]


def l2norm(x):
    return x * lax.rsqrt(jnp.sum(x * x, axis=-1, keepdims=True) + EPS)


def rotary(x, pos):
    d = x.shape[-1]
    inv = 1.0 / (ROPE_BASE ** jnp.linspace(0.0, 1.0, d // 2, dtype=F32))
    ang = pos.astype(F32)[:, None] * inv[None, :]
    cos = jnp.cos(ang)[None, :, None, :]
    sin = jnp.sin(ang)[None, :, None, :]
    x1, x2 = x[..., 0::2], x[..., 1::2]
    return jnp.stack([x1 * cos - x2 * sin, x2 * cos + x1 * sin], axis=-1).reshape(x.shape)


def to_blocks(a, chunk):
    b, t = a.shape[:2]
    a = a.reshape((b, t // chunk, chunk) + a.shape[2:])
    return jnp.moveaxis(a, 3, 2)


def from_blocks(o):
    o = jnp.moveaxis(o, 2, 3)
    b, n, c = o.shape[:3]
    return o.reshape((b, n * c) + o.shape[3:])


def gated_delta_chunked(q, k, v, g, beta, s0, chunk):
    dv = v.shape[-1]
    qb, kb, vb = to_blocks(q, chunk), to_blocks(k, chunk), to_blocks(v, chunk)
    gb, bb = to_blocks(g, chunk), to_blocks(beta, chunk)
    decay = jnp.cumsum(gb, axis=-1)
    causal = jnp.tril(jnp.ones((chunk, chunk), bool))
    strict = jnp.tril(jnp.ones((chunk, chunk), bool), k=-1)
    diff = decay[..., :, None] - decay[..., None, :]
    gam = jnp.where(causal, jnp.exp(jnp.where(causal, diff, 0.0)), 0.0)
    k_beta = kb * bb[..., None]
    m = jnp.where(strict, jnp.einsum('bnhid,bnhjd->bnhij', k_beta, kb) * gam, 0.0)
    eye = jnp.eye(chunk, dtype=F32)
    rhs = jnp.concatenate([vb * bb[..., None], k_beta * jnp.exp(decay)[..., None]], axis=-1)
    sol = lax.linalg.triangular_solve(m + eye, rhs, left_side=True, lower=True, unit_diagonal=True)
    u_base, w_dec = sol[..., :dv], sol[..., dv:]
    qk = jnp.einsum('bnhid,bnhjd->bnhij', qb, kb) * gam
    q_dec = qb * jnp.exp(decay)[..., None]
    last = decay[..., -1:]
    k_dec = kb * jnp.exp(last - decay)[..., None]
    chunk_decay = jnp.exp(last[..., 0])

    def step(s, xs):
        u_b, w_c, qk_c, qd, kd, cd = xs
        u = u_b - jnp.einsum('bhck,bhkv->bhcv', w_c, s)
        o = jnp.einsum('bhck,bhkv->bhcv', qd, s) + jnp.einsum('bhij,bhjv->bhiv', qk_c, u)
        s = s * cd[..., None, None] + jnp.einsum('bhck,bhcv->bhkv', kd, u)
        return s, o

    xs = tuple(jnp.moveaxis(a, 1, 0) for a in (u_base, w_dec, qk, q_dec, k_dec, chunk_decay))
    s_fin, o = lax.scan(step, s0, xs)
    return from_blocks(jnp.moveaxis(o, 0, 1)), s_fin


def retention_chunked(q, k, v, log_gamma, s0, chunk):
    qb, kb, vb = to_blocks(q, chunk), to_blocks(k, chunk), to_blocks(v, chunk)
    idx = jnp.arange(chunk, dtype=F32)
    diff = idx[:, None] - idx[None, :]
    lg = log_gamma[:, None, None]
    dmask = jnp.where(diff >= 0, jnp.exp(jnp.where(diff >= 0, diff, 0.0) * lg), 0.0)
    inner = jnp.einsum('bnhij,bnhjv->bnhiv', jnp.einsum('bnhid,bnhjd->bnhij', qb, kb) * dmask, vb)
    xi = jnp.exp((idx + 1.0)[None, :] * log_gamma[:, None])
    zeta = jnp.exp((chunk - 1.0 - idx)[None, :] * log_gamma[:, None])
    cd = jnp.exp(chunk * log_gamma)[None, :, None, None]
    q_x = qb * xi[..., None]
    k_z = kb * zeta[..., None]

    def step(s, xs):
        inn, qx, kz, vv = xs
        o = inn + jnp.einsum('bhck,bhkv->bhcv', qx, s)
        s = s * cd + jnp.einsum('bhck,bhcv->bhkv', kz, vv)
        return s, o

    xs = tuple(jnp.moveaxis(a, 1, 0) for a in (inner, q_x, k_z, vb))
    s_fin, o = lax.scan(step, s0, xs)
    return from_blocks(jnp.moveaxis(o, 0, 1)), s_fin


def block(x, c, conv_st, dn_st, ret_st, pos, chunk, norm_mix, norm_ffn, w_ada, b_ada, w_in, conv_w,
          dn_a_log, dn_dt_bias, dn_norm, w_out, w_gu, w_down):
    dt = x.dtype
    b, t, _ = x.shape
    ada = (jax.nn.silu(c) @ w_ada + b_ada).astype(F32)
    sh_m, sc_m, g_m, sh_f, sc_f, g_f = jnp.split(ada, N_ADA, axis=-1)

    h = modulate(rmsnorm(x, norm_mix), sh_m, sc_m).astype(dt)
    p = (h @ w_in).astype(F32)
    points = np.cumsum(SPLIT_SIZES)[:-1].tolist()
    qkv_pre, z, b_raw, a_raw, rq, rk, rv, rg = jnp.split(p, points, axis=-1)

    xcat = jnp.concatenate([conv_st.astype(F32), qkv_pre], axis=1)
    new_conv = xcat[:, -(CONV_W - 1):].astype(conv_st.dtype)
    wc = conv_w.astype(F32)
    conv = xcat[:, 0:t] * wc[0]
    for i in range(1, CONV_W):
        conv = conv + xcat[:, i:i + t] * wc[i]
    qkv = jax.nn.silu(conv)
    dq, dk, dv = jnp.split(qkv, [DN_HEADS * DN_DK, 2 * DN_HEADS * DN_DK], axis=-1)
    dq = l2norm(dq.reshape(b, t, DN_HEADS, DN_DK)) * (DN_DK ** -0.5)
    dk = l2norm(dk.reshape(b, t, DN_HEADS, DN_DK))
    dv = dv.reshape(b, t, DN_HEADS, DN_DV)
    beta = jax.nn.sigmoid(b_raw)
    g = -jnp.exp(dn_a_log.astype(F32)) * jax.nn.softplus(a_raw + dn_dt_bias.astype(F32))
    o_dn, new_dn = gated_delta_chunked(dq, dk, dv, g, beta, dn_st.astype(F32), chunk)
    o_dn = rmsnorm(o_dn, dn_norm) * jax.nn.silu(z.reshape(b, t, DN_HEADS, DN_DV))

    rq = rotary(rq.reshape(b, t, RET_HEADS, RET_DK), pos)
    rk = rotary(rk.reshape(b, t, RET_HEADS, RET_DK), pos) * (RET_DK ** -0.5)
    rv = rv.reshape(b, t, RET_HEADS, RET_DV)
    log_gamma = jnp.log(1.0 - 2.0 ** (-5.0 - jnp.arange(RET_HEADS, dtype=F32)))
    o_r, new_ret = retention_chunked(rq, rk, rv, log_gamma, ret_st.astype(F32), chunk)
    o_r = rmsnorm(o_r) * jax.nn.silu(rg.reshape(b, t, RET_HEADS, RET_DV))

    mixed = jnp.concatenate([o_dn.reshape(b, t, -1), o_r.reshape(b, t, -1)], axis=-1).astype(dt)
    x = (x.astype(F32) + g_m[:, None, :] * (mixed @ w_out).astype(F32)).astype(dt)

    h = modulate(rmsnorm(x, norm_ffn), sh_f, sc_f).astype(dt)
    gate, up = jnp.split(h @ w_gu, 2, axis=-1)
    f = (jax.nn.silu(gate) * up) @ w_down
    x = (x.astype(F32) + g_f[:, None, :] * f.astype(F32)).astype(dt)
    return x, new_conv, new_dn.astype(dn_st.dtype), new_ret.astype(ret_st.dtype)


def setup_inputs(seed: int = 0) -> dict:
    key = jax.random.key(seed)
    ks = jax.random.split(key, 24)

    def nrm(k, shape, scale):
        return jax.random.normal(k, shape, F32) * scale

    dt_min, dt_max = 0.001, 0.1
    dt = jnp.exp(jax.random.uniform(ks[13], (DEPTH, DN_HEADS), F32) * (math.log(dt_max) - math.log(dt_min)) + math.log(dt_min))
    return {
        'x_prompt': nrm(ks[0], (BATCH, SEQ, D_MODEL), 1.0),
        'x_sample': nrm(ks[1], (DEC_BATCH, DEC_SEQ, D_MODEL), 1.0),
        'state_conv': nrm(ks[2], (DEPTH, DEC_BATCH, CONV_W - 1, CONV_CH), 1.0),
        'state_delta': nrm(ks[3], (DEPTH, DEC_BATCH, DN_HEADS, DN_DK, DN_DV), 0.1),
        'state_ret': nrm(ks[4], (DEPTH, DEC_BATCH, RET_HEADS, RET_DK, RET_DV), 0.5),
        'c_prompt': nrm(ks[5], (BATCH, D_MODEL), 1.0),
        'c_sample': nrm(ks[6], (DEC_BATCH, D_MODEL), 1.0),
        'norm_mix': 1.0 + nrm(ks[7], (DEPTH, D_MODEL), 0.02),
        'norm_ffn': 1.0 + nrm(ks[8], (DEPTH, D_MODEL), 0.02),
        'w_ada': nrm(ks[9], (DEPTH, D_MODEL, N_ADA * D_MODEL), 0.5 * D_MODEL ** -0.5),
        'b_ada': nrm(ks[10], (DEPTH, N_ADA * D_MODEL), 0.01),
        'w_in': nrm(ks[11], (DEPTH, D_MODEL, PROJ_WIDTH), D_MODEL ** -0.5),
        'conv_w': nrm(ks[12], (DEPTH, CONV_W, CONV_CH), CONV_W ** -0.5),
        'dn_a_log': jnp.log(jax.random.uniform(ks[14], (DEPTH, DN_HEADS), F32, 1.0, 16.0)),
        'dn_dt_bias': dt + jnp.log(-jnp.expm1(-dt)),
        'dn_norm': 1.0 + nrm(ks[15], (DEPTH, DN_DV), 0.02),
        'w_out': nrm(ks[16], (DEPTH, MIX_WIDTH, D_MODEL), MIX_WIDTH ** -0.5),
        'w_gu': nrm(ks[17], (DEPTH, D_MODEL, 2 * D_FF), D_MODEL ** -0.5),
        'w_down': nrm(ks[18], (DEPTH, D_FF, D_MODEL), D_FF ** -0.5),
        'norm_final': 1.0 + nrm(ks[19], (D_MODEL,), 0.02),
        'w_ada_final': nrm(ks[20], (D_MODEL, 2 * D_MODEL), 0.5 * D_MODEL ** -0.5),
        'b_ada_final': nrm(ks[21], (2 * D_MODEL,), 0.01),
    }


def reference(x_prompt, x_sample, state_conv, state_delta, state_ret, c_prompt, c_sample,
              norm_mix, norm_ffn, w_ada, b_ada, w_in, conv_w, dn_a_log, dn_dt_bias, dn_norm,
              w_out, w_gu, w_down, norm_final, w_ada_final, b_ada_final):
    bp, tp = x_prompt.shape[0], x_prompt.shape[1]
    ts = x_sample.shape[1]
    pos_p = jnp.arange(tp)
    pos_s = PAST_LEN + jnp.arange(ts)
    zc = jnp.zeros((bp, CONV_W - 1, CONV_CH), x_prompt.dtype)
    zd = jnp.zeros((bp, DN_HEADS, DN_DK, DN_DV), state_delta.dtype)
    zr = jnp.zeros((bp, RET_HEADS, RET_DK, RET_DV), state_ret.dtype)
    hp, hs = x_prompt, x_sample
    cp_l, dp_l, rp_l, cs_l, ds_l, rs_l = [], [], [], [], [], []
    for l in range(DEPTH):
        hp, cp, dp, rp = block(hp, c_prompt, zc, zd, zr, pos_p, CHUNK, norm_mix[l], norm_ffn[l],
                               w_ada[l], b_ada[l], w_in[l], conv_w[l], dn_a_log[l], dn_dt_bias[l],
                               dn_norm[l], w_out[l], w_gu[l], w_down[l])
        hs, cs, ds, rs = block(hs, c_sample, state_conv[l], state_delta[l], state_ret[l], pos_s, ts,
                               norm_mix[l], norm_ffn[l], w_ada[l], b_ada[l], w_in[l], conv_w[l],
                               dn_a_log[l], dn_dt_bias[l], dn_norm[l], w_out[l], w_gu[l], w_down[l])
        cp_l.append(cp); dp_l.append(dp); rp_l.append(rp)
        cs_l.append(cs); ds_l.append(ds); rs_l.append(rs)

    def final(h, c):
        ada = (jax.nn.silu(c) @ w_ada_final + b_ada_final).astype(F32)
        shift, scale = jnp.split(ada, 2, axis=-1)
        return modulate(rmsnorm(h, norm_final), shift, scale).astype(h.dtype)

    y_prompt = final(hp, c_prompt)
    y_sample = final(hs, c_sample)
    return (y_prompt, y_sample, jnp.stack(cp_l), jnp.stack(dp_l), jnp.stack(rp_l),
            jnp.stack(cs_l), jnp.stack(ds_l), jnp.stack(rs_l))
```

```python
import math
import contextlib
import numpy as np
import concourse.bass as bass
import concourse.mybir as mybir
from concourse.bass_utils import run_bass_kernel_spmd

F32 = mybir.dt.float32
BF16 = mybir.dt.bfloat16
I32 = mybir.dt.int32
ALU = mybir.AluOpType
AF = mybir.ActivationFunctionType

D = 2048
KD = 16
DFF = 5632
NJ = 44
NCH = 17
PW = NCH * 128
EPS = 1e-6
PAST_LEN = 2048
MAGIC = 12582912.0
TWO_PI = 2.0 * math.pi


class Eng:
    def __init__(self, name, obj, sem, step):
        self.name, self.obj, self.sem, self.step = name, obj, sem, step
        self.cnt = 0
        self.waited = {}


class KB:
    def __init__(self, nc, es):
        self.nc, self.es = nc, es
        sem = lambda n: es.enter_context(nc.semaphore(n))
        self.pe = Eng("pe", nc.tensor, sem("s_pe"), 1)
        self.dve = Eng("dve", nc.vector, sem("s_dve"), 1)
        self.act = Eng("act", nc.scalar, sem("s_act"), 1)
        self.pool = Eng("pool", nc.gpsimd, sem("s_pool"), 1)
        self.sp = Eng("sp", nc.sync, sem("s_sp"), 1)
        self.engs = [self.pe, self.dve, self.act, self.pool, self.sp]
        self.dsem = {}
        for q in ("sp", "pool", "act"):
            self.dsem[q] = [Eng(f"d_{q}{i}", None, sem(f"s_d_{q}{i}"), 16) for i in range(8)]
        self.dnext = {"sp": 0, "pool": 0, "act": 0}
        self.cc = Eng("cc", None, sem("s_cc"), 1)
        self.lw = {}
        self.rd = {}
        self.psn = 0
        self.ns = ""
        self.alias = {}
        self.rec = None
        self.psub = {"d0_": [0, 1], "d1_": [2, 3], "r_": [4, 5], "x_": [6, 7]}
        self.psc = {}
        self.ps_t = [es.enter_context(nc.psum_tensor(f"ps{i}", [128, 512], F32)) for i in range(8)]

    def sb(self, name, shape, dt=F32, es=None):
        return (es or self.es).enter_context(self.nc.sbuf_tensor("sb_" + name, shape, dt))

    def ps(self):
        if self.ns in self.psub:
            sub = self.psub[self.ns]
            c = self.psc.get(self.ns, 0)
            self.psc[self.ns] = c + 1
            i = sub[c % len(sub)]
            return self.ps_t[i], ("ps", i)
        i = self.psn % 8
        self.psn += 1
        return self.ps_t[i], ("ps", i)

    def replay(self, lists):
        idx = [0] * len(lists)
        ns_save, self.ns = self.ns, ""
        while True:
            best, bf = -1, 2.0
            for i, l in enumerate(lists):
                if idx[i] < len(l):
                    f = idx[i] / len(l)
                    if f < bf:
                        best, bf = i, f
            if best < 0:
                break
            it = lists[best][idx[best]]
            idx[best] += 1
            if it[0] == "op":
                self.op(it[1], it[2], it[3], it[4])
            else:
                self.dma(it[1], it[2], it[3], it[4], it[5], it[6], it[7])
        self.ns = ns_save

    def _nk(self, ks):
        if not self.ns:
            return ks
        pf = ("f_", "b_", "c_", "cb_", "cb2_", "dcol", "st3")
        al = self.alias.get(self.ns, {})
        return [self.ns + al.get(k, k) if isinstance(k, str) and k.startswith(pf) else k for k in ks]

    def _deps(self, E, r, w, extra=()):
        deps = {}

        def need(p):
            if p is None:
                return
            e, s = p
            if e is self.pe and E is self.pe:
                return
            if deps.get(e, (None, 0))[1] < s:
                deps[e] = (e, s)
        for k in r:
            need(self.lw.get(k))
        for k in w:
            need(self.lw.get(k))
            for e, s in self.rd.get(k, {}).items():
                need((e, s))
        for p in extra:
            need(p)
        for e, s in deps.values():
            if E.waited.get(e.name, 0) < s:
                E.obj.wait_ge(e.sem, s)
                E.waited[e.name] = s

    def _mark(self, P, seq, r, w):
        for k in r:
            self.rd.setdefault(k, {})[P] = seq
        for k in w:
            self.lw[k] = (P, seq)
            self.rd[k] = {}

    def op(self, E, fn, r=(), w=()):
        r, w = self._nk(r), self._nk(w)
        if self.rec is not None:
            self.rec.append(("op", E, fn, r, w))
            return
        self._deps(E, r, w)
        ins = fn(E.obj)
        E.cnt += 1
        ins.then_inc(E.sem, 1)
        self._mark(E, E.cnt, r, w)

    def dma(self, Q, out, in_, r=(), w=(), indirect=None, eoff=0):
        r, w = self._nk(r), self._nk(w)
        if self.rec is not None:
            self.rec.append(("dma", Q, out, in_, r, w, indirect, eoff))
            return
        pool = self.dsem[Q.name]
        Dk = pool[self.dnext[Q.name] % len(pool)]
        self.dnext[Q.name] += 1
        extra = [(Dk, Dk.cnt * 16)] if Dk.cnt else []
        self._deps(Q, r, w, extra)
        if indirect is not None:
            ins = Q.obj.indirect_dma_start(out=out, out_offset=None, in_=in_,
                                           in_offset=bass.IndirectOffsetOnAxis(ap=indirect, axis=0), element_offset=eoff)
        else:
            ins = Q.obj.dma_start(out=out, in_=in_)
        ins.then_inc(Dk.sem, 16)
        Dk.cnt += 1
        self._mark(Dk, Dk.cnt * 16, r, w)

    def allgather(self, in_ap, out_ap, groups, r=(), w=()):
        Q = self.pool
        self._deps(Q, r, w)
        ins = Q.obj.collective_compute("AllGather", ALU.bypass, replica_groups=groups, ins=[in_ap], outs=[out_ap])
        ins.then_inc(self.cc.sem, 1)
        self.cc.cnt += 1
        self._mark(self.cc, self.cc.cnt, r, w)

    def barrier(self):
        allp = self.engs + [d for q in self.dsem.values() for d in q] + [self.cc]
        for E in self.engs:
            for P in allp:
                s = P.cnt * P.step
                if P is E or s == 0:
                    continue
                if E.waited.get(P.name, 0) < s:
                    E.obj.wait_ge(P.sem, s)
                    E.waited[P.name] = s
        self.lw.clear()
        self.rd.clear()


def build(S):
    SEG = S // 4
    SEGW = SEG + 16
    T1 = 256
    assert S % T1 == 0 and SEG % 128 == 0
    nc = bass.Bass("TRN2", target_bir_lowering=False)
    din = lambda n, sh, dt=F32: nc.dram_tensor(n, sh, dt, kind="ExternalInput")
    dout = lambda n, sh: nc.dram_tensor(n, sh, F32, kind="ExternalOutput")
    x1 = din("x1", [S, D]); xs1 = din("xs1", [64, D]); x2 = din("x2", [SEG, D]); xs2 = din("xs2", [16, D])
    cT = din("cT", [D, 6]); w_in_o = din("w_in_o", [D, PW]); w_out_p = din("w_out_p", [D, D])
    w_gu = din("w_gu", [D, 2 * DFF]); w_down = din("w_down", [DFF, D])
    w_ada = din("w_ada", [D, 6 * D]); b_ada_f = din("b_ada_f", [128, 96]); b_ada_r = din("b_ada_r", [1, 6 * D])
    w_adaf = din("w_adaf", [D, 2 * D]); b_adaf_r = din("b_adaf_r", [1, 2 * D])
    nmix = din("nmix", [128, KD]); nffn = din("nffn", [128, KD]); nfin = din("nfin", [128, D])
    cw_in = din("cw", [128, 24]); cst_in = din("cst", [4, 128, 18]); dnc_in = din("dnc", [128, 5])
    sd_in = din("sd", [4, 2, 128, 128]); sr_in = din("sr", [4, 256, 256])
    cmat = din("cmat", [128, 15 * 128]); crow = din("crow", [128, 512 + 1]); rc_in = din("rc", [128, 802])
    gidx_in = din("gidx", [128, 2 * KD], I32)
    y2 = dout("y2", [SEGW, D]); convp = dout("convp", [128, 18]); deltap = dout("deltap", [2, 128, 128])
    retp = dout("retp", [256, 256]); convs = dout("convs", [4, 128, 18]); deltas = dout("deltas", [4, 2, 128, 128])
    rets = dout("rets", [4, 256, 256])
    NSUP = S // T1
    ib = nc.dram_tensor("ib", [NSUP * 512, T1], BF16)
    ob = nc.dram_tensor("ob", [NSUP * 2048, T1], BF16)
    ibs = nc.dram_tensor("ibs", [4 * 512, 16], BF16)
    obs = nc.dram_tensor("obs", [4 * 2048, 16], BF16)
    wo_bf = nc.dram_tensor("wo_bf", [4, 128, KD, 512], BF16)
    wg_bf = nc.dram_tensor("wg_bf", [11, 2, 128, KD, 512], BF16)
    wd_bf = nc.dram_tensor("wd_bf", [4, 4, 128, 11, 512], BF16)
    adaT_d = nc.dram_tensor("adaT_d", [128, 8 * D], F32)

    with contextlib.ExitStack() as es, nc.Block() as block:
        @block.sync
        def _(_sync):
            K = KB(nc, es)
            pe, dve, act, pool, sp = K.pe, K.dve, K.act, K.pool, K.sp
            cm = K.sb("cm", [128, 15 * 128])
            cr = K.sb("cr", [128, 513])
            rc = K.sb("rc", [128, 802])
            kc = K.sb("kc", [128, 8])
            identb = K.sb("identb", [128, 128], BF16)
            K.dma(sp, cm[:, :], cmat[:, :], w=["cm"])
            K.dma(sp, cr[:, :], crow[:, :], w=["cr"])
            K.dma(sp, rc[:, :], rc_in[:, :], w=["rc"])
            K.op(dve, lambda e: e.memset(kc[:, 0:1], EPS), w=["kc"])
            K.op(dve, lambda e: e.memset(kc[:, 1:2], 1.0), w=["kc"])
            K.op(dve, lambda e: e.memset(kc[:, 2:3], 0.0), w=["kc"])
            ident = cm[:, 0:128]; triI = cm[:, 128:256]; triS = cm[:, 256:384]; ones = cm[:, 384:512]
            sel = [cm[:, 512 + 128 * g: 640 + 128 * g] for g in range(4)]
            bm16 = cm[:, 1024:1152]
            lvm = [(cm[:, 1152 + 256 * i: 1280 + 256 * i], cm[:, 1280 + 256 * i: 1408 + 256 * i]) for i in range(3)]
            K.op(dve, lambda e: e.tensor_copy(out=identb[:, :], in_=ident), r=["cm"], w=["identb"])
            iota = cr[:, 0:256]; rmask = cr[:, 256:512]; inv2pi = cr[:, 512:513]
            xi128 = rc[:, 0:256]; ze128 = rc[:, 256:512]; dm128 = rc[:, 512:640]; cd128 = rc[:, 640:641]
            xi16 = rc[:, 641:657]; ze16 = rc[:, 657:673]; dm16 = rc[:, 673:801]; cd16 = rc[:, 801:802]
            adaF = K.sb("adaF", [128, 4, KD, 6])
            gmul = K.sb("gmul", [128, 2, KD, 6])
            nm = K.sb("nm", [128, 2, KD])
            K.dma(sp, nm[:, 0, :], nmix[:, :], w=["nm"])
            K.dma(sp, nm[:, 1, :], nffn[:, :], w=["nm"])

            def cast_weights_gen():
                for n in range(4):
                    for k in range(KD):
                        K.dma(pool, wo_bf[n, :, k, :], w_out_p[k * 128:(k + 1) * 128, n * 512:(n + 1) * 512],
                              w=[("wo", n)])
                        yield
                for jb in range(11):
                    for gu in range(2):
                        for k in range(KD):
                            K.dma(pool, wg_bf[jb, gu, :, k, :],
                                  w_gu[k * 128:(k + 1) * 128, gu * DFF + jb * 512: gu * DFF + (jb + 1) * 512],
                                  w=[("wg", jb, gu)])
                            yield
                for n in range(4):
                    for jq in range(4):
                        for jj in range(11):
                            j = jq * 11 + jj
                            K.dma(pool, wd_bf[n, jq, :, jj, :], w_down[j * 128:(j + 1) * 128, n * 512:(n + 1) * 512],
                                  w=[("wd", n, jq)])
                            yield

            with contextlib.ExitStack() as e0:
                cs = K.sb("cs", [128, KD, 6], es=e0)
                csr = K.sb("csr", [128, 2, KD, 128], BF16, es=e0)
                csb = K.sb("csb", [128, KD, 6], BF16, es=e0)
                wblk = [K.sb(f"wblk{i}", [128, KD, 512], es=e0) for i in range(2)]
                wbf = [K.sb(f"wbf{i}", [128, KD, 512], BF16, es=e0) for i in range(2)]
                onesb = K.sb("onesb", [1, 128], BF16, es=e0)
                browb = [K.sb(f"browb{i}", [1, 512], BF16, es=e0) for i in range(2)]
                K.op(dve, lambda e: e.memset(onesb[:, :], 1.0), w=["onesb"])
                bfe = K.sb("bfe", [128, 96], es=e0)
                browt = [K.sb(f"brow{i}", [1, 512], es=e0) for i in range(2)]
                adaT = K.sb("adaT", [128, 2, 4, D], es=e0)
                K.dma(sp, cs[:, :, :], cT.ap().rearrange("(k p) r -> p k r", p=128), w=["cs"])
                K.dma(sp, bfe[:, :], b_ada_f[:, :], w=["bfe"])
                K.op(act, lambda e: e.activation(out=cs[:, :, :], in_=cs[:, :, :], func=AF.Silu), r=["cs"], w=["cs"])
                K.op(dve, lambda e: e.tensor_copy(out=csb[:, :, :], in_=cs[:, :, :]), r=["cs"], w=["csb"])
                for ri, row in enumerate((0, 5)):
                    for k in range(KD):
                        K.op(dve, lambda e, ri=ri, row=row, k=k: e.tensor_copy(
                            out=csr[:, ri, k, :], in_=cs[:, k, row:row + 1].to_broadcast([128, 128])),
                            r=["cs"], w=["csr"])
                fmap = {0: 0, 1: 1, 3: 2, 4: 3}
                tmap = {2: 0, 5: 1}
                for blk in range(32):
                    wt = wblk[blk % 2]; wk = ("wblk", blk % 2)
                    wb = wbf[blk % 2]; wbk = ("wbf", blk % 2)
                    if blk < 24:
                        src = w_ada[:, blk * 512:(blk + 1) * 512]
                    else:
                        src = w_adaf[:, (blk - 24) * 512:(blk - 23) * 512]
                    srcv = src.rearrange("(k p) n -> p k n", p=128)
                    K.dma(sp, wt[:, 0:8, :], srcv[:, 0:8, :], w=[wk + (0,)])
                    K.dma(act, wt[:, 8:16, :], srcv[:, 8:16, :], w=[wk + (1,)])
                    split = blk // 4 if blk < 24 else 6 + (blk - 24) // 4
                    brow = browt[blk % 2]; bk = ("brow", blk % 2)
                    bsrc = b_ada_r[:, blk * 512:(blk + 1) * 512] if blk < 24 else b_adaf_r[:, (blk - 24) * 512:(blk - 23) * 512]
                    K.dma(sp, brow[:, :], bsrc, w=[bk])
                    bb = browb[blk % 2]; bbk = ("browb", blk % 2)
                    K.op(dve, lambda e, bb=bb, brow=brow: e.tensor_copy(out=bb[:, :], in_=brow[:, :]), r=[bk], w=[bbk])
                    K.op(dve, lambda e, wb=wb, wt=wt: e.tensor_copy(out=wb[:, 0:8, :], in_=wt[:, 0:8, :]), r=[wk + (0,)], w=[wbk + (0,)])
                    K.op(act, lambda e, wb=wb, wt=wt: e.copy(out=wb[:, 8:12, :], in_=wt[:, 8:12, :]), r=[wk + (1,)], w=[wbk + (1,)])
                    K.op(pool, lambda e, wb=wb, wt=wt: e.tensor_copy(out=wb[:, 12:16, :], in_=wt[:, 12:16, :]), r=[wk + (1,)], w=[wbk + (2,)])
                    wbkk = lambda k, wbk=wbk: wbk + ((0,) if k < 8 else (1,) if k < 12 else (2,))
                    q = blk % 4
                    if split in fmap:
                        for cc in range(4):
                            pt, pk = K.ps()
                            for k in range(KD):
                                K.op(pe, lambda e, k=k, cc=cc, pt=pt, wb=wb: e.matmul(
                                    pt[:, 0:6], lhsT=wb[:, k, cc * 128:(cc + 1) * 128], rhs=csb[:, k, :],
                                    start=(k == 0), stop=(k == KD - 1)), r=[wbkk(k), "csb"], w=[pk])
                            n = q * 4 + cc
                            col = split * 16 + n
                            K.op(act, lambda e, n=n, col=col, pt=pt, sl=fmap[split]: e.activation(
                                out=adaF[:, sl, n, :], in_=pt[:, 0:6], func=AF.Identity,
                                bias=bfe[:, col:col + 1], scale=1.0), r=[pk, "bfe"], w=["adaF"])
                    else:
                        slot = tmap[split] if split in tmap else (2 if split == 6 else 3)
                        for ri in range(2):
                            pt, pk = K.ps()
                            for k in range(KD):
                                K.op(pe, lambda e, k=k, ri=ri, pt=pt, wb=wb: e.matmul(
                                    pt[:, :], lhsT=csr[:, ri, k, :], rhs=wb[:, k, :],
                                    start=(k == 0), stop=False), r=[wbkk(k), "csr"], w=[pk])
                            K.op(pe, lambda e, pt=pt, bb=bb: e.matmul(
                                pt[:, :], lhsT=onesb[0:1, :], rhs=bb[0:1, :],
                                start=False, stop=True), r=[bbk, "onesb"], w=[pk])
                            K.op(dve if ri == 0 else act,
                                 (lambda e, pt=pt, ri=ri, slot=slot, q=q: e.tensor_copy(
                                     out=adaT[:, ri, slot, q * 512:(q + 1) * 512], in_=pt[:, :])) if ri == 0 else
                                 (lambda e, pt=pt, ri=ri, slot=slot, q=q: e.copy(
                                     out=adaT[:, ri, slot, q * 512:(q + 1) * 512], in_=pt[:, :])),
                                 r=[pk], w=["adaT"])
                for i, sl in enumerate((1, 3)):
                    K.op(dve, lambda e, i=i, sl=sl: e.tensor_scalar(
                        out=gmul[:, i, :, :], in0=adaF[:, sl, :, :], scalar1=1.0, scalar2=None, op0=ALU.add),
                        r=["adaF"], w=["gmul"])
                    K.op(dve, lambda e, i=i: e.tensor_tensor(
                        out=gmul[:, i, :, :], in0=gmul[:, i, :, :],
                        in1=nm[:, i, :].unsqueeze(2).to_broadcast([128, KD, 6]), op=ALU.mult),
                        r=["gmul", "nm"], w=["gmul"])
                nf = wblk[0]
                nfv = nf[:, 0:4, :].rearrange("p a b -> p (a b)")
                K.dma(sp, nfv, nfin[:, :], w=[("wblk", 0, 0), ("wblk", 0, 1)])
                for ri in range(2):
                    K.op(dve, lambda e, ri=ri: e.scalar_tensor_tensor(
                        out=adaT[:, ri, 3, :], in0=adaT[:, ri, 3, :], scalar=1.0, in1=nfv,
                        op0=ALU.add, op1=ALU.mult), r=["adaT", ("wblk", 0, 0)], w=["adaT"])
                K.dma(sp, adaT_d[:, :], adaT[:, :, :, :].rearrange("p a b c -> p (a b c)"), r=["adaT"], w=["adaT_d"])
            K.barrier()
            import os
            if os.environ.get("KSTOP") == "0":
                return
            cgen = cast_weights_gen()

            def pump(n):
                for _ in range(n):
                    if next(cgen, "done") == "done":
                        break
            if os.environ.get("KSTOP") == "0b":
                K.barrier()
                return

            with contextlib.ExitStack() as e1:
                w1 = K.sb("w1", [128, KD, PW], BF16, es=e1)
                if os.environ.get("W1_CASTDMA"):
                    for k in range(KD):
                        K.dma(pool, w1[:, k, :], w_in_o[k * 128:(k + 1) * 128, :], w=["w1"])
                cwt = K.sb("cwt", [128, 6, 4], es=e1)
                K.dma(sp, cwt[:, :, :].rearrange("p a b -> p (a b)"), cw_in[:, :], w=["cwt"])
                dnc = K.sb("dnc", [128, 5], es=e1)
                K.dma(sp, dnc[:, :], dnc_in[:, :], w=["dnc"])
                negA = K.sb("negA", [128, 2], es=e1)
                K.op(act, lambda e: e.activation(out=negA[:, :], in_=dnc[:, 0:2], func=AF.Exp), r=["dnc"], w=["negA"])
                K.op(dve, lambda e: e.tensor_scalar(out=negA[:, :], in0=negA[:, :], scalar1=-1.0, scalar2=None,
                                                    op0=ALU.mult), r=["negA"], w=["negA"])
                xt = [K.sb("xt0", [128, D], es=e1)]
                xn = K.sb("xn", [128, D], BF16, es=e1)
                sq_junk = xn
                st = K.sb("st", [128, 4], es=e1)
                hT = K.sb("hT", [128, KD, T1], BF16, es=e1)
                Pbufs = [K.sb(f"P{i}", [128, NCH, T1 + 3], es=e1) for i in range(2)]
                P = Pbufs[0]
                Sdn = K.sb("Sdn", [128, 2, 128], es=e1)
                Sdnb = K.sb("Sdnb", [128, 2, 128], BF16, es=e1)
                Sr = K.sb("Sr", [128, 2, 256], es=e1)
                Srb = K.sb("Srb", [128, 2, 256], BF16, es=e1)
                def mk(tag, fn, bn, cn, cbn, al=None):
                    al = al or {}
                    K.alias[tag + "_"] = {"f_" + a_: "f_" + b_ for a_, b_ in al.items()}
                    fd = {n: K.sb(tag + "f_" + n, [128, T1], es=e1) for n in fn if n not in al}
                    for a_, b_ in al.items():
                        fd[a_] = fd[b_]
                    return (fd,
                            {n: K.sb(tag + "b_" + n, [128, T1], BF16, es=e1) for n in bn},
                            {n: K.sb(tag + "c_" + n, [128, 128], es=e1) for n in cn},
                            {n: K.sb(tag + "cb_" + n, [128, 128], BF16, es=e1) for n in cbn})
                DNS = []
                for tag in ("d0", "d1"):
                    f_, b_, c_, cb_ = mk(tag, ("acc", "csq", "csk", "csv", "sq", "rr", "qn", "beta", "g", "d", "e", "kd", "eb", "oT", "zs", "ta"),
                                         ("q", "k", "kb", "kc", "kdT", "vb", "qd", "mx"), ("junk", "arg", "gam", "gamI", "gamS"),
                                         ("Pa", "Pb", "PTa", "PTb", "Ya", "Yb", "qk", "kdec", "R", "u", "N0", "PT0", "Dva", "Dvb", "W", "V", "N0m", "PTm"),
                                         al={"sq": "acc", "ta": "acc", "eb": "g", "zs": "csv", "qn": "csq"})
                    DNS.append((f_, b_, c_, cb_, K.sb(tag + "dcol", [128, 2], es=e1)))
                f_, b_, c_, cb_ = mk("r", ("u", "t1", "nf", "sin", "cos", "ta", "tb", "q1", "q2", "k1", "k2", "or0", "or1", "sq", "rr", "zs"),
                                     ("q1", "q2", "k1", "k2", "qx1", "qx2", "kz1", "kz2", "rv0", "rv1", "mx"), (), ("rqk",),
                                     al={"sq": "t1", "rr": "nf", "zs": "u"})
                RTS = (f_, b_, cb_, {n: K.sb("rcb2_" + n, [128, 256], BF16, es=e1) for n in ("v", "kz")})
                if not os.environ.get("W1_CASTDMA"):
                    for k in range(KD):
                        for hh in range(2):
                            stg = xt[0]; sk_ = ("xt", id(stg))
                            K.dma(sp, stg[:, 0:PW // 2], w_in_o[k * 128:(k + 1) * 128, hh * (PW // 2):(hh + 1) * (PW // 2)], w=[sk_])
                            K.op(dve if hh == 0 else pool, lambda e, k=k, hh=hh, stg=stg: e.tensor_copy(
                                out=w1[:, k, hh * (PW // 2):(hh + 1) * (PW // 2)], in_=stg[:, 0:PW // 2]), r=[sk_], w=["w1"])
                for pb_ in Pbufs:
                    K.op(dve, lambda e, pb_=pb_: e.memset(pb_[:, :, 0:3], 0.0), w=[("P", id(pb_), m) for m in range(NCH)])
                if os.environ.get("KSTOP") == "1a":
                    K.barrier()
                    return
                K.op(dve, lambda e: e.memset(Sdn[:, :, :], 0.0), w=["Sdn"])
                K.op(dve, lambda e: e.memset(Sdnb[:, :, :], 0.0), w=["Sdnb"])
                K.op(dve, lambda e: e.memset(Sr[:, :, :], 0.0), w=["Sr"])
                K.op(dve, lambda e: e.memset(Srb[:, :, :], 0.0), w=["Srb"])

                def norm_transpose(src_ap, ntok, toff, xbuf, scal):
                    xk = ("xt", id(xbuf))
                    K.dma(sp, xbuf[0:ntok, :], src_ap, w=[xk])
                    K.op(act, lambda e: e.activation(out=sq_junk[0:ntok, :], in_=xbuf[0:ntok, :], func=AF.Square,
                                                     accum_out=st[0:ntok, 0:1]), r=[xk], w=["xn", "st"])
                    K.op(act, lambda e: e.activation(out=st[0:ntok, 1:2], in_=st[0:ntok, 0:1], func=AF.Ln,
                                                     bias=kc[0:ntok, 0:1], scale=1.0 / D), r=["st", "kc"], w=["st"])
                    K.op(act, lambda e: e.activation(out=st[0:ntok, 2:3], in_=st[0:ntok, 1:2], func=AF.Exp, scale=-0.5),
                         r=["st"], w=["st"])
                    K.op(dve, lambda e: e.tensor_scalar(out=xn[0:ntok, :], in0=xbuf[0:ntok, :], scalar1=st[0:ntok, 2:3],
                                                        scalar2=None, op0=ALU.mult), r=[xk, "st"], w=["xn"])
                    for k in range(KD):
                        pt, pk = K.ps()
                        K.op(pe, lambda e, k=k, pt=pt: e.matmul(
                            pt[:, 0:ntok], lhsT=xn[0:ntok, k * 128:(k + 1) * 128],
                            rhs=identb[0:ntok, 0:ntok], start=True, stop=True), r=["xn", "identb"], w=[pk])
                        for (lo, hi, row) in scal:
                            E = dve if (k % 2 == 0) else act
                            if E is dve:
                                fn = lambda e, k=k, lo=lo, hi=hi, row=row, pt=pt: e.tensor_scalar(
                                    out=hT[:, k, toff + lo: toff + hi], in0=pt[:, lo:hi],
                                    scalar1=gmul[:, 0, k, row:row + 1], scalar2=adaF[:, 0, k, row:row + 1],
                                    op0=ALU.mult, op1=ALU.add)
                            else:
                                fn = lambda e, k=k, lo=lo, hi=hi, row=row, pt=pt: e.activation(
                                    out=hT[:, k, toff + lo: toff + hi], in_=pt[:, lo:hi],
                                    func=AF.Identity, scale=gmul[:, 0, k, row:row + 1],
                                    bias=adaF[:, 0, k, row:row + 1])
                            K.op(E, fn, r=[pk, "gmul", "adaF"], w=[("hT", k)])

                def project(T, dst, doff):
                    for m in range(NCH):
                        pt, pk = K.ps()
                        for k in range(KD):
                            K.op(pe, lambda e, k=k, m=m, pt=pt: e.matmul(
                                pt[:, 0:T], lhsT=w1[:, k, m * 128:(m + 1) * 128], rhs=hT[:, k, 0:T],
                                start=(k == 0), stop=(k == KD - 1)), r=["w1", ("hT", k)], w=[pk])
                        if m % 2 == 0:
                            K.op(dve, lambda e, m=m, pt=pt: e.tensor_copy(out=dst[:, m, doff:doff + T], in_=pt[:, 0:T]),
                                 r=[pk], w=[("P", id(dst), m)])
                        else:
                            K.op(act, lambda e, m=m, pt=pt: e.copy(out=dst[:, m, doff:doff + T], in_=pt[:, 0:T]),
                                 r=[pk], w=[("P", id(dst), m)])

                def rms_gate_out(srcs, gates, nfeat, wcol, T, dsts, F, B):
                    pt, pk = K.ps()
                    for i, (sa, skey) in enumerate(srcs):
                        K.op(dve, lambda e, sa=sa: e.tensor_tensor(out=F["sq"][:, 0:T], in0=sa, in1=sa, op=ALU.mult),
                             r=[skey], w=["f_sq"])
                        K.op(pe, lambda e, i=i, pt=pt: e.matmul(pt[:, 0:T], lhsT=ones, rhs=F["sq"][:, 0:T],
                                                                 start=(i == 0), stop=(i == len(srcs) - 1)),
                             r=["f_sq", "cm"], w=[pk])
                    K.op(act, lambda e, pt=pt: e.activation(out=F["rr"][:, 0:T], in_=pt[:, 0:T], func=AF.Ln,
                                                            bias=kc[:, 0:1], scale=1.0 / nfeat), r=[pk, "kc"], w=["f_rr"])
                    K.op(act, lambda e: e.activation(out=F["rr"][:, 0:T], in_=F["rr"][:, 0:T], func=AF.Exp, scale=-0.5),
                         r=["f_rr"], w=["f_rr"])
                    for i, (sa, skey) in enumerate(srcs):
                        K.op(act, lambda e, i=i: e.activation(out=F["zs"][:, 0:T], in_=P[:, gates[i], 3:3 + T], func=AF.Silu),
                             r=[("P", id(P), gates[i])], w=["f_zs"])
                        if wcol is not None:
                            K.op(dve, lambda e, sa=sa: e.scalar_tensor_tensor(
                                out=F["ta"][:, 0:T], in0=sa, scalar=wcol, in1=F["rr"][:, 0:T], op0=ALU.mult, op1=ALU.mult),
                                r=[skey, "f_rr", "dnc"], w=["f_ta"])
                        else:
                            K.op(dve, lambda e, sa=sa: e.tensor_tensor(out=F["ta"][:, 0:T], in0=sa, in1=F["rr"][:, 0:T],
                                                                       op=ALU.mult), r=[skey, "f_rr"], w=["f_ta"])
                        K.op(dve, lambda e: e.tensor_tensor(out=B["mx"][:, 0:T], in0=F["ta"][:, 0:T], in1=F["zs"][:, 0:T],
                                                            op=ALU.mult), r=["f_ta", "f_zs"], w=["b_mx"])
                        K.dma(sp, dsts[i][0], B["mx"][:, 0:T], r=["b_mx"], w=[dsts[i][1]])

                def dn_gen(j, T, C, dst_rows_col, F, B, C_, CB, dcol):
                    nchunk = T // C
                    L = int(math.log2(C))
                    Pk = lambda m: ("P", id(P), m)
                    m0 = 4 * j
                    for ci, nm_ in enumerate(("csq", "csk", "csv")):
                        m = m0 + ci
                        cwi = 3 * j + ci
                        K.op(dve, lambda e, m=m, cwi=cwi: e.tensor_scalar(
                            out=F["acc"][:, 0:T], in0=P[:, m, 0:T], scalar1=cwt[:, cwi, 0:1], scalar2=None,
                            op0=ALU.mult), r=[Pk(m), "cwt"], w=["f_acc"])
                        for i in range(1, 4):
                            K.op(dve, lambda e, m=m, cwi=cwi, i=i: e.scalar_tensor_tensor(
                                out=F["acc"][:, 0:T], in0=P[:, m, i:i + T], scalar=cwt[:, cwi, i:i + 1],
                                in1=F["acc"][:, 0:T], op0=ALU.mult, op1=ALU.add), r=[Pk(m), "cwt", "f_acc"], w=["f_acc"])
                        K.op(act, lambda e, nm_=nm_: e.activation(out=F[nm_][:, 0:T], in_=F["acc"][:, 0:T], func=AF.Silu),
                             r=["f_acc"], w=["f_" + nm_])
                        yield
                    for nm_ in ("csq", "csk"):
                        K.op(dve, lambda e, nm_=nm_: e.tensor_tensor(out=F["sq"][:, 0:T], in0=F[nm_][:, 0:T],
                                                                    in1=F[nm_][:, 0:T], op=ALU.mult),
                             r=["f_" + nm_], w=["f_sq"])
                        pt, pk = K.ps()
                        K.op(pe, lambda e, pt=pt: e.matmul(pt[:, 0:T], lhsT=ones, rhs=F["sq"][:, 0:T], start=True,
                                                           stop=True), r=["f_sq", "cm"], w=[pk])
                        K.op(act, lambda e, pt=pt: e.activation(out=F["rr"][:, 0:T], in_=pt[:, 0:T], func=AF.Ln,
                                                                bias=kc[:, 0:1], scale=1.0), r=[pk, "kc"], w=["f_rr"])
                        K.op(act, lambda e: e.activation(out=F["rr"][:, 0:T], in_=F["rr"][:, 0:T], func=AF.Exp, scale=-0.5),
                             r=["f_rr"], w=["f_rr"])
                        yield
                        if nm_ == "csq":
                            K.op(dve, lambda e: e.scalar_tensor_tensor(
                                out=F["qn"][:, 0:T], in0=F["csq"][:, 0:T], scalar=128.0 ** -0.5, in1=F["rr"][:, 0:T],
                                op0=ALU.mult, op1=ALU.mult), r=["f_csq", "f_rr"], w=["f_qn"])
                            K.op(pool, lambda e: e.tensor_copy(out=B["q"][:, 0:T], in_=F["qn"][:, 0:T]),
                                 r=["f_qn"], w=["b_q"])
                        else:
                            K.op(dve, lambda e: e.tensor_tensor(out=F["csk"][:, 0:T], in0=F["csk"][:, 0:T],
                                                                in1=F["rr"][:, 0:T], op=ALU.mult),
                                 r=["f_csk", "f_rr"], w=["f_csk"])
                            K.op(pool, lambda e: e.tensor_copy(out=B["k"][:, 0:T], in_=F["csk"][:, 0:T]),
                                 r=["f_csk"], w=["b_k"])
                    pt, pk = K.ps()
                    K.op(pe, lambda e, pt=pt: e.matmul(pt[:, 0:T], lhsT=sel[j], rhs=P[:, 16, 3:3 + T], start=True,
                                                       stop=True), r=[Pk(16), "cm"], w=[pk])
                    K.op(act, lambda e, pt=pt: e.activation(out=F["beta"][:, 0:T], in_=pt[:, 0:T], func=AF.Sigmoid),
                         r=[pk], w=["f_beta"])
                    pt, pk = K.ps()
                    K.op(pe, lambda e, pt=pt: e.matmul(pt[:, 0:T], lhsT=sel[2 + j], rhs=P[:, 16, 3:3 + T], start=True,
                                                       stop=True), r=[Pk(16), "cm"], w=[pk])
                    K.op(act, lambda e, pt=pt: e.activation(out=F["g"][:, 0:T], in_=pt[:, 0:T], func=AF.Exp,
                                                            bias=dnc[:, 2 + j:3 + j], scale=1.0), r=[pk, "dnc"], w=["f_g"])
                    K.op(act, lambda e: e.activation(out=F["g"][:, 0:T], in_=F["g"][:, 0:T], func=AF.Ln,
                                                     bias=kc[:, 1:2], scale=1.0), r=["f_g", "kc"], w=["f_g"])
                    K.op(dve, lambda e: e.tensor_scalar(out=F["g"][:, 0:T], in0=F["g"][:, 0:T], scalar1=negA[:, j:j + 1],
                                                        scalar2=None, op0=ALU.mult), r=["f_g", "negA"], w=["f_g"])
                    K.op(dve, lambda e: e.tensor_tensor_scan(out=F["d"][:, 0:T], data0=rmask[:, 0:T], data1=F["g"][:, 0:T],
                                                             initial=0.0, op0=ALU.mult, op1=ALU.add),
                         r=["f_g", "cr"], w=["f_d"])
                    K.op(act, lambda e: e.activation(out=F["e"][:, 0:T], in_=F["d"][:, 0:T], func=AF.Exp), r=["f_d"], w=["f_e"])
                    yield
                    for c in range(nchunk):
                        c0 = c * C
                        K.op(act, lambda e, c0=c0: e.activation(
                            out=F["kd"][:, c0:c0 + C], in_=F["d"][:, c0:c0 + C], func=AF.Exp, scale=-1.0,
                            bias=F["d"][:, c0 + C - 1:c0 + C]), r=["f_d"], w=["f_kd"])
                    K.op(dve, lambda e: e.tensor_tensor(out=F["eb"][:, 0:T], in0=F["e"][:, 0:T], in1=F["beta"][:, 0:T],
                                                        op=ALU.mult), r=["f_e", "f_beta"], w=["f_eb"])
                    K.op(pool, lambda e: e.tensor_tensor(out=B["kb"][:, 0:T], in0=F["csk"][:, 0:T], in1=F["beta"][:, 0:T],
                                                         op=ALU.mult), r=["f_csk", "f_beta"], w=["b_kb"])
                    K.op(dve, lambda e: e.scalar_tensor_tensor(out=B["kc"][:, 0:T], in0=F["csk"][:, 0:T], scalar=-1.0,
                                                               in1=F["eb"][:, 0:T], op0=ALU.mult, op1=ALU.mult),
                         r=["f_csk", "f_eb"], w=["b_kc"])
                    K.op(pool, lambda e: e.tensor_tensor(out=B["kdT"][:, 0:T], in0=F["csk"][:, 0:T], in1=F["kd"][:, 0:T],
                                                         op=ALU.mult), r=["f_csk", "f_kd"], w=["b_kdT"])
                    K.op(pool, lambda e: e.tensor_tensor(out=B["vb"][:, 0:T], in0=F["csv"][:, 0:T], in1=F["beta"][:, 0:T],
                                                         op=ALU.mult), r=["f_csv", "f_beta"], w=["b_vb"])
                    K.op(dve, lambda e: e.tensor_tensor(out=B["qd"][:, 0:T], in0=F["qn"][:, 0:T], in1=F["e"][:, 0:T],
                                                        op=ALU.mult), r=["f_qn", "f_e"], w=["b_qd"])
                    yield
                    for c in range(nchunk):
                        c0 = c * C
                        cs_ = slice(c0, c0 + C)
                        K.op(dve, lambda e, cs_=cs_: e.scalar_tensor_tensor(
                            out=C_["junk"][0:C, 0:C], in0=F["d"][0:C, cs_], scalar=1.0, in1=ident[0:C, 0:C],
                            op0=ALU.mult, op1=ALU.mult, accum_out=dcol[0:C, 0:1]), r=["f_d", "cm"], w=["c_junk", "dcol"])
                        K.op(dve, lambda e, cs_=cs_: e.tensor_scalar(
                            out=C_["arg"][0:C, 0:C], in0=F["d"][0:C, cs_], scalar1=dcol[0:C, 0:1], scalar2=0.0,
                            op0=ALU.subtract, op1=ALU.min), r=["f_d", "dcol"], w=["c_arg"])
                        K.op(act, lambda e: e.activation(out=C_["gam"][0:C, 0:C], in_=C_["arg"][0:C, 0:C], func=AF.Exp),
                             r=["c_arg"], w=["c_gam"])
                        K.op(pool, lambda e: e.tensor_tensor(out=C_["gamI"][0:C, 0:C], in0=C_["gam"][0:C, 0:C],
                                                             in1=triI[0:C, 0:C], op=ALU.mult), r=["c_gam", "cm"], w=["c_gamI"])
                        K.op(pool, lambda e: e.tensor_tensor(out=C_["gamS"][0:C, 0:C], in0=C_["gam"][0:C, 0:C],
                                                             in1=triS[0:C, 0:C], op=ALU.mult), r=["c_gam", "cm"], w=["c_gamS"])
                        pt, pk = K.ps()
                        K.op(pe, lambda e, pt=pt, cs_=cs_: e.matmul(pt[0:C, 0:C], lhsT=B["k"][:, cs_], rhs=B["kb"][:, cs_],
                                                                    start=True, stop=True), r=["b_k", "b_kb"], w=[pk])
                        K.op(dve, lambda e, pt=pt: e.scalar_tensor_tensor(
                            out=CB["N0" if C == 128 else "Pa"][0:C, 0:C], in0=pt[0:C, 0:C], scalar=-1.0, in1=C_["gamS"][0:C, 0:C],
                            op0=ALU.mult, op1=ALU.mult), r=[pk, "c_gamS"], w=["cb_N0" if C == 128 else "cb_Pa"])
                        if C == 128:
                            pt, pk = K.ps()
                            K.op(pe, lambda e, pt=pt: e.matmul(pt[:, 0:128], lhsT=CB["N0"][:, :], rhs=identb[:, :],
                                                               start=True, stop=True), r=["cb_N0", "identb"], w=[pk])
                            K.op(act, lambda e, pt=pt: e.copy(out=CB["PT0"][:, :], in_=pt[:, 0:128]), r=[pk], w=["cb_PT0"])
                            K.op(pool, lambda e: e.tensor_tensor(out=CB["Pa"][:, :], in0=CB["N0"][:, :], in1=bm16, op=ALU.mult),
                                 r=["cb_N0", "cm"], w=["cb_Pa"])
                        pt, pk = K.ps()
                        K.op(pe, lambda e, pt=pt, cs_=cs_: e.matmul(pt[0:C, 0:C], lhsT=B["k"][:, cs_], rhs=B["q"][:, cs_],
                                                                    start=True, stop=True), r=["b_k", "b_q"], w=[pk])
                        K.op(dve, lambda e, pt=pt: e.tensor_tensor(out=CB["qk"][0:C, 0:C], in0=pt[0:C, 0:C],
                                                                   in1=C_["gamI"][0:C, 0:C], op=ALU.mult),
                             r=[pk, "c_gamI"], w=["cb_qk"])
                        yield
                        pt, pk = K.ps()
                        K.op(pe, lambda e, pt=pt: e.matmul(pt[0:C, 0:C], lhsT=CB["Pa"][0:C, 0:C], rhs=identb[0:C, 0:C],
                                                           start=True, stop=True), r=["cb_Pa", "identb"], w=[pk])
                        K.op(act, lambda e, pt=pt: e.copy(out=CB["PTa"][0:C, 0:C], in_=pt[0:C, 0:C]), r=[pk], w=["cb_PTa"])
                        K.op(dve, lambda e: e.tensor_tensor(out=CB["Ya"][0:C, 0:C], in0=CB["Pa"][0:C, 0:C],
                                                            in1=ident[0:C, 0:C], op=ALU.add), r=["cb_Pa", "cm"], w=["cb_Ya"])
                        cur, nxt = "a", "b"
                        for l in range(1, min(L, 4)):
                            Pc, PTc, Yc = "P" + cur, "PT" + cur, "Y" + cur
                            Pn, PTn, Yn = "P" + nxt, "PT" + nxt, "Y" + nxt
                            pt, pk = K.ps()
                            K.op(pe, lambda e, pt=pt, Pc=Pc, PTc=PTc: e.matmul(
                                pt[0:C, 0:C], lhsT=CB[Pc][0:C, 0:C], rhs=CB[PTc][0:C, 0:C], start=True, stop=True),
                                r=["cb_" + Pc, "cb_" + PTc], w=[pk])
                            K.op(act, lambda e, pt=pt, PTn=PTn: e.copy(out=CB[PTn][0:C, 0:C], in_=pt[0:C, 0:C]),
                                 r=[pk], w=["cb_" + PTn])
                            if l < min(L, 4) - 1:
                                pt, pk = K.ps()
                                K.op(pe, lambda e, pt=pt, Pc=Pc, PTc=PTc: e.matmul(
                                    pt[0:C, 0:C], lhsT=CB[PTc][0:C, 0:C], rhs=CB[Pc][0:C, 0:C], start=True, stop=True),
                                    r=["cb_" + Pc, "cb_" + PTc], w=[pk])
                                K.op(act, lambda e, pt=pt, Pn=Pn: e.copy(out=CB[Pn][0:C, 0:C], in_=pt[0:C, 0:C]),
                                     r=[pk], w=["cb_" + Pn])
                            pt, pk = K.ps()
                            K.op(pe, lambda e, pt=pt, PTn=PTn, Yc=Yc: e.matmul(
                                pt[0:C, 0:C], lhsT=CB[PTn][0:C, 0:C], rhs=CB[Yc][0:C, 0:C], start=True, stop=True),
                                r=["cb_" + PTn, "cb_" + Yc], w=[pk])
                            K.op(dve, lambda e, pt=pt, Yc=Yc, Yn=Yn: e.tensor_tensor(
                                out=CB[Yn][0:C, 0:C], in0=pt[0:C, 0:C], in1=CB[Yc][0:C, 0:C], op=ALU.add),
                                r=[pk, "cb_" + Yc], w=["cb_" + Yn])
                            yield
                            cur, nxt = nxt, cur
                        Yf = "Y" + cur
                        if C == 128:
                            Ec, En = "Y" + cur, "Y" + nxt
                            Dc, Dn = "Dva", "Dvb"
                            pt, pk = K.ps()
                            K.op(pe, lambda e, pt=pt, Ec=Ec: e.matmul(pt[:, 0:128], lhsT=CB[Ec][:, :], rhs=identb[:, :],
                                                                      start=True, stop=True), r=["cb_" + Ec, "identb"], w=[pk])
                            K.op(act, lambda e, pt=pt, Dc=Dc: e.copy(out=CB[Dc][:, :], in_=pt[:, 0:128]), r=[pk], w=["cb_" + Dc])
                            yield
                            for li in range(3):
                                mk, mkT = lvm[li]
                                K.op(pool, lambda e, mk=mk: e.tensor_tensor(out=CB["PTm"][:, :], in0=CB["PT0"][:, :], in1=mk, op=ALU.mult),
                                     r=["cb_PT0", "cm"], w=["cb_PTm"])
                                pt, pk = K.ps()
                                K.op(pe, lambda e, pt=pt, Ec=Ec: e.matmul(pt[:, 0:128], lhsT=CB["PTm"][:, :], rhs=CB[Ec][:, :],
                                                                          start=True, stop=True), r=["cb_PTm", "cb_" + Ec], w=[pk])
                                K.op(act, lambda e, pt=pt: e.copy(out=CB["W"][:, :], in_=pt[:, 0:128]), r=[pk], w=["cb_W"])
                                pt, pk = K.ps()
                                K.op(pe, lambda e, pt=pt, Dc=Dc: e.matmul(pt[:, 0:128], lhsT=CB[Dc][:, :], rhs=CB["W"][:, :],
                                                                          start=True, stop=True), r=["cb_W", "cb_" + Dc], w=[pk])
                                K.op(dve, lambda e, pt=pt, Ec=Ec, En=En: e.tensor_tensor(out=CB[En][:, :], in0=pt[:, 0:128], in1=CB[Ec][:, :],
                                                                                       op=ALU.add), r=[pk, "cb_" + Ec], w=["cb_" + En])
                                yield
                                if li < 2:
                                    K.op(pool, lambda e, mkT=mkT: e.tensor_tensor(out=CB["N0m"][:, :], in0=CB["N0"][:, :], in1=mkT, op=ALU.mult),
                                         r=["cb_N0", "cm"], w=["cb_N0m"])
                                    pt, pk = K.ps()
                                    K.op(pe, lambda e, pt=pt, Dc=Dc: e.matmul(pt[:, 0:128], lhsT=CB["N0m"][:, :], rhs=CB[Dc][:, :],
                                                                              start=True, stop=True), r=["cb_N0m", "cb_" + Dc], w=[pk])
                                    K.op(act, lambda e, pt=pt: e.copy(out=CB["V"][:, :], in_=pt[:, 0:128]), r=[pk], w=["cb_V"])
                                    pt, pk = K.ps()
                                    K.op(pe, lambda e, pt=pt, Ec=Ec: e.matmul(pt[:, 0:128], lhsT=CB[Ec][:, :], rhs=CB["V"][:, :],
                                                                              start=True, stop=True), r=["cb_V", "cb_" + Ec], w=[pk])
                                    K.op(dve, lambda e, pt=pt, Dc=Dc, Dn=Dn: e.tensor_tensor(out=CB[Dn][:, :], in0=pt[:, 0:128], in1=CB[Dc][:, :],
                                                                                           op=ALU.add), r=[pk, "cb_" + Dc], w=["cb_" + Dn])
                                    Dc, Dn = Dn, Dc
                                Ec, En = En, Ec
                            Yf = Ec
                        pt, pk = K.ps()
                        K.op(pe, lambda e, pt=pt, cs_=cs_: e.matmul(pt[0:C, 0:128], lhsT=B["kdT"][:, cs_], rhs=identb[:, :],
                                                                    start=True, stop=True), r=["b_kdT", "identb"], w=[pk])
                        K.op(act, lambda e, pt=pt: e.copy(out=CB["kdec"][0:C, :], in_=pt[0:C, 0:128]), r=[pk], w=["cb_kdec"])
                        yield
                        sk, skb = ("Sdn", j), ("Sdnb", j)
                        pt, pk = K.ps()
                        K.op(pe, lambda e, pt=pt, cs_=cs_: e.matmul(pt[0:C, 0:128], lhsT=B["vb"][:, cs_], rhs=identb[:, :],
                                                                    start=True, stop=False), r=["b_vb", "identb"], w=[pk])
                        K.op(pe, lambda e, pt=pt, cs_=cs_: e.matmul(pt[0:C, 0:128], lhsT=B["kc"][:, cs_], rhs=Sdnb[:, j, :],
                                                                    start=False, stop=True), r=["b_kc", skb], w=[pk])
                        K.op(act, lambda e, pt=pt: e.copy(out=CB["R"][0:C, :], in_=pt[0:C, 0:128]), r=[pk], w=["cb_R"])
                        yield
                        pt, pk = K.ps()
                        K.op(pe, lambda e, pt=pt, Yf=Yf: e.matmul(pt[0:C, 0:128], lhsT=CB[Yf][0:C, 0:C], rhs=CB["R"][0:C, :],
                                                           start=True, stop=True), r=["cb_" + Yf, "cb_R"], w=[pk])
                        K.op(act, lambda e, pt=pt: e.copy(out=CB["u"][0:C, :], in_=pt[0:C, 0:128]), r=[pk], w=["cb_u"])
                        yield
                        pt, pk = K.ps()
                        K.op(pe, lambda e, pt=pt, cs_=cs_: e.matmul(pt[:, 0:C], lhsT=Sdnb[:, j, :], rhs=B["qd"][:, cs_],
                                                                    start=True, stop=False), r=[skb, "b_qd"], w=[pk])
                        K.op(pe, lambda e, pt=pt: e.matmul(pt[:, 0:C], lhsT=CB["u"][0:C, :], rhs=CB["qk"][0:C, 0:C],
                                                           start=False, stop=True), r=["cb_u", "cb_qk"], w=[pk])
                        K.op(act, lambda e, pt=pt, cs_=cs_: e.copy(out=F["oT"][:, cs_], in_=pt[:, 0:C]), r=[pk], w=["f_oT"])
                        yield
                        pt, pk = K.ps()
                        K.op(pe, lambda e, pt=pt: e.matmul(pt[:, 0:128], lhsT=CB["kdec"][0:C, :], rhs=CB["u"][0:C, :],
                                                           start=True, stop=True), r=["cb_kdec", "cb_u"], w=[pk])
                        K.op(act, lambda e, c0=c0: e.activation(out=dcol[:, 1:2], in_=F["d"][:, c0 + C - 1:c0 + C], func=AF.Exp),
                             r=["f_d"], w=["st3"])
                        K.op(dve, lambda e, pt=pt: e.scalar_tensor_tensor(
                            out=Sdn[:, j, :], in0=Sdn[:, j, :], scalar=dcol[:, 1:2], in1=pt[:, 0:128],
                            op0=ALU.mult, op1=ALU.add), r=[sk, "st3", pk], w=[sk])
                        K.op(pool, lambda e: e.tensor_copy(out=Sdnb[:, j, :], in_=Sdn[:, j, :]), r=[sk], w=[skb])
                        yield
                    rms_gate_out([(F["oT"][:, 0:T], "f_oT")], [m0 + 3], 128.0, dnc[:, 4:5], T, [dst_rows_col(j)], F, B)

                def ret_gen(T, C, pos0, dst_rows_col, F, B, CB, CB2):
                    nchunk = T // C
                    Pk = lambda m: ("P", id(P), m)
                    xi_t, ze_t, dmT, cdr = (xi128, ze128, dm128, cd128) if C == 128 else (xi16, ze16, dm16, cd16)
                    K.op(dve, lambda e: e.tensor_scalar(out=F["u"][:, 0:T], in0=iota[:, 0:T], scalar1=float(pos0),
                                                        scalar2=inv2pi, op0=ALU.add, op1=ALU.mult), r=["cr"], w=["f_u"])
                    for fn_, off in (("sin", 0.0), ("cos", 0.25)):
                        if off:
                            K.op(dve, lambda e: e.tensor_scalar(out=F["u"][:, 0:T], in0=F["u"][:, 0:T], scalar1=0.25,
                                                                scalar2=None, op0=ALU.add), r=["f_u"], w=["f_u"])
                        K.op(dve, lambda e: e.tensor_scalar(out=F["t1"][:, 0:T], in0=F["u"][:, 0:T], scalar1=MAGIC,
                                                            scalar2=None, op0=ALU.add), r=["f_u"], w=["f_t1"])
                        K.op(dve, lambda e: e.scalar_tensor_tensor(out=F["nf"][:, 0:T], in0=F["t1"][:, 0:T], scalar=MAGIC,
                                                                   in1=F["u"][:, 0:T], op0=ALU.subtract, op1=ALU.subtract),
                             r=["f_t1", "f_u"], w=["f_nf"])
                        K.op(act, lambda e, fn_=fn_: e.activation(out=F[fn_][:, 0:T], in_=F["nf"][:, 0:T], func=AF.Sin,
                                                                  scale=-6.28318), r=["f_nf"], w=["f_" + fn_])
                        yield
                    for (ce, co, o1, o2) in ((8, 9, "q1", "q2"), (10, 11, "k1", "k2")):
                        xe = P[:, ce, 3:3 + T]; xo = P[:, co, 3:3 + T]
                        rk_ = [Pk(ce), Pk(co), "f_sin", "f_cos"]
                        K.op(dve, lambda e, xe=xe: e.tensor_tensor(out=F["ta"][:, 0:T], in0=xe, in1=F["cos"][:, 0:T], op=ALU.mult),
                             r=rk_, w=["f_ta"])
                        K.op(pool, lambda e, xo=xo: e.tensor_tensor(out=F["tb"][:, 0:T], in0=xo, in1=F["sin"][:, 0:T], op=ALU.mult),
                             r=rk_, w=["f_tb"])
                        K.op(dve, lambda e, o1=o1: e.tensor_tensor(out=F[o1][:, 0:T], in0=F["ta"][:, 0:T], in1=F["tb"][:, 0:T],
                                                                   op=ALU.subtract), r=["f_ta", "f_tb"], w=["f_" + o1])
                        K.op(dve, lambda e, xo=xo: e.tensor_tensor(out=F["ta"][:, 0:T], in0=xo, in1=F["cos"][:, 0:T], op=ALU.mult),
                             r=rk_ + ["f_ta"], w=["f_ta"])
                        K.op(pool, lambda e, xe=xe: e.tensor_tensor(out=F["tb"][:, 0:T], in0=xe, in1=F["sin"][:, 0:T], op=ALU.mult),
                             r=rk_ + ["f_tb"], w=["f_tb"])
                        K.op(dve, lambda e, o2=o2: e.tensor_tensor(out=F[o2][:, 0:T], in0=F["ta"][:, 0:T], in1=F["tb"][:, 0:T],
                                                                   op=ALU.add), r=["f_ta", "f_tb"], w=["f_" + o2])
                        yield
                    for n_ in ("q1", "q2", "k1", "k2"):
                        K.op(pool, lambda e, n_=n_: e.tensor_copy(out=B[n_][:, 0:T], in_=F[n_][:, 0:T]), r=["f_" + n_], w=["b_" + n_])
                    for n_, s_ in (("qx1", "q1"), ("qx2", "q2")):
                        K.op(dve, lambda e, n_=n_, s_=s_: e.tensor_tensor(out=B[n_][:, 0:T], in0=F[s_][:, 0:T], in1=xi_t[:, 0:T],
                                                                          op=ALU.mult), r=["f_" + s_, "rc"], w=["b_" + n_])
                    for n_, s_ in (("kz1", "k1"), ("kz2", "k2")):
                        K.op(pool, lambda e, n_=n_, s_=s_: e.tensor_tensor(out=B[n_][:, 0:T], in0=F[s_][:, 0:T], in1=ze_t[:, 0:T],
                                                                           op=ALU.mult), r=["f_" + s_, "rc"], w=["b_" + n_])
                    for i_ in range(2):
                        K.op(pool, lambda e, i_=i_: e.tensor_copy(out=B["rv%d" % i_][:, 0:T], in_=P[:, 12 + i_, 3:3 + T]),
                             r=[Pk(12 + i_)], w=["b_rv%d" % i_])
                        yield
                    for c in range(nchunk):
                        c0 = c * C
                        cs_ = slice(c0, c0 + C)
                        pt, pk = K.ps()
                        for h_ in range(2):
                            K.op(pe, lambda e, pt=pt, h_=h_, cs_=cs_: e.matmul(
                                pt[0:C, 0:C], lhsT=B["k%d" % (h_ + 1)][:, cs_], rhs=B["q%d" % (h_ + 1)][:, cs_],
                                start=(h_ == 0), stop=(h_ == 1)), r=["b_k1", "b_k2", "b_q1", "b_q2"], w=[pk])
                        K.op(dve, lambda e, pt=pt: e.tensor_tensor(out=CB["rqk"][0:C, 0:C], in0=pt[0:C, 0:C], in1=dmT[0:C, 0:C],
                                                                   op=ALU.mult), r=[pk, "rc"], w=["cb_rqk"])
                        yield
                        for dst_, srcs_ in (("v", ("rv0", "rv1")), ("kz", ("kz1", "kz2"))):
                            for h_ in range(2):
                                pt, pk = K.ps()
                                K.op(pe, lambda e, pt=pt, h_=h_, cs_=cs_, srcs_=srcs_: e.matmul(
                                    pt[0:C, 0:128], lhsT=B[srcs_[h_]][:, cs_], rhs=identb[:, :],
                                    start=True, stop=True), r=["b_" + srcs_[h_], "identb"], w=[pk])
                                K.op(act, lambda e, pt=pt, dst_=dst_, h_=h_: e.copy(out=CB2[dst_][0:C, h_ * 128:(h_ + 1) * 128],
                                                                                  in_=pt[0:C, 0:128]), r=[pk], w=["cb2_" + dst_])
                                yield
                        for hv in range(2):
                            pt, pk = K.ps()
                            K.op(pe, lambda e, pt=pt, hv=hv: e.matmul(pt[:, 0:C], lhsT=CB2["v"][0:C, hv * 128:(hv + 1) * 128],
                                                                      rhs=CB["rqk"][0:C, 0:C], start=True, stop=False),
                                 r=["cb2_v", "cb_rqk"], w=[pk])
                            for kh in range(2):
                                K.op(pe, lambda e, pt=pt, hv=hv, kh=kh, cs_=cs_: e.matmul(
                                    pt[:, 0:C], lhsT=Srb[:, kh, hv * 128:(hv + 1) * 128], rhs=B["qx%d" % (kh + 1)][:, cs_],
                                    start=False, stop=(kh == 1)), r=["Srb", "b_qx1", "b_qx2"], w=[pk])
                            K.op(act, lambda e, pt=pt, hv=hv, cs_=cs_: e.copy(out=F["or%d" % hv][:, cs_], in_=pt[:, 0:C]),
                                 r=[pk], w=["f_or%d" % hv])
                            yield
                        for kh in range(2):
                            pt, pk = K.ps()
                            K.op(pe, lambda e, pt=pt, kh=kh: e.matmul(pt[:, 0:256], lhsT=CB2["kz"][0:C, kh * 128:(kh + 1) * 128],
                                                                      rhs=CB2["v"][0:C, :], start=True, stop=True),
                                 r=["cb2_kz", "cb2_v"], w=[pk])
                            K.op(dve, lambda e, pt=pt, kh=kh: e.scalar_tensor_tensor(
                                out=Sr[:, kh, :], in0=Sr[:, kh, :], scalar=cdr, in1=pt[:, 0:256], op0=ALU.mult, op1=ALU.add),
                                r=["Sr", "rc", pk], w=["Sr"])
                        K.op(pool, lambda e: e.tensor_copy(out=Srb[:, :, :], in_=Sr[:, :, :]), r=["Sr"], w=["Srb"])
                        yield
                    rms_gate_out([(F["or0"][:, 0:T], "f_or0"), (F["or1"][:, 0:T], "f_or1")], [14, 15], 256.0, None, T,
                                 [dst_rows_col(2), dst_rows_col(3)], F, B)


                def mixer_tile(T, C, pos0, dst_rows_col, extra=None):
                    gens = [("d0_", dn_gen(0, T, C, dst_rows_col, *DNS[0])), ("d1_", dn_gen(1, T, C, dst_rows_col, *DNS[1])),
                            ("r_", ret_gen(T, C, pos0, dst_rows_col, *RTS))]
                    lists = []
                    for ns_, g_ in gens:
                        K.ns = ns_
                        K.rec = []
                        for _ in g_:
                            pass
                        lists.append(K.rec)
                        K.rec = None
                    if extra is not None:
                        K.ns = "x_"
                        K.rec = []
                        extra()
                        lists.append(K.rec)
                        K.rec = None
                    K.ns = ""
                    K.replay(lists)

                conv_ch = (0, 1, 2, 4, 5, 6)
                nsup = S // T1
                per_sup = 592 // nsup + 1
                def prep(t, dst):
                    for sub in range(2):
                        norm_transpose(x1[t * T1 + sub * 128: t * T1 + (sub + 1) * 128, :], 128, sub * 128, xt[0], [(0, 128, 0)])
                    project(T1, dst, 3)

                prep(0, Pbufs[0])
                for t in range(nsup):
                    pump(per_sup)
                    P = Pbufs[t % 2]
                    extra_fn = None
                    if t + 1 < nsup:
                        def extra_fn(t=t, cur=Pbufs[t % 2], nxt=Pbufs[(t + 1) % 2]):
                            K.op(pool, lambda e: e.tensor_copy(out=nxt[:, :, 0:3], in_=cur[:, :, T1:T1 + 3]),
                                 r=[("P", id(cur), m) for m in range(NCH)], w=[("P", id(nxt), m) for m in range(NCH)])
                            prep(t + 1, nxt)
                    mixer_tile(T1, 128, t * T1, lambda jj, t=t: (ib[t * 512 + jj * 128: t * 512 + (jj + 1) * 128, :], ("ib", t)),
                               extra=extra_fn)
                    if not os.environ.get("SKIP_CC"):
                        K.allgather(ib[t * 512:(t + 1) * 512, :].opt(), ob[t * 2048:(t + 1) * 2048, :].opt(),
                                    [[0, 1, 2, 3], [4, 5, 6, 7]], r=[("ib", t)], w=[("ob", t)])
                Psm = Pbufs[0] if P is Pbufs[1] else Pbufs[1]
                for ci, m in enumerate(conv_ch):
                    K.dma(sp, convp[:, ci * 3:(ci + 1) * 3], P[:, m, T1:T1 + 3], r=[("P", id(P), m)])
                K.dma(sp, deltap.ap().rearrange("j k v -> k j v"), Sdn[:, :, :], r=[("Sdn", 0), ("Sdn", 1)])
                K.dma(sp, retp.ap().rearrange("(h k) v -> k h v", h=2), Sr[:, :, :], r=["Sr"])
                norm_transpose(xs1[:, :], 64, 0, xt[0], [(16 * i, 16 * (i + 1), 1 + i) for i in range(4)])
                project(64, Psm, 0)
                cstt = K.sb("cstt", [128, 6, 3], es=e1)
                for i in range(4):
                    K.dma(sp, cstt[:, :, :].rearrange("p a b -> p (a b)"), cst_in[i, :, :], w=["cstt"])
                    for m in range(NCH):
                        K.op(pool, lambda e, m=m, i=i: e.tensor_copy(out=P[:, m, 3:19], in_=Psm[:, m, 16 * i:16 * (i + 1)]),
                             r=[("P", id(Psm), m)], w=[("P", id(P), m)])
                    for ci, m in enumerate(conv_ch):
                        K.op(pool, lambda e, m=m, ci=ci: e.tensor_copy(out=P[:, m, 0:3], in_=cstt[:, ci, :]),
                             r=["cstt"], w=[("P", id(P), m)])
                    K.dma(sp, Sdn[:, :, :], sd_in[i].rearrange("j k v -> k j v"), w=[("Sdn", 0), ("Sdn", 1)])
                    K.dma(sp, Sr[:, :, :], sr_in[i].rearrange("(h k) v -> k h v", h=2), w=["Sr"])
                    for j in range(2):
                        K.op(pool, lambda e, j=j: e.tensor_copy(out=Sdnb[:, j, :], in_=Sdn[:, j, :]), r=[("Sdn", j)], w=[("Sdnb", j)])
                    K.op(pool, lambda e: e.tensor_copy(out=Srb[:, :, :], in_=Sr[:, :, :]), r=["Sr"], w=["Srb"])
                    mixer_tile(16, 16, PAST_LEN, lambda jj, i=i: (ibs[i * 512 + jj * 128: i * 512 + (jj + 1) * 128, :], "ibs"))
                    for ci, m in enumerate(conv_ch):
                        K.dma(sp, convs[i, :, ci * 3:(ci + 1) * 3], P[:, m, 16:19], r=[("P", id(P), m)])
                    K.dma(sp, deltas[i].rearrange("j k v -> k j v"), Sdn[:, :, :], r=[("Sdn", 0), ("Sdn", 1)])
                    K.dma(sp, rets[i].rearrange("(h k) v -> k h v", h=2), Sr[:, :, :], r=["Sr"])
            pump(10000)
            K.barrier()
            if not os.environ.get("SKIP_CC"):
                K.allgather(ibs.ap().opt(), obs.ap().opt(), [[0, 1, 2, 3], [4, 5, 6, 7]], r=["ibs"], w=["obs"])

            with contextlib.ExitStack() as e2:
                NSUB = 4
                TT = NSUB * 128
                gi = K.sb("gi", [128, 2 * KD], I32, es=e2)
                adaT1 = K.sb("adaT2", [128, 4, D], es=e2)
                K.dma(sp, adaT1[:, :, :].rearrange("p b c -> p (b c)"), adaT_d[:, 0:4 * D], w=["adaT"])
                K.dma(sp, gi[:, :], gidx_in[:, :], w=["gi"])
                xr = [K.sb(f"xr{i}", [128, D], es=e2) for i in range(NSUB)]
                xn2s = [K.sb(f"xn2_{i}", [128, D], BF16, es=e2) for i in range(2)]
                st2s = [K.sb(f"st2_{i}", [128, 4], es=e2) for i in range(2)]
                mT = K.sb("mT", [128, KD, TT], BF16, es=e2)
                aT = K.sb("aT", [128, NJ, TT], BF16, es=e2)
                gsb = K.sb("gsb", [128, TT], es=e2)
                tmp = K.sb("tmp2", [128, 512], es=e2)
                NSLOT = 3
                wsl = [K.sb(f"wsl{i}", [128, KD, 512], BF16, es=e2) for i in range(NSLOT)]
                slot_n = [0]

                def wload(src_ap, nk, keys):
                    i = slot_n[0] % NSLOT
                    slot_n[0] += 1
                    K.dma(sp if i % 2 == 0 else act, wsl[i][:, 0:nk, :], src_ap, r=keys, w=[("wsl", i)])
                    return wsl[i], ("wsl", i)

                tiles = []
                t0 = 0
                while t0 < SEG:
                    n = min(TT, SEG - t0)
                    tiles.append((t0, [128] * (n // 128), 0))
                    t0 += n
                tiles.append((SEG, [16], 1))
                for (tok0, subs, ri) in tiles:
                    if ri == 1:
                        K.dma(sp, adaT1[:, :, :].rearrange("p b c -> p (b c)"), adaT_d[:, 4 * D:8 * D], r=["adaT"], w=["adaT"])
                    nt = sum(subs)
                    offs = [sum(subs[:i]) for i in range(len(subs))]
                    for si, ns in enumerate(subs):
                        src = x2[tok0 + offs[si]: tok0 + offs[si] + ns, :] if ri == 0 else xs2[:, :]
                        K.dma(sp, xr[si][0:ns, :], src, w=[("xr", si)])
                    for k in range(KD):
                        if ri == 1:
                            K.dma(pool, mT[:, k, 0:16], obs[:, :], r=["obs", "gi"], w=[("mT", k)], indirect=gi[:, KD + k:KD + k + 1])
                            continue
                        for h in range(nt // T1):
                            tl = tok0 // T1 + h
                            K.dma(pool, mT[:, k, h * T1:(h + 1) * T1], ob[:, :], r=[("ob", t_) for t_ in range(NSUP)] + ["gi"],
                                  w=[("mT", k)], indirect=gi[:, k:k + 1], eoff=tl * 2048 * T1)
                    for n in range(4):
                        wt, wk = wload(wo_bf[n], KD, [("wo", n)])
                        for si, ns in enumerate(subs):
                            pt, pk = K.ps()
                            for k in range(KD):
                                K.op(pe, lambda e, pt=pt, k=k, si=si, ns=ns, wt=wt: e.matmul(
                                    pt[0:ns, :], lhsT=mT[:, k, offs[si]:offs[si] + ns], rhs=wt[:, k, :],
                                    start=(k == 0), stop=(k == KD - 1)), r=[("mT", k), wk], w=[pk])
                            K.op(dve, lambda e, pt=pt, ns=ns, n=n: e.tensor_tensor(
                                out=tmp[0:ns, :], in0=pt[0:ns, :], in1=adaT1[0:ns, 0, n * 512:(n + 1) * 512], op=ALU.mult),
                                r=[pk, "adaT"], w=["tmp2"])
                            K.op(pool, lambda e, si=si, ns=ns, n=n: e.tensor_tensor(
                                out=xr[si][0:ns, n * 512:(n + 1) * 512], in0=xr[si][0:ns, n * 512:(n + 1) * 512],
                                in1=tmp[0:ns, :], op=ALU.add), r=["tmp2", ("xr", si)], w=[("xr", si)])
                    row = 0 if ri == 0 else 5
                    for si, ns in enumerate(subs):
                        xk = ("xr", si)
                        xn2 = xn2s[si % 2]; sq2 = xn2; st2 = st2s[si % 2]
                        kx2 = ("xn2", si % 2); ks2 = ("st2", si % 2)
                        K.op(act, lambda e, si=si, ns=ns: e.activation(out=sq2[0:ns, :], in_=xr[si][0:ns, :], func=AF.Square,
                                                                       accum_out=st2[0:ns, 0:1]), r=[xk], w=[kx2, ks2])
                        K.op(act, lambda e, ns=ns: e.activation(out=st2[0:ns, 1:2], in_=st2[0:ns, 0:1], func=AF.Sqrt,
                                                                bias=kc[0:ns, 0:1], scale=1.0 / D), r=[ks2, "kc"], w=[ks2])
                        K.op(dve, lambda e, ns=ns: e.reciprocal(out=st2[0:ns, 2:3], in_=st2[0:ns, 1:2]), r=[ks2], w=[ks2])
                        K.op(dve, lambda e, si=si, ns=ns: e.tensor_scalar(out=xn2[0:ns, :], in0=xr[si][0:ns, :],
                                                                          scalar1=st2[0:ns, 2:3], scalar2=None, op0=ALU.mult),
                             r=[xk, ks2], w=[kx2])
                        for k in range(KD):
                            pt, pk = K.ps()
                            K.op(pe, lambda e, k=k, pt=pt, ns=ns: e.matmul(
                                pt[:, 0:ns], lhsT=xn2[0:ns, k * 128:(k + 1) * 128],
                                rhs=identb[0:ns, 0:ns], start=True, stop=True), r=[kx2, "identb"], w=[pk])
                            if k % 2 == 0:
                                K.op(dve, lambda e, k=k, pt=pt, si=si, ns=ns: e.tensor_scalar(
                                    out=mT[:, k, offs[si]:offs[si] + ns], in0=pt[:, 0:ns],
                                    scalar1=gmul[:, 1, k, row:row + 1], scalar2=adaF[:, 2, k, row:row + 1],
                                    op0=ALU.mult, op1=ALU.add), r=[pk, "gmul", "adaF"], w=[("mT", k)])
                            else:
                                K.op(act, lambda e, k=k, pt=pt, si=si, ns=ns: e.activation(
                                    out=mT[:, k, offs[si]:offs[si] + ns], in_=pt[:, 0:ns],
                                    func=AF.Identity, scale=gmul[:, 1, k, row:row + 1], bias=adaF[:, 2, k, row:row + 1]),
                                    r=[pk, "gmul", "adaF"], w=[("mT", k)])
                    mTk = [("mT", k) for k in range(KD)]
                    for jb in range(11):
                        wg, wgk = wload(wg_bf[jb, 0], KD, [("wg", jb, 0)])
                        wu, wuk = wload(wg_bf[jb, 1], KD, [("wg", jb, 1)])
                        for jj in range(4):
                            j = jb * 4 + jj
                            pg, pgk = K.ps()
                            for k in range(KD):
                                K.op(pe, lambda e, pg=pg, k=k, jj=jj, wg=wg: e.matmul(
                                    pg[:, 0:nt], lhsT=wg[:, k, jj * 128:(jj + 1) * 128], rhs=mT[:, k, 0:nt],
                                    start=(k == 0), stop=(k == KD - 1)), r=[wgk, ("mT", k)], w=[pgk])
                            pu, puk = K.ps()
                            for k in range(KD):
                                K.op(pe, lambda e, pu=pu, k=k, jj=jj, wu=wu: e.matmul(
                                    pu[:, 0:nt], lhsT=wu[:, k, jj * 128:(jj + 1) * 128], rhs=mT[:, k, 0:nt],
                                    start=(k == 0), stop=(k == KD - 1)), r=[wuk, ("mT", k)], w=[puk])
                            K.op(act, lambda e, pg=pg: e.activation(out=gsb[:, 0:nt], in_=pg[:, 0:nt], func=AF.Silu),
                                 r=[pgk], w=["gsb"])
                            K.op(dve, lambda e, pu=pu, j=j: e.tensor_tensor(out=aT[:, j, 0:nt], in0=gsb[:, 0:nt], in1=pu[:, 0:nt],
                                                                            op=ALU.mult), r=["gsb", puk], w=[("aT", j)])
                    for n in range(4):
                        pts = [K.ps() for _ in subs]
                        for jq in range(4):
                            wd, wdk = wload(wd_bf[n, jq], 11, [("wd", n, jq)])
                            for si, ns in enumerate(subs):
                                pt, pk = pts[si]
                                for jj in range(11):
                                    j = jq * 11 + jj
                                    K.op(pe, lambda e, pt=pt, j=j, jj=jj, si=si, ns=ns, wd=wd: e.matmul(
                                        pt[0:ns, :], lhsT=aT[:, j, offs[si]:offs[si] + ns], rhs=wd[:, jj, :],
                                        start=(j == 0), stop=(j == NJ - 1)), r=[("aT", j), wdk], w=[pk])
                        for si, ns in enumerate(subs):
                            pt, pk = pts[si]
                            K.op(dve, lambda e, pt=pt, ns=ns, n=n: e.tensor_tensor(
                                out=tmp[0:ns, :], in0=pt[0:ns, :], in1=adaT1[0:ns, 1, n * 512:(n + 1) * 512], op=ALU.mult),
                                r=[pk, "adaT"], w=["tmp2"])
                            K.op(pool, lambda e, si=si, ns=ns, n=n: e.tensor_tensor(
                                out=xr[si][0:ns, n * 512:(n + 1) * 512], in0=xr[si][0:ns, n * 512:(n + 1) * 512],
                                in1=tmp[0:ns, :], op=ALU.add), r=["tmp2", ("xr", si)], w=[("xr", si)])
                    for si, ns in enumerate(subs):
                        xk = ("xr", si)
                        xn2 = xn2s[si % 2]; sq2 = xn2; st2 = st2s[si % 2]
                        kx2 = ("xn2", si % 2); ks2 = ("st2", si % 2)
                        K.op(act, lambda e, si=si, ns=ns: e.activation(out=sq2[0:ns, :], in_=xr[si][0:ns, :], func=AF.Square,
                                                                       accum_out=st2[0:ns, 0:1]), r=[xk], w=[kx2, ks2])
                        K.op(act, lambda e, ns=ns: e.activation(out=st2[0:ns, 1:2], in_=st2[0:ns, 0:1], func=AF.Sqrt,
                                                                bias=kc[0:ns, 0:1], scale=1.0 / D), r=[ks2, "kc"], w=[ks2])
                        K.op(dve, lambda e, ns=ns: e.reciprocal(out=st2[0:ns, 2:3], in_=st2[0:ns, 1:2]), r=[ks2], w=[ks2])
                        K.op(dve, lambda e, si=si, ns=ns: e.scalar_tensor_tensor(
                            out=xr[si][0:ns, :], in0=xr[si][0:ns, :], scalar=st2[0:ns, 2:3], in1=adaT1[0:ns, 3, :],
                            op0=ALU.mult, op1=ALU.mult), r=[xk, ks2, "adaT"], w=[xk])
                        K.op(pool, lambda e, si=si, ns=ns: e.tensor_tensor(out=xr[si][0:ns, :], in0=xr[si][0:ns, :],
                                                                           in1=adaT1[0:ns, 2, :], op=ALU.add),
                             r=[xk, "adaT"], w=[xk])
                        K.dma(sp, y2[tok0 + offs[si]: tok0 + offs[si] + ns, :], xr[si][0:ns, :], r=[xk], w=["y2"])
            K.barrier()
    return nc


def _consts(r):
    j = np.arange(128)[:, None]; i = np.arange(128)[None, :]
    ident = (i == j).astype(np.float32)
    triI = (i >= j).astype(np.float32)
    triS = (i > j).astype(np.float32)
    ones = np.ones((128, 128), np.float32)
    sel = [(np.broadcast_to(j == 32 * g, (128, 128))).astype(np.float32) for g in range(4)]
    bm16 = ((i // 16) == (j // 16)).astype(np.float32)
    lv = []
    for s_ in (16, 32, 64):
        mk = (((j // (2 * s_)) == (i // (2 * s_))) & ((j % (2 * s_)) >= s_) & ((i % (2 * s_)) < s_)).astype(np.float32)
        lv += [mk, mk.T.copy()]
    cmat = np.concatenate([ident, triI, triS, ones] + sel + [bm16] + lv, axis=1)
    iota = np.broadcast_to(np.arange(256, dtype=np.float32)[None, :], (128, 256))
    rmask = np.broadcast_to((np.arange(256) % 128 != 0).astype(np.float32)[None, :], (128, 256))
    inv = (1.0 / (10000.0 ** np.linspace(0.0, 1.0, 128, dtype=np.float32))).astype(np.float32)
    inv2pi = (inv.astype(np.float64) / (2 * np.pi)).astype(np.float32)[:, None]
    crow = np.concatenate([iota, rmask, inv2pi], axis=1).astype(np.float32)
    lg = math.log(1.0 - 2.0 ** (-5.0 - r))

    def rcs(C, reps):
        idx = np.arange(C, dtype=np.float64)
        xi = np.exp((idx + 1.0) * lg); ze = np.exp((C - 1.0 - idx) * lg) / 16.0
        dm = np.zeros((128, 128))
        jj = np.arange(C)[:, None]; ii = np.arange(C)[None, :]
        dm[:C, :C] = np.where(ii >= jj, np.exp(np.where(ii >= jj, ii - jj, 0) * lg), 0.0) / 16.0
        return (np.broadcast_to(np.tile(xi, reps)[None, :], (128, C * reps)), np.broadcast_to(np.tile(ze, reps)[None, :], (128, C * reps)),
                dm, np.full((128, 1), math.exp(C * lg)))
    a = rcs(128, 2); b_ = rcs(16, 1)
    rc = np.concatenate(list(a) + list(b_), axis=1).astype(np.float32)
    assert rc.shape == (128, 802)
    return cmat, crow, rc


def _chan(r):
    out = []
    for jh in range(2):
        h = 2 * r + jh
        for base in (0, 1024, 2048):
            out.append(base + h * 128 + np.arange(128))
    return np.stack(out)


def _wcols(r):
    cols = []
    for jh in range(2):
        h = 2 * r + jh
        for base in (0, 1024, 2048, 3072):
            cols.append(base + h * 128 + np.arange(128))
    rq = 4112 + r * 256 + np.arange(256); rk = 5136 + r * 256 + np.arange(256)
    rv = 6160 + r * 256 + np.arange(256); rg = 7184 + r * 256 + np.arange(256)
    cols += [rq[0::2], rq[1::2], rk[0::2], rk[1::2], rv[:128], rv[128:], rg[:128], rg[128:]]
    small = np.concatenate([np.full(32, 4096 + 2 * r), np.full(32, 4096 + 2 * r + 1),
                            np.full(32, 4104 + 2 * r), np.full(32, 4104 + 2 * r + 1)])
    cols.append(small)
    return np.concatenate(cols)


_ROWPERM = np.concatenate([np.arange(0, 256, 2), np.arange(1, 256, 2)])


def make_in_maps(inp, S):
    SEG = S // 4
    f = lambda a: np.ascontiguousarray(a, dtype=np.float32)
    maps = []
    w_in = inp["w_in"][0]; w_out = inp["w_out"][0]
    shared = dict(
        w_gu=f(inp["w_gu"][0]), w_down=f(inp["w_down"][0]), w_ada=f(inp["w_ada"][0]),
        b_ada_f=f(inp["b_ada"][0].reshape(96, 128).T[:, :]), b_ada_r=f(inp["b_ada"][0][None, :]),
        w_adaf=f(inp["w_ada_final"]), b_adaf_r=f(inp["b_ada_final"][None, :]),
        nmix=f(inp["norm_mix"][0].reshape(16, 128).T), nffn=f(inp["norm_ffn"][0].reshape(16, 128).T),
        nfin=f(np.broadcast_to(inp["norm_final"][None, :], (128, D))),
    )
    for c in range(8):
        b, r = c // 4, c % 4
        cmat, crow, rc = _consts(r)
        ch = _chan(r)
        rows_w = np.concatenate([np.concatenate([np.arange(256 * q, 256 * q + 256), 1024 + np.arange(256 * q, 256 * q + 256)])
                                 for q in range(4)])
        crows = [inp["c_prompt"][b]] + [inp["c_sample"][4 * b + i] for i in range(4)] + [inp["c_sample"][4 * b + r]]
        m = dict(shared)
        m.update(
            x1=f(inp["x_prompt"][b]), xs1=f(inp["x_sample"][4 * b:4 * b + 4].reshape(64, D)),
            x2=f(inp["x_prompt"][b, r * SEG:(r + 1) * SEG]), xs2=f(inp["x_sample"][4 * b + r]),
            cT=f(np.stack(crows, axis=1)), w_in_o=f(w_in[:, _wcols(r)]), w_out_p=f(w_out[rows_w, :]),
            cw=f(inp["conv_w"][0][:, ch].transpose(2, 1, 0).reshape(128, 24)),
            cst=f(inp["state_conv"][0, 4 * b:4 * b + 4][:, :, ch].transpose(0, 3, 2, 1).reshape(4, 128, 18)),
            dnc=f(np.concatenate([np.broadcast_to(inp["dn_a_log"][0, 2 * r:2 * r + 2][None, :], (128, 2)),
                                  np.broadcast_to(inp["dn_dt_bias"][0, 2 * r:2 * r + 2][None, :], (128, 2)),
                                  inp["dn_norm"][0][:, None]], axis=1)),
            sd=f(inp["state_delta"][0, 4 * b:4 * b + 4, 2 * r:2 * r + 2]),
            sr=f(inp["state_ret"][0, 4 * b:4 * b + 4, r][:, _ROWPERM, :]),
            cmat=cmat, crow=crow, rc=rc,
            gidx=np.ascontiguousarray(np.concatenate([
                r * (SEG // 256) * 2048 + (np.arange(16)[None, :] // 4) * 512 + (np.arange(16)[None, :] % 4) * 128 + np.arange(128)[:, None],
                (np.arange(16)[None, :] // 4) * 2048 + r * 512 + (np.arange(16)[None, :] % 4) * 128 + np.arange(128)[:, None]], axis=1),
                dtype=np.int32),
        )
        maps.append(m)
    return maps


def assemble(res, S):
    SEG = S // 4
    yp = np.zeros((2, S, D), np.float32); ys = np.zeros((8, 16, D), np.float32)
    cp = np.zeros((1, 2, 3, 3072), np.float32); dp = np.zeros((1, 2, 8, 128, 128), np.float32)
    rp = np.zeros((1, 2, 4, 256, 256), np.float32)
    cs = np.zeros((1, 8, 3, 3072), np.float32); ds = np.zeros((1, 8, 8, 128, 128), np.float32)
    rs = np.zeros((1, 8, 4, 256, 256), np.float32)
    for c in range(8):
        b, r = c // 4, c % 4
        o = res[c]
        yp[b, r * SEG:(r + 1) * SEG] = o["y2"][:SEG]
        ys[4 * b + r] = o["y2"][SEG:]
        ch = _chan(r)
        cv = o["convp"].reshape(128, 6, 3)
        for ci in range(6):
            cp[0, b][:, ch[ci]] = cv[:, ci, :].T
        dp[0, b, 2 * r:2 * r + 2] = o["deltap"]
        rp[0, b, r][_ROWPERM] = o["retp"]
        for i in range(4):
            cv = o["convs"][i].reshape(128, 6, 3)
            for ci in range(6):
                cs[0, 4 * b + i][:, ch[ci]] = cv[:, ci, :].T
            ds[0, 4 * b + i, 2 * r:2 * r + 2] = o["deltas"][i]
            rs[0, 4 * b + i, r][_ROWPERM] = o["rets"][i]
    return yp, ys, cp, dp, rp, cs, ds, rs


_NC_CACHE = {}


def kernel(**inputs):
    inp = {k: np.asarray(v) for k, v in inputs.items()}
    S = inp["x_prompt"].shape[1]
    if S not in _NC_CACHE:
        _NC_CACHE[S] = build(S)
    nc = _NC_CACHE[S]
    maps = make_in_maps(inp, S)
    res = run_bass_kernel_spmd(nc, maps, core_ids=list(range(8)))
    return assemble(res.results, S)
```

```python
import math
import contextlib
import numpy as np
import concourse.bass as bass
import concourse.mybir as mybir
from concourse.bass_utils import run_bass_kernel_spmd

F32 = mybir.dt.float32
BF16 = mybir.dt.bfloat16
I32 = mybir.dt.int32
ALU = mybir.AluOpType
AF = mybir.ActivationFunctionType

D = 2048
KD = 16
DFF = 5632
NJ = 44
NCH = 17
PW = NCH * 128
EPS = 1e-6
PAST_LEN = 2048
MAGIC = 12582912.0
TWO_PI = 2.0 * math.pi


class Eng:
    def __init__(self, name, obj, sem, step):
        self.name, self.obj, self.sem, self.step = name, obj, sem, step
        self.cnt = 0
        self.waited = {}


class KB:
    def __init__(self, nc, es):
        self.nc, self.es = nc, es
        sem = lambda n: es.enter_context(nc.semaphore(n))
        self.pe = Eng("pe", nc.tensor, sem("s_pe"), 1)
        self.dve = Eng("dve", nc.vector, sem("s_dve"), 1)
        self.act = Eng("act", nc.scalar, sem("s_act"), 1)
        self.pool = Eng("pool", nc.gpsimd, sem("s_pool"), 1)
        self.sp = Eng("sp", nc.sync, sem("s_sp"), 1)
        self.engs = [self.pe, self.dve, self.act, self.pool, self.sp]
        self.dsem = {}
        for q in ("sp", "pool", "act"):
            self.dsem[q] = [Eng(f"d_{q}{i}", None, sem(f"s_d_{q}{i}"), 16) for i in range(8)]
        self.dnext = {"sp": 0, "pool": 0, "act": 0}
        self.cc = Eng("cc", None, sem("s_cc"), 1)
        self.lw = {}
        self.rd = {}
        self.psn = 0
        self.ns = ""
        self.alias = {}
        self.rec = None
        self.psub = {"d0_": [0, 1], "d1_": [2, 3], "r_": [4, 5], "x_": [6, 7]}
        self.psc = {}
        self.ps_t = [es.enter_context(nc.psum_tensor(f"ps{i}", [128, 512], F32)) for i in range(8)]

    def sb(self, name, shape, dt=F32, es=None):
        return (es or self.es).enter_context(self.nc.sbuf_tensor("sb_" + name, shape, dt))

    def ps(self):
        if self.ns in self.psub:
            sub = self.psub[self.ns]
            c = self.psc.get(self.ns, 0)
            self.psc[self.ns] = c + 1
            i = sub[c % len(sub)]
            return self.ps_t[i], ("ps", i)
        i = self.psn % 8
        self.psn += 1
        return self.ps_t[i], ("ps", i)

    def replay(self, lists):
        idx = [0] * len(lists)
        ns_save, self.ns = self.ns, ""
        while True:
            best, bf = -1, 2.0
            for i, l in enumerate(lists):
                if idx[i] < len(l):
                    f = idx[i] / len(l)
                    if f < bf:
                        best, bf = i, f
            if best < 0:
                break
            it = lists[best][idx[best]]
            idx[best] += 1
            if it[0] == "op":
                self.op(it[1], it[2], it[3], it[4])
            else:
                self.dma(it[1], it[2], it[3], it[4], it[5], it[6], it[7])
        self.ns = ns_save

    def _nk(self, ks):
        if not self.ns:
            return ks
        pf = ("f_", "b_", "c_", "cb_", "cb2_", "dcol", "st3")
        al = self.alias.get(self.ns, {})
        return [self.ns + al.get(k, k) if isinstance(k, str) and k.startswith(pf) else k for k in ks]

    def _deps(self, E, r, w, extra=()):
        deps = {}

        def need(p):
            if p is None:
                return
            e, s = p
            if e is self.pe and E is self.pe:
                return
            if deps.get(e, (None, 0))[1] < s:
                deps[e] = (e, s)
        for k in r:
            need(self.lw.get(k))
        for k in w:
            need(self.lw.get(k))
            for e, s in self.rd.get(k, {}).items():
                need((e, s))
        for p in extra:
            need(p)
        for e, s in deps.values():
            if E.waited.get(e.name, 0) < s:
                E.obj.wait_ge(e.sem, s)
                E.waited[e.name] = s

    def _mark(self, P, seq, r, w):
        for k in r:
            self.rd.setdefault(k, {})[P] = seq
        for k in w:
            self.lw[k] = (P, seq)
            self.rd[k] = {}

    def op(self, E, fn, r=(), w=()):
        r, w = self._nk(r), self._nk(w)
        if self.rec is not None:
            self.rec.append(("op", E, fn, r, w))
            return
        self._deps(E, r, w)
        ins = fn(E.obj)
        E.cnt += 1
        ins.then_inc(E.sem, 1)
        self._mark(E, E.cnt, r, w)

    def dma(self, Q, out, in_, r=(), w=(), indirect=None, eoff=0):
        r, w = self._nk(r), self._nk(w)
        if self.rec is not None:
            self.rec.append(("dma", Q, out, in_, r, w, indirect, eoff))
            return
        pool = self.dsem[Q.name]
        Dk = pool[self.dnext[Q.name] % len(pool)]
        self.dnext[Q.name] += 1
        extra = [(Dk, Dk.cnt * 16)] if Dk.cnt else []
        self._deps(Q, r, w, extra)
        if indirect is not None:
            ins = Q.obj.indirect_dma_start(out=out, out_offset=None, in_=in_,
                                           in_offset=bass.IndirectOffsetOnAxis(ap=indirect, axis=0), element_offset=eoff)
        else:
            ins = Q.obj.dma_start(out=out, in_=in_)
        ins.then_inc(Dk.sem, 16)
        Dk.cnt += 1
        self._mark(Dk, Dk.cnt * 16, r, w)

    def allgather(self, in_ap, out_ap, groups, r=(), w=()):
        Q = self.pool
        self._deps(Q, r, w)
        ins = Q.obj.collective_compute("AllGather", ALU.bypass, replica_groups=groups, ins=[in_ap], outs=[out_ap])
        ins.then_inc(self.cc.sem, 1)
        self.cc.cnt += 1
        self._mark(self.cc, self.cc.cnt, r, w)

    def barrier(self):
        allp = self.engs + [d for q in self.dsem.values() for d in q] + [self.cc]
        for E in self.engs:
            for P in allp:
                s = P.cnt * P.step
                if P is E or s == 0:
                    continue
                if E.waited.get(P.name, 0) < s:
                    E.obj.wait_ge(P.sem, s)
                    E.waited[P.name] = s
        self.lw.clear()
        self.rd.clear()


def build(S):
    SEG = S // 4
    SEGW = SEG + 16
    T1 = 256
    assert S % T1 == 0 and SEG % 128 == 0
    nc = bass.Bass("TRN2", target_bir_lowering=False)
    din = lambda n, sh, dt=F32: nc.dram_tensor(n, sh, dt, kind="ExternalInput")
    dout = lambda n, sh: nc.dram_tensor(n, sh, F32, kind="ExternalOutput")
    x1 = din("x1", [S, D]); xs1 = din("xs1", [64, D]); x2 = din("x2", [SEG, D]); xs2 = din("xs2", [16, D])
    cT = din("cT", [D, 6]); w_in_o = din("w_in_o", [D, PW]); w_out_p = din("w_out_p", [D, D])
    w_gu = din("w_gu", [D, 2 * DFF]); w_down = din("w_down", [DFF, D])
    w_ada = din("w_ada", [D, 6 * D]); b_ada_f = din("b_ada_f", [128, 96]); b_ada_r = din("b_ada_r", [1, 6 * D])
    w_adaf = din("w_adaf", [D, 2 * D]); b_adaf_r = din("b_adaf_r", [1, 2 * D])
    nmix = din("nmix", [128, KD]); nffn = din("nffn", [128, KD]); nfin = din("nfin", [128, D])
    cw_in = din("cw", [128, 24]); cst_in = din("cst", [4, 128, 18]); dnc_in = din("dnc", [128, 5])
    sd_in = din("sd", [4, 2, 128, 128]); sr_in = din("sr", [4, 256, 256])
    cmat = din("cmat", [128, 15 * 128]); crow = din("crow", [128, 512 + 1]); rc_in = din("rc", [128, 802])
    gidx_in = din("gidx", [128, 2 * KD], I32)
    y2 = dout("y2", [SEGW, D]); convp = dout("convp", [128, 18]); deltap = dout("deltap", [2, 128, 128])
    retp = dout("retp", [256, 256]); convs = dout("convs", [4, 128, 18]); deltas = dout("deltas", [4, 2, 128, 128])
    rets = dout("rets", [4, 256, 256])
    NSUP = S // T1
    ib = nc.dram_tensor("ib", [NSUP * 512, T1], BF16)
    ob = nc.dram_tensor("ob", [NSUP * 2048, T1], BF16)
    ibs = nc.dram_tensor("ibs", [4 * 512, 16], BF16)
    obs = nc.dram_tensor("obs", [4 * 2048, 16], BF16)
    wo_bf = nc.dram_tensor("wo_bf", [4, 128, KD, 512], BF16)
    wg_bf = nc.dram_tensor("wg_bf", [11, 2, 128, KD, 512], BF16)
    wd_bf = nc.dram_tensor("wd_bf", [4, 4, 128, 11, 512], BF16)
    adaT_d = nc.dram_tensor("adaT_d", [128, 8 * D], F32)

    with contextlib.ExitStack() as es, nc.Block() as block:
        @block.sync
        def _(_sync):
            K = KB(nc, es)
            pe, dve, act, pool, sp = K.pe, K.dve, K.act, K.pool, K.sp
            cm = K.sb("cm", [128, 15 * 128])
            cr = K.sb("cr", [128, 513])
            rc = K.sb("rc", [128, 802])
            kc = K.sb("kc", [128, 8])
            identb = K.sb("identb", [128, 128], BF16)
            K.dma(sp, cm[:, :], cmat[:, :], w=["cm"])
            K.dma(sp, cr[:, :], crow[:, :], w=["cr"])
            K.dma(sp, rc[:, :], rc_in[:, :], w=["rc"])
            K.op(dve, lambda e: e.memset(kc[:, 0:1], EPS), w=["kc"])
            K.op(dve, lambda e: e.memset(kc[:, 1:2], 1.0), w=["kc"])
            K.op(dve, lambda e: e.memset(kc[:, 2:3], 0.0), w=["kc"])
            ident = cm[:, 0:128]; triI = cm[:, 128:256]; triS = cm[:, 256:384]; ones = cm[:, 384:512]
            sel = [cm[:, 512 + 128 * g: 640 + 128 * g] for g in range(4)]
            bm16 = cm[:, 1024:1152]
            lvm = [(cm[:, 1152 + 256 * i: 1280 + 256 * i], cm[:, 1280 + 256 * i: 1408 + 256 * i]) for i in range(3)]
            K.op(dve, lambda e: e.tensor_copy(out=identb[:, :], in_=ident), r=["cm"], w=["identb"])
            iota = cr[:, 0:256]; rmask = cr[:, 256:512]; inv2pi = cr[:, 512:513]
            xi128 = rc[:, 0:256]; ze128 = rc[:, 256:512]; dm128 = rc[:, 512:640]; cd128 = rc[:, 640:641]
            xi16 = rc[:, 641:657]; ze16 = rc[:, 657:673]; dm16 = rc[:, 673:801]; cd16 = rc[:, 801:802]
            adaF = K.sb("adaF", [128, 4, KD, 6])
            gmul = K.sb("gmul", [128, 2, KD, 6])
            nm = K.sb("nm", [128, 2, KD])
            K.dma(sp, nm[:, 0, :], nmix[:, :], w=["nm"])
            K.dma(sp, nm[:, 1, :], nffn[:, :], w=["nm"])

            def cast_weights_gen():
                for n in range(4):
                    for k in range(KD):
                        K.dma(pool, wo_bf[n, :, k, :], w_out_p[k * 128:(k + 1) * 128, n * 512:(n + 1) * 512],
                              w=[("wo", n)])
                        yield
                for jb in range(11):
                    for gu in range(2):
                        for k in range(KD):
                            K.dma(pool, wg_bf[jb, gu, :, k, :],
                                  w_gu[k * 128:(k + 1) * 128, gu * DFF + jb * 512: gu * DFF + (jb + 1) * 512],
                                  w=[("wg", jb, gu)])
                            yield
                for n in range(4):
                    for jq in range(4):
                        for jj in range(11):
                            j = jq * 11 + jj
                            K.dma(pool, wd_bf[n, jq, :, jj, :], w_down[j * 128:(j + 1) * 128, n * 512:(n + 1) * 512],
                                  w=[("wd", n, jq)])
                            yield

            with contextlib.ExitStack() as e0:
                cs = K.sb("cs", [128, KD, 6], es=e0)
                csr = K.sb("csr", [128, 2, KD, 128], BF16, es=e0)
                csb = K.sb("csb", [128, KD, 6], BF16, es=e0)
                wblk = [K.sb(f"wblk{i}", [128, KD, 512], es=e0) for i in range(2)]
                wbf = [K.sb(f"wbf{i}", [128, KD, 512], BF16, es=e0) for i in range(2)]
                onesb = K.sb("onesb", [1, 128], BF16, es=e0)
                browb = [K.sb(f"browb{i}", [1, 512], BF16, es=e0) for i in range(2)]
                K.op(dve, lambda e: e.memset(onesb[:, :], 1.0), w=["onesb"])
                bfe = K.sb("bfe", [128, 96], es=e0)
                browt = [K.sb(f"brow{i}", [1, 512], es=e0) for i in range(2)]
                adaT = K.sb("adaT", [128, 2, 4, D], es=e0)
                K.dma(sp, cs[:, :, :], cT.ap().rearrange("(k p) r -> p k r", p=128), w=["cs"])
                K.dma(sp, bfe[:, :], b_ada_f[:, :], w=["bfe"])
                K.op(act, lambda e: e.activation(out=cs[:, :, :], in_=cs[:, :, :], func=AF.Silu), r=["cs"], w=["cs"])
                K.op(dve, lambda e: e.tensor_copy(out=csb[:, :, :], in_=cs[:, :, :]), r=["cs"], w=["csb"])
                for ri, row in enumerate((0, 5)):
                    for k in range(KD):
                        K.op(dve, lambda e, ri=ri, row=row, k=k: e.tensor_copy(
                            out=csr[:, ri, k, :], in_=cs[:, k, row:row + 1].to_broadcast([128, 128])),
                            r=["cs"], w=["csr"])
                fmap = {0: 0, 1: 1, 3: 2, 4: 3}
                tmap = {2: 0, 5: 1}
                for blk in range(32):
                    wt = wblk[blk % 2]; wk = ("wblk", blk % 2)
                    wb = wbf[blk % 2]; wbk = ("wbf", blk % 2)
                    if blk < 24:
                        src = w_ada[:, blk * 512:(blk + 1) * 512]
                    else:
                        src = w_adaf[:, (blk - 24) * 512:(blk - 23) * 512]
                    srcv = src.rearrange("(k p) n -> p k n", p=128)
                    K.dma(sp, wt[:, 0:8, :], srcv[:, 0:8, :], w=[wk + (0,)])
                    K.dma(act, wt[:, 8:16, :], srcv[:, 8:16, :], w=[wk + (1,)])
                    split = blk // 4 if blk < 24 else 6 + (blk - 24) // 4
                    brow = browt[blk % 2]; bk = ("brow", blk % 2)
                    bsrc = b_ada_r[:, blk * 512:(blk + 1) * 512] if blk < 24 else b_adaf_r[:, (blk - 24) * 512:(blk - 23) * 512]
                    K.dma(sp, brow[:, :], bsrc, w=[bk])
                    bb = browb[blk % 2]; bbk = ("browb", blk % 2)
                    K.op(dve, lambda e, bb=bb, brow=brow: e.tensor_copy(out=bb[:, :], in_=brow[:, :]), r=[bk], w=[bbk])
                    K.op(dve, lambda e, wb=wb, wt=wt: e.tensor_copy(out=wb[:, 0:8, :], in_=wt[:, 0:8, :]), r=[wk + (0,)], w=[wbk + (0,)])
                    K.op(act, lambda e, wb=wb, wt=wt: e.copy(out=wb[:, 8:12, :], in_=wt[:, 8:12, :]), r=[wk + (1,)], w=[wbk + (1,)])
                    K.op(pool, lambda e, wb=wb, wt=wt: e.tensor_copy(out=wb[:, 12:16, :], in_=wt[:, 12:16, :]), r=[wk + (1,)], w=[wbk + (2,)])
                    wbkk = lambda k, wbk=wbk: wbk + ((0,) if k < 8 else (1,) if k < 12 else (2,))
                    q = blk % 4
                    if split in fmap:
                        for cc in range(4):
                            pt, pk = K.ps()
                            for k in range(KD):
                                K.op(pe, lambda e, k=k, cc=cc, pt=pt, wb=wb: e.matmul(
                                    pt[:, 0:6], lhsT=wb[:, k, cc * 128:(cc + 1) * 128], rhs=csb[:, k, :],
                                    start=(k == 0), stop=(k == KD - 1)), r=[wbkk(k), "csb"], w=[pk])
                            n = q * 4 + cc
                            col = split * 16 + n
                            K.op(act, lambda e, n=n, col=col, pt=pt, sl=fmap[split]: e.activation(
                                out=adaF[:, sl, n, :], in_=pt[:, 0:6], func=AF.Identity,
                                bias=bfe[:, col:col + 1], scale=1.0), r=[pk, "bfe"], w=["adaF"])
                    else:
                        slot = tmap[split] if split in tmap else (2 if split == 6 else 3)
                        for ri in range(2):
                            pt, pk = K.ps()
                            for k in range(KD):
                                K.op(pe, lambda e, k=k, ri=ri, pt=pt, wb=wb: e.matmul(
                                    pt[:, :], lhsT=csr[:, ri, k, :], rhs=wb[:, k, :],
                                    start=(k == 0), stop=False), r=[wbkk(k), "csr"], w=[pk])
                            K.op(pe, lambda e, pt=pt, bb=bb: e.matmul(
                                pt[:, :], lhsT=onesb[0:1, :], rhs=bb[0:1, :],
                                start=False, stop=True), r=[bbk, "onesb"], w=[pk])
                            K.op(dve if ri == 0 else act,
                                 (lambda e, pt=pt, ri=ri, slot=slot, q=q: e.tensor_copy(
                                     out=adaT[:, ri, slot, q * 512:(q + 1) * 512], in_=pt[:, :])) if ri == 0 else
                                 (lambda e, pt=pt, ri=ri, slot=slot, q=q: e.copy(
                                     out=adaT[:, ri, slot, q * 512:(q + 1) * 512], in_=pt[:, :])),
                                 r=[pk], w=["adaT"])
                for i, sl in enumerate((1, 3)):
                    K.op(dve, lambda e, i=i, sl=sl: e.tensor_scalar(
                        out=gmul[:, i, :, :], in0=adaF[:, sl, :, :], scalar1=1.0, scalar2=None, op0=ALU.add),
                        r=["adaF"], w=["gmul"])
                    K.op(dve, lambda e, i=i: e.tensor_tensor(
                        out=gmul[:, i, :, :], in0=gmul[:, i, :, :],
                        in1=nm[:, i, :].unsqueeze(2).to_broadcast([128, KD, 6]), op=ALU.mult),
                        r=["gmul", "nm"], w=["gmul"])
                nf = wblk[0]
                nfv = nf[:, 0:4, :].rearrange("p a b -> p (a b)")
                K.dma(sp, nfv, nfin[:, :], w=[("wblk", 0, 0), ("wblk", 0, 1)])
                for ri in range(2):
                    K.op(dve, lambda e, ri=ri: e.scalar_tensor_tensor(
                        out=adaT[:, ri, 3, :], in0=adaT[:, ri, 3, :], scalar=1.0, in1=nfv,
                        op0=ALU.add, op1=ALU.mult), r=["adaT", ("wblk", 0, 0)], w=["adaT"])
                K.dma(sp, adaT_d[:, :], adaT[:, :, :, :].rearrange("p a b c -> p (a b c)"), r=["adaT"], w=["adaT_d"])
            K.barrier()
            import os
            if os.environ.get("KSTOP") == "0":
                return
            cgen = cast_weights_gen()

            def pump(n):
                for _ in range(n):
                    if next(cgen, "done") == "done":
                        break
            if os.environ.get("KSTOP") == "0b":
                K.barrier()
                return

            with contextlib.ExitStack() as e1:
                w1 = K.sb("w1", [128, KD, PW], BF16, es=e1)
                if os.environ.get("W1_CASTDMA"):
                    for k in range(KD):
                        K.dma(pool, w1[:, k, :], w_in_o[k * 128:(k + 1) * 128, :], w=["w1"])
                cwt = K.sb("cwt", [128, 6, 4], es=e1)
                K.dma(sp, cwt[:, :, :].rearrange("p a b -> p (a b)"), cw_in[:, :], w=["cwt"])
                dnc = K.sb("dnc", [128, 5], es=e1)
                K.dma(sp, dnc[:, :], dnc_in[:, :], w=["dnc"])
                negA = K.sb("negA", [128, 2], es=e1)
                K.op(act, lambda e: e.activation(out=negA[:, :], in_=dnc[:, 0:2], func=AF.Exp), r=["dnc"], w=["negA"])
                K.op(dve, lambda e: e.tensor_scalar(out=negA[:, :], in0=negA[:, :], scalar1=-1.0, scalar2=None,
                                                    op0=ALU.mult), r=["negA"], w=["negA"])
                xt = [K.sb("xt0", [128, D], es=e1)]
                xn = K.sb("xn", [128, D], BF16, es=e1)
                sq_junk = xn
                st = K.sb("st", [128, 4], es=e1)
                hT = K.sb("hT", [128, KD, T1], BF16, es=e1)
                Pbufs = [K.sb(f"P{i}", [128, NCH, T1 + 3], es=e1) for i in range(2)]
                P = Pbufs[0]
                Sdn = K.sb("Sdn", [128, 2, 128], es=e1)
                Sdnb = K.sb("Sdnb", [128, 2, 128], BF16, es=e1)
                Sr = K.sb("Sr", [128, 2, 256], es=e1)
                Srb = K.sb("Srb", [128, 2, 256], BF16, es=e1)
                def mk(tag, fn, bn, cn, cbn, al=None):
                    al = al or {}
                    K.alias[tag + "_"] = {"f_" + a_: "f_" + b_ for a_, b_ in al.items()}
                    fd = {n: K.sb(tag + "f_" + n, [128, T1], es=e1) for n in fn if n not in al}
                    for a_, b_ in al.items():
                        fd[a_] = fd[b_]
                    return (fd,
                            {n: K.sb(tag + "b_" + n, [128, T1], BF16, es=e1) for n in bn},
                            {n: K.sb(tag + "c_" + n, [128, 128], es=e1) for n in cn},
                            {n: K.sb(tag + "cb_" + n, [128, 128], BF16, es=e1) for n in cbn})
                DNS = []
                for tag in ("d0", "d1"):
                    f_, b_, c_, cb_ = mk(tag, ("acc", "csq", "csk", "csv", "sq", "rr", "qn", "beta", "g", "d", "e", "kd", "eb", "oT", "zs", "ta"),
                                         ("q", "k", "kb", "kc", "kdT", "vb", "qd", "mx"), ("junk", "arg", "gam", "gamI", "gamS"),
                                         ("Pa", "Pb", "PTa", "PTb", "Ya", "Yb", "qk", "kdec", "R", "u", "N0", "PT0", "Dva", "Dvb", "W", "V", "N0m", "PTm"),
                                         al={"sq": "acc", "ta": "acc", "eb": "g", "zs": "csv", "qn": "csq"})
                    DNS.append((f_, b_, c_, cb_, K.sb(tag + "dcol", [128, 2], es=e1)))
                f_, b_, c_, cb_ = mk("r", ("u", "t1", "nf", "sin", "cos", "ta", "tb", "q1", "q2", "k1", "k2", "or0", "or1", "sq", "rr", "zs"),
                                     ("q1", "q2", "k1", "k2", "qx1", "qx2", "kz1", "kz2", "rv0", "rv1", "mx"), (), ("rqk",),
                                     al={"sq": "t1", "rr": "nf", "zs": "u"})
                RTS = (f_, b_, cb_, {n: K.sb("rcb2_" + n, [128, 256], BF16, es=e1) for n in ("v", "kz")})
                if not os.environ.get("W1_CASTDMA"):
                    for k in range(KD):
                        for hh in range(2):
                            stg = xt[0]; sk_ = ("xt", id(stg))
                            K.dma(sp, stg[:, 0:PW // 2], w_in_o[k * 128:(k + 1) * 128, hh * (PW // 2):(hh + 1) * (PW // 2)], w=[sk_])
                            K.op(dve if hh == 0 else pool, lambda e, k=k, hh=hh, stg=stg: e.tensor_copy(
                                out=w1[:, k, hh * (PW // 2):(hh + 1) * (PW // 2)], in_=stg[:, 0:PW // 2]), r=[sk_], w=["w1"])
                for pb_ in Pbufs:
                    K.op(dve, lambda e, pb_=pb_: e.memset(pb_[:, :, 0:3], 0.0), w=[("P", id(pb_), m) for m in range(NCH)])
                if os.environ.get("KSTOP") == "1a":
                    K.barrier()
                    return
                K.op(dve, lambda e: e.memset(Sdn[:, :, :], 0.0), w=["Sdn"])
                K.op(dve, lambda e: e.memset(Sdnb[:, :, :], 0.0), w=["Sdnb"])
                K.op(dve, lambda e: e.memset(Sr[:, :, :], 0.0), w=["Sr"])
                K.op(dve, lambda e: e.memset(Srb[:, :, :], 0.0), w=["Srb"])

                def norm_transpose(src_ap, ntok, toff, xbuf, scal):
                    xk = ("xt", id(xbuf))
                    K.dma(sp, xbuf[0:ntok, :], src_ap, w=[xk])
                    K.op(act, lambda e: e.activation(out=sq_junk[0:ntok, :], in_=xbuf[0:ntok, :], func=AF.Square,
                                                     accum_out=st[0:ntok, 0:1]), r=[xk], w=["xn", "st"])
                    K.op(act, lambda e: e.activation(out=st[0:ntok, 1:2], in_=st[0:ntok, 0:1], func=AF.Ln,
                                                     bias=kc[0:ntok, 0:1], scale=1.0 / D), r=["st", "kc"], w=["st"])
                    K.op(act, lambda e: e.activation(out=st[0:ntok, 2:3], in_=st[0:ntok, 1:2], func=AF.Exp, scale=-0.5),
                         r=["st"], w=["st"])
                    K.op(dve, lambda e: e.tensor_scalar(out=xn[0:ntok, :], in0=xbuf[0:ntok, :], scalar1=st[0:ntok, 2:3],
                                                        scalar2=None, op0=ALU.mult), r=[xk, "st"], w=["xn"])
                    for k in range(KD):
                        pt, pk = K.ps()
                        K.op(pe, lambda e, k=k, pt=pt: e.matmul(
                            pt[:, 0:ntok], lhsT=xn[0:ntok, k * 128:(k + 1) * 128],
                            rhs=identb[0:ntok, 0:ntok], start=True, stop=True), r=["xn", "identb"], w=[pk])
                        for (lo, hi, row) in scal:
                            E = dve if (k % 2 == 0) else act
                            if E is dve:
                                fn = lambda e, k=k, lo=lo, hi=hi, row=row, pt=pt: e.tensor_scalar(
                                    out=hT[:, k, toff + lo: toff + hi], in0=pt[:, lo:hi],
                                    scalar1=gmul[:, 0, k, row:row + 1], scalar2=adaF[:, 0, k, row:row + 1],
                                    op0=ALU.mult, op1=ALU.add)
                            else:
                                fn = lambda e, k=k, lo=lo, hi=hi, row=row, pt=pt: e.activation(
                                    out=hT[:, k, toff + lo: toff + hi], in_=pt[:, lo:hi],
                                    func=AF.Identity, scale=gmul[:, 0, k, row:row + 1],
                                    bias=adaF[:, 0, k, row:row + 1])
                            K.op(E, fn, r=[pk, "gmul", "adaF"], w=[("hT", k)])

                def project(T, dst, doff):
                    for m in range(NCH):
                        pt, pk = K.ps()
                        for k in range(KD):
                            K.op(pe, lambda e, k=k, m=m, pt=pt: e.matmul(
                                pt[:, 0:T], lhsT=w1[:, k, m * 128:(m + 1) * 128], rhs=hT[:, k, 0:T],
                                start=(k == 0), stop=(k == KD - 1)), r=["w1", ("hT", k)], w=[pk])
                        if m % 2 == 0:
                            K.op(dve, lambda e, m=m, pt=pt: e.tensor_copy(out=dst[:, m, doff:doff + T], in_=pt[:, 0:T]),
                                 r=[pk], w=[("P", id(dst), m)])
                        else:
                            K.op(act, lambda e, m=m, pt=pt: e.copy(out=dst[:, m, doff:doff + T], in_=pt[:, 0:T]),
                                 r=[pk], w=[("P", id(dst), m)])

                def rms_gate_out(srcs, gates, nfeat, wcol, T, dsts, F, B):
                    pt, pk = K.ps()
                    for i, (sa, skey) in enumerate(srcs):
                        K.op(dve, lambda e, sa=sa: e.tensor_tensor(out=F["sq"][:, 0:T], in0=sa, in1=sa, op=ALU.mult),
                             r=[skey], w=["f_sq"])
                        K.op(pe, lambda e, i=i, pt=pt: e.matmul(pt[:, 0:T], lhsT=ones, rhs=F["sq"][:, 0:T],
                                                                 start=(i == 0), stop=(i == len(srcs) - 1)),
                             r=["f_sq", "cm"], w=[pk])
                    K.op(act, lambda e, pt=pt: e.activation(out=F["rr"][:, 0:T], in_=pt[:, 0:T], func=AF.Ln,
                                                            bias=kc[:, 0:1], scale=1.0 / nfeat), r=[pk, "kc"], w=["f_rr"])
                    K.op(act, lambda e: e.activation(out=F["rr"][:, 0:T], in_=F["rr"][:, 0:T], func=AF.Exp, scale=-0.5),
                         r=["f_rr"], w=["f_rr"])
                    for i, (sa, skey) in enumerate(srcs):
                        K.op(act, lambda e, i=i: e.activation(out=F["zs"][:, 0:T], in_=P[:, gates[i], 3:3 + T], func=AF.Silu),
                             r=[("P", id(P), gates[i])], w=["f_zs"])
                        if wcol is not None:
                            K.op(dve, lambda e, sa=sa: e.scalar_tensor_tensor(
                                out=F["ta"][:, 0:T], in0=sa, scalar=wcol, in1=F["rr"][:, 0:T], op0=ALU.mult, op1=ALU.mult),
                                r=[skey, "f_rr", "dnc"], w=["f_ta"])
                        else:
                            K.op(dve, lambda e, sa=sa: e.tensor_tensor(out=F["ta"][:, 0:T], in0=sa, in1=F["rr"][:, 0:T],
                                                                       op=ALU.mult), r=[skey, "f_rr"], w=["f_ta"])
                        K.op(dve, lambda e: e.tensor_tensor(out=B["mx"][:, 0:T], in0=F["ta"][:, 0:T], in1=F["zs"][:, 0:T],
                                                            op=ALU.mult), r=["f_ta", "f_zs"], w=["b_mx"])
                        K.dma(sp, dsts[i][0], B["mx"][:, 0:T], r=["b_mx"], w=[dsts[i][1]])

                def dn_gen(j, T, C, dst_rows_col, F, B, C_, CB, dcol):
                    nchunk = T // C
                    L = int(math.log2(C))
                    Pk = lambda m: ("P", id(P), m)
                    m0 = 4 * j
                    for ci, nm_ in enumerate(("csq", "csk", "csv")):
                        m = m0 + ci
                        cwi = 3 * j + ci
                        K.op(dve, lambda e, m=m, cwi=cwi: e.tensor_scalar(
                            out=F["acc"][:, 0:T], in0=P[:, m, 0:T], scalar1=cwt[:, cwi, 0:1], scalar2=None,
                            op0=ALU.mult), r=[Pk(m), "cwt"], w=["f_acc"])
                        for i in range(1, 4):
                            K.op(dve, lambda e, m=m, cwi=cwi, i=i: e.scalar_tensor_tensor(
                                out=F["acc"][:, 0:T], in0=P[:, m, i:i + T], scalar=cwt[:, cwi, i:i + 1],
                                in1=F["acc"][:, 0:T], op0=ALU.mult, op1=ALU.add), r=[Pk(m), "cwt", "f_acc"], w=["f_acc"])
                        K.op(act, lambda e, nm_=nm_: e.activation(out=F[nm_][:, 0:T], in_=F["acc"][:, 0:T], func=AF.Silu),
                             r=["f_acc"], w=["f_" + nm_])
                        yield
                    for nm_ in ("csq", "csk"):
                        K.op(dve, lambda e, nm_=nm_: e.tensor_tensor(out=F["sq"][:, 0:T], in0=F[nm_][:, 0:T],
                                                                    in1=F[nm_][:, 0:T], op=ALU.mult),
                             r=["f_" + nm_], w=["f_sq"])
                        pt, pk = K.ps()
                        K.op(pe, lambda e, pt=pt: e.matmul(pt[:, 0:T], lhsT=ones, rhs=F["sq"][:, 0:T], start=True,
                                                           stop=True), r=["f_sq", "cm"], w=[pk])
                        K.op(act, lambda e, pt=pt: e.activation(out=F["rr"][:, 0:T], in_=pt[:, 0:T], func=AF.Ln,
                                                                bias=kc[:, 0:1], scale=1.0), r=[pk, "kc"], w=["f_rr"])
                        K.op(act, lambda e: e.activation(out=F["rr"][:, 0:T], in_=F["rr"][:, 0:T], func=AF.Exp, scale=-0.5),
                             r=["f_rr"], w=["f_rr"])
                        yield
                        if nm_ == "csq":
                            K.op(dve, lambda e: e.scalar_tensor_tensor(
                                out=F["qn"][:, 0:T], in0=F["csq"][:, 0:T], scalar=128.0 ** -0.5, in1=F["rr"][:, 0:T],
                                op0=ALU.mult, op1=ALU.mult), r=["f_csq", "f_rr"], w=["f_qn"])
                            K.op(pool, lambda e: e.tensor_copy(out=B["q"][:, 0:T], in_=F["qn"][:, 0:T]),
                                 r=["f_qn"], w=["b_q"])
                        else:
                            K.op(dve, lambda e: e.tensor_tensor(out=F["csk"][:, 0:T], in0=F["csk"][:, 0:T],
                                                                in1=F["rr"][:, 0:T], op=ALU.mult),
                                 r=["f_csk", "f_rr"], w=["f_csk"])
                            K.op(pool, lambda e: e.tensor_copy(out=B["k"][:, 0:T], in_=F["csk"][:, 0:T]),
                                 r=["f_csk"], w=["b_k"])
                    pt, pk = K.ps()
                    K.op(pe, lambda e, pt=pt: e.matmul(pt[:, 0:T], lhsT=sel[j], rhs=P[:, 16, 3:3 + T], start=True,
                                                       stop=True), r=[Pk(16), "cm"], w=[pk])
                    K.op(act, lambda e, pt=pt: e.activation(out=F["beta"][:, 0:T], in_=pt[:, 0:T], func=AF.Sigmoid),
                         r=[pk], w=["f_beta"])
                    pt, pk = K.ps()
                    K.op(pe, lambda e, pt=pt: e.matmul(pt[:, 0:T], lhsT=sel[2 + j], rhs=P[:, 16, 3:3 + T], start=True,
                                                       stop=True), r=[Pk(16), "cm"], w=[pk])
                    K.op(act, lambda e, pt=pt: e.activation(out=F["g"][:, 0:T], in_=pt[:, 0:T], func=AF.Exp,
                                                            bias=dnc[:, 2 + j:3 + j], scale=1.0), r=[pk, "dnc"], w=["f_g"])
                    K.op(act, lambda e: e.activation(out=F["g"][:, 0:T], in_=F["g"][:, 0:T], func=AF.Ln,
                                                     bias=kc[:, 1:2], scale=1.0), r=["f_g", "kc"], w=["f_g"])
                    K.op(dve, lambda e: e.tensor_scalar(out=F["g"][:, 0:T], in0=F["g"][:, 0:T], scalar1=negA[:, j:j + 1],
                                                        scalar2=None, op0=ALU.mult), r=["f_g", "negA"], w=["f_g"])
                    K.op(dve, lambda e: e.tensor_tensor_scan(out=F["d"][:, 0:T], data0=rmask[:, 0:T], data1=F["g"][:, 0:T],
                                                             initial=0.0, op0=ALU.mult, op1=ALU.add),
                         r=["f_g", "cr"], w=["f_d"])
                    K.op(act, lambda e: e.activation(out=F["e"][:, 0:T], in_=F["d"][:, 0:T], func=AF.Exp), r=["f_d"], w=["f_e"])
                    yield
                    for c in range(nchunk):
                        c0 = c * C
                        K.op(act, lambda e, c0=c0: e.activation(
                            out=F["kd"][:, c0:c0 + C], in_=F["d"][:, c0:c0 + C], func=AF.Exp, scale=-1.0,
                            bias=F["d"][:, c0 + C - 1:c0 + C]), r=["f_d"], w=["f_kd"])
                    K.op(dve, lambda e: e.tensor_tensor(out=F["eb"][:, 0:T], in0=F["e"][:, 0:T], in1=F["beta"][:, 0:T],
                                                        op=ALU.mult), r=["f_e", "f_beta"], w=["f_eb"])
                    K.op(pool, lambda e: e.tensor_tensor(out=B["kb"][:, 0:T], in0=F["csk"][:, 0:T], in1=F["beta"][:, 0:T],
                                                         op=ALU.mult), r=["f_csk", "f_beta"], w=["b_kb"])
                    K.op(dve, lambda e: e.scalar_tensor_tensor(out=B["kc"][:, 0:T], in0=F["csk"][:, 0:T], scalar=-1.0,
                                                               in1=F["eb"][:, 0:T], op0=ALU.mult, op1=ALU.mult),
                         r=["f_csk", "f_eb"], w=["b_kc"])
                    K.op(pool, lambda e: e.tensor_tensor(out=B["kdT"][:, 0:T], in0=F["csk"][:, 0:T], in1=F["kd"][:, 0:T],
                                                         op=ALU.mult), r=["f_csk", "f_kd"], w=["b_kdT"])
                    K.op(pool, lambda e: e.tensor_tensor(out=B["vb"][:, 0:T], in0=F["csv"][:, 0:T], in1=F["beta"][:, 0:T],
                                                         op=ALU.mult), r=["f_csv", "f_beta"], w=["b_vb"])
                    K.op(dve, lambda e: e.tensor_tensor(out=B["qd"][:, 0:T], in0=F["qn"][:, 0:T], in1=F["e"][:, 0:T],
                                                        op=ALU.mult), r=["f_qn", "f_e"], w=["b_qd"])
                    yield
                    for c in range(nchunk):
                        c0 = c * C
                        cs_ = slice(c0, c0 + C)
                        K.op(dve, lambda e, cs_=cs_: e.scalar_tensor_tensor(
                            out=C_["junk"][0:C, 0:C], in0=F["d"][0:C, cs_], scalar=1.0, in1=ident[0:C, 0:C],
                            op0=ALU.mult, op1=ALU.mult, accum_out=dcol[0:C, 0:1]), r=["f_d", "cm"], w=["c_junk", "dcol"])
                        K.op(dve, lambda e, cs_=cs_: e.tensor_scalar(
                            out=C_["arg"][0:C, 0:C], in0=F["d"][0:C, cs_], scalar1=dcol[0:C, 0:1], scalar2=0.0,
                            op0=ALU.subtract, op1=ALU.min), r=["f_d", "dcol"], w=["c_arg"])
                        K.op(act, lambda e: e.activation(out=C_["gam"][0:C, 0:C], in_=C_["arg"][0:C, 0:C], func=AF.Exp),
                             r=["c_arg"], w=["c_gam"])
                        K.op(pool, lambda e: e.tensor_tensor(out=C_["gamI"][0:C, 0:C], in0=C_["gam"][0:C, 0:C],
                                                             in1=triI[0:C, 0:C], op=ALU.mult), r=["c_gam", "cm"], w=["c_gamI"])
                        K.op(pool, lambda e: e.tensor_tensor(out=C_["gamS"][0:C, 0:C], in0=C_["gam"][0:C, 0:C],
                                                             in1=triS[0:C, 0:C], op=ALU.mult), r=["c_gam", "cm"], w=["c_gamS"])
                        pt, pk = K.ps()
                        K.op(pe, lambda e, pt=pt, cs_=cs_: e.matmul(pt[0:C, 0:C], lhsT=B["k"][:, cs_], rhs=B["kb"][:, cs_],
                                                                    start=True, stop=True), r=["b_k", "b_kb"], w=[pk])
                        K.op(dve, lambda e, pt=pt: e.scalar_tensor_tensor(
                            out=CB["N0" if C == 128 else "Pa"][0:C, 0:C], in0=pt[0:C, 0:C], scalar=-1.0, in1=C_["gamS"][0:C, 0:C],
                            op0=ALU.mult, op1=ALU.mult), r=[pk, "c_gamS"], w=["cb_N0" if C == 128 else "cb_Pa"])
                        if C == 128:
                            pt, pk = K.ps()
                            K.op(pe, lambda e, pt=pt: e.matmul(pt[:, 0:128], lhsT=CB["N0"][:, :], rhs=identb[:, :],
                                                               start=True, stop=True), r=["cb_N0", "identb"], w=[pk])
                            K.op(act, lambda e, pt=pt: e.copy(out=CB["PT0"][:, :], in_=pt[:, 0:128]), r=[pk], w=["cb_PT0"])
                            K.op(pool, lambda e: e.tensor_tensor(out=CB["Pa"][:, :], in0=CB["N0"][:, :], in1=bm16, op=ALU.mult),
                                 r=["cb_N0", "cm"], w=["cb_Pa"])
                        pt, pk = K.ps()
                        K.op(pe, lambda e, pt=pt, cs_=cs_: e.matmul(pt[0:C, 0:C], lhsT=B["k"][:, cs_], rhs=B["q"][:, cs_],
                                                                    start=True, stop=True), r=["b_k", "b_q"], w=[pk])
                        K.op(dve, lambda e, pt=pt: e.tensor_tensor(out=CB["qk"][0:C, 0:C], in0=pt[0:C, 0:C],
                                                                   in1=C_["gamI"][0:C, 0:C], op=ALU.mult),
                             r=[pk, "c_gamI"], w=["cb_qk"])
                        yield
                        pt, pk = K.ps()
                        K.op(pe, lambda e, pt=pt: e.matmul(pt[0:C, 0:C], lhsT=CB["Pa"][0:C, 0:C], rhs=identb[0:C, 0:C],
                                                           start=True, stop=True), r=["cb_Pa", "identb"], w=[pk])
                        K.op(act, lambda e, pt=pt: e.copy(out=CB["PTa"][0:C, 0:C], in_=pt[0:C, 0:C]), r=[pk], w=["cb_PTa"])
                        K.op(dve, lambda e: e.tensor_tensor(out=CB["Ya"][0:C, 0:C], in0=CB["Pa"][0:C, 0:C],
                                                            in1=ident[0:C, 0:C], op=ALU.add), r=["cb_Pa", "cm"], w=["cb_Ya"])
                        cur, nxt = "a", "b"
                        for l in range(1, min(L, 4)):
                            Pc, PTc, Yc = "P" + cur, "PT" + cur, "Y" + cur
                            Pn, PTn, Yn = "P" + nxt, "PT" + nxt, "Y" + nxt
                            pt, pk = K.ps()
                            K.op(pe, lambda e, pt=pt, Pc=Pc, PTc=PTc: e.matmul(
                                pt[0:C, 0:C], lhsT=CB[Pc][0:C, 0:C], rhs=CB[PTc][0:C, 0:C], start=True, stop=True),
                                r=["cb_" + Pc, "cb_" + PTc], w=[pk])
                            K.op(act, lambda e, pt=pt, PTn=PTn: e.copy(out=CB[PTn][0:C, 0:C], in_=pt[0:C, 0:C]),
                                 r=[pk], w=["cb_" + PTn])
                            if l < min(L, 4) - 1:
                                pt, pk = K.ps()
                                K.op(pe, lambda e, pt=pt, Pc=Pc, PTc=PTc: e.matmul(
                                    pt[0:C, 0:C], lhsT=CB[PTc][0:C, 0:C], rhs=CB[Pc][0:C, 0:C], start=True, stop=True),
                                    r=["cb_" + Pc, "cb_" + PTc], w=[pk])
                                K.op(act, lambda e, pt=pt, Pn=Pn: e.copy(out=CB[Pn][0:C, 0:C], in_=pt[0:C, 0:C]),
                                     r=[pk], w=["cb_" + Pn])
                            pt, pk = K.ps()
                            K.op(pe, lambda e, pt=pt, PTn=PTn, Yc=Yc: e.matmul(
                                pt[0:C, 0:C], lhsT=CB[PTn][0:C, 0:C], rhs=CB[Yc][0:C, 0:C], start=True, stop=True),
                                r=["cb_" + PTn, "cb_" + Yc], w=[pk])
                            K.op(dve, lambda e, pt=pt, Yc=Yc, Yn=Yn: e.tensor_tensor(
                                out=CB[Yn][0:C, 0:C], in0=pt[0:C, 0:C], in1=CB[Yc][0:C, 0:C], op=ALU.add),
                                r=[pk, "cb_" + Yc], w=["cb_" + Yn])
                            yield
                            cur, nxt = nxt, cur
                        Yf = "Y" + cur
                        if C == 128:
                            Ec, En = "Y" + cur, "Y" + nxt
                            Dc, Dn = "Dva", "Dvb"
                            pt, pk = K.ps()
                            K.op(pe, lambda e, pt=pt, Ec=Ec: e.matmul(pt[:, 0:128], lhsT=CB[Ec][:, :], rhs=identb[:, :],
                                                                      start=True, stop=True), r=["cb_" + Ec, "identb"], w=[pk])
                            K.op(act, lambda e, pt=pt, Dc=Dc: e.copy(out=CB[Dc][:, :], in_=pt[:, 0:128]), r=[pk], w=["cb_" + Dc])
                            yield
                            for li in range(3):
                                mk, mkT = lvm[li]
                                K.op(pool, lambda e, mk=mk: e.tensor_tensor(out=CB["PTm"][:, :], in0=CB["PT0"][:, :], in1=mk, op=ALU.mult),
                                     r=["cb_PT0", "cm"], w=["cb_PTm"])
                                pt, pk = K.ps()
                                K.op(pe, lambda e, pt=pt, Ec=Ec: e.matmul(pt[:, 0:128], lhsT=CB["PTm"][:, :], rhs=CB[Ec][:, :],
                                                                          start=True, stop=True), r=["cb_PTm", "cb_" + Ec], w=[pk])
                                K.op(act, lambda e, pt=pt: e.copy(out=CB["W"][:, :], in_=pt[:, 0:128]), r=[pk], w=["cb_W"])
                                pt, pk = K.ps()
                                K.op(pe, lambda e, pt=pt, Dc=Dc: e.matmul(pt[:, 0:128], lhsT=CB[Dc][:, :], rhs=CB["W"][:, :],
                                                                          start=True, stop=True), r=["cb_W", "cb_" + Dc], w=[pk])
                                K.op(dve, lambda e, pt=pt, Ec=Ec, En=En: e.tensor_tensor(out=CB[En][:, :], in0=pt[:, 0:128], in1=CB[Ec][:, :],
                                                                                       op=ALU.add), r=[pk, "cb_" + Ec], w=["cb_" + En])
                                yield
                                if li < 2:
                                    K.op(pool, lambda e, mkT=mkT: e.tensor_tensor(out=CB["N0m"][:, :], in0=CB["N0"][:, :], in1=mkT, op=ALU.mult),
                                         r=["cb_N0", "cm"], w=["cb_N0m"])
                                    pt, pk = K.ps()
                                    K.op(pe, lambda e, pt=pt, Dc=Dc: e.matmul(pt[:, 0:128], lhsT=CB["N0m"][:, :], rhs=CB[Dc][:, :],
                                                                              start=True, stop=True), r=["cb_N0m", "cb_" + Dc], w=[pk])
                                    K.op(act, lambda e, pt=pt: e.copy(out=CB["V"][:, :], in_=pt[:, 0:128]), r=[pk], w=["cb_V"])
                                    pt, pk = K.ps()
                                    K.op(pe, lambda e, pt=pt, Ec=Ec: e.matmul(pt[:, 0:128], lhsT=CB[Ec][:, :], rhs=CB["V"][:, :],
                                                                              start=True, stop=True), r=["cb_V", "cb_" + Ec], w=[pk])
                                    K.op(dve, lambda e, pt=pt, Dc=Dc, Dn=Dn: e.tensor_tensor(out=CB[Dn][:, :], in0=pt[:, 0:128], in1=CB[Dc][:, :],
                                                                                           op=ALU.add), r=[pk, "cb_" + Dc], w=["cb_" + Dn])
                                    Dc, Dn = Dn, Dc
                                Ec, En = En, Ec
                            Yf = Ec
                        pt, pk = K.ps()
                        K.op(pe, lambda e, pt=pt, cs_=cs_: e.matmul(pt[0:C, 0:128], lhsT=B["kdT"][:, cs_], rhs=identb[:, :],
                                                                    start=True, stop=True), r=["b_kdT", "identb"], w=[pk])
                        K.op(act, lambda e, pt=pt: e.copy(out=CB["kdec"][0:C, :], in_=pt[0:C, 0:128]), r=[pk], w=["cb_kdec"])
                        yield
                        sk, skb = ("Sdn", j), ("Sdnb", j)
                        pt, pk = K.ps()
                        K.op(pe, lambda e, pt=pt, cs_=cs_: e.matmul(pt[0:C, 0:128], lhsT=B["vb"][:, cs_], rhs=identb[:, :],
                                                                    start=True, stop=False), r=["b_vb", "identb"], w=[pk])
                        K.op(pe, lambda e, pt=pt, cs_=cs_: e.matmul(pt[0:C, 0:128], lhsT=B["kc"][:, cs_], rhs=Sdnb[:, j, :],
                                                                    start=False, stop=True), r=["b_kc", skb], w=[pk])
                        K.op(act, lambda e, pt=pt: e.copy(out=CB["R"][0:C, :], in_=pt[0:C, 0:128]), r=[pk], w=["cb_R"])
                        yield
                        pt, pk = K.ps()
                        K.op(pe, lambda e, pt=pt, Yf=Yf: e.matmul(pt[0:C, 0:128], lhsT=CB[Yf][0:C, 0:C], rhs=CB["R"][0:C, :],
                                                           start=True, stop=True), r=["cb_" + Yf, "cb_R"], w=[pk])
                        K.op(act, lambda e, pt=pt: e.copy(out=CB["u"][0:C, :], in_=pt[0:C, 0:128]), r=[pk], w=["cb_u"])
                        yield
                        pt, pk = K.ps()
                        K.op(pe, lambda e, pt=pt, cs_=cs_: e.matmul(pt[:, 0:C], lhsT=Sdnb[:, j, :], rhs=B["qd"][:, cs_],
                                                                    start=True, stop=False), r=[skb, "b_qd"], w=[pk])
                        K.op(pe, lambda e, pt=pt: e.matmul(pt[:, 0:C], lhsT=CB["u"][0:C, :], rhs=CB["qk"][0:C, 0:C],
                                                           start=False, stop=True), r=["cb_u", "cb_qk"], w=[pk])
                        K.op(act, lambda e, pt=pt, cs_=cs_: e.copy(out=F["oT"][:, cs_], in_=pt[:, 0:C]), r=[pk], w=["f_oT"])
                        yield
                        pt, pk = K.ps()
                        K.op(pe, lambda e, pt=pt: e.matmul(pt[:, 0:128], lhsT=CB["kdec"][0:C, :], rhs=CB["u"][0:C, :],
                                                           start=True, stop=True), r=["cb_kdec", "cb_u"], w=[pk])
                        K.op(act, lambda e, c0=c0: e.activation(out=dcol[:, 1:2], in_=F["d"][:, c0 + C - 1:c0 + C], func=AF.Exp),
                             r=["f_d"], w=["st3"])
                        K.op(dve, lambda e, pt=pt: e.scalar_tensor_tensor(
                            out=Sdn[:, j, :], in0=Sdn[:, j, :], scalar=dcol[:, 1:2], in1=pt[:, 0:128],
                            op0=ALU.mult, op1=ALU.add), r=[sk, "st3", pk], w=[sk])
                        K.op(pool, lambda e: e.tensor_copy(out=Sdnb[:, j, :], in_=Sdn[:, j, :]), r=[sk], w=[skb])
                        yield
                    rms_gate_out([(F["oT"][:, 0:T], "f_oT")], [m0 + 3], 128.0, dnc[:, 4:5], T, [dst_rows_col(j)], F, B)

                def ret_gen(T, C, pos0, dst_rows_col, F, B, CB, CB2):
                    nchunk = T // C
                    Pk = lambda m: ("P", id(P), m)
                    xi_t, ze_t, dmT, cdr = (xi128, ze128, dm128, cd128) if C == 128 else (xi16, ze16, dm16, cd16)
                    K.op(dve, lambda e: e.tensor_scalar(out=F["u"][:, 0:T], in0=iota[:, 0:T], scalar1=float(pos0),
                                                        scalar2=inv2pi, op0=ALU.add, op1=ALU.mult), r=["cr"], w=["f_u"])
                    for fn_, off in (("sin", 0.0), ("cos", 0.25)):
                        if off:
                            K.op(dve, lambda e: e.tensor_scalar(out=F["u"][:, 0:T], in0=F["u"][:, 0:T], scalar1=0.25,
                                                                scalar2=None, op0=ALU.add), r=["f_u"], w=["f_u"])
                        K.op(dve, lambda e: e.tensor_scalar(out=F["t1"][:, 0:T], in0=F["u"][:, 0:T], scalar1=MAGIC,
                                                            scalar2=None, op0=ALU.add), r=["f_u"], w=["f_t1"])
                        K.op(dve, lambda e: e.scalar_tensor_tensor(out=F["nf"][:, 0:T], in0=F["t1"][:, 0:T], scalar=MAGIC,
                                                                   in1=F["u"][:, 0:T], op0=ALU.subtract, op1=ALU.subtract),
                             r=["f_t1", "f_u"], w=["f_nf"])
                        K.op(act, lambda e, fn_=fn_: e.activation(out=F[fn_][:, 0:T], in_=F["nf"][:, 0:T], func=AF.Sin,
                                                                  scale=-6.28318), r=["f_nf"], w=["f_" + fn_])
                        yield
                    for (ce, co, o1, o2) in ((8, 9, "q1", "q2"), (10, 11, "k1", "k2")):
                        xe = P[:, ce, 3:3 + T]; xo = P[:, co, 3:3 + T]
                        rk_ = [Pk(ce), Pk(co), "f_sin", "f_cos"]
                        K.op(dve, lambda e, xe=xe: e.tensor_tensor(out=F["ta"][:, 0:T], in0=xe, in1=F["cos"][:, 0:T], op=ALU.mult),
                             r=rk_, w=["f_ta"])
                        K.op(pool, lambda e, xo=xo: e.tensor_tensor(out=F["tb"][:, 0:T], in0=xo, in1=F["sin"][:, 0:T], op=ALU.mult),
                             r=rk_, w=["f_tb"])
                        K.op(dve, lambda e, o1=o1: e.tensor_tensor(out=F[o1][:, 0:T], in0=F["ta"][:, 0:T], in1=F["tb"][:, 0:T],
                                                                   op=ALU.subtract), r=["f_ta", "f_tb"], w=["f_" + o1])
                        K.op(dve, lambda e, xo=xo: e.tensor_tensor(out=F["ta"][:, 0:T], in0=xo, in1=F["cos"][:, 0:T], op=ALU.mult),
                             r=rk_ + ["f_ta"], w=["f_ta"])
                        K.op(pool, lambda e, xe=xe: e.tensor_tensor(out=F["tb"][:, 0:T], in0=xe, in1=F["sin"][:, 0:T], op=ALU.mult),
                             r=rk_ + ["f_tb"], w=["f_tb"])
                        K.op(dve, lambda e, o2=o2: e.tensor_tensor(out=F[o2][:, 0:T], in0=F["ta"][:, 0:T], in1=F["tb"][:, 0:T],
                                                                   op=ALU.add), r=["f_ta", "f_tb"], w=["f_" + o2])
                        yield
                    for n_ in ("q1", "q2", "k1", "k2"):
                        K.op(pool, lambda e, n_=n_: e.tensor_copy(out=B[n_][:, 0:T], in_=F[n_][:, 0:T]), r=["f_" + n_], w=["b_" + n_])
                    for n_, s_ in (("qx1", "q1"), ("qx2", "q2")):
                        K.op(dve, lambda e, n_=n_, s_=s_: e.tensor_tensor(out=B[n_][:, 0:T], in0=F[s_][:, 0:T], in1=xi_t[:, 0:T],
                                                                          op=ALU.mult), r=["f_" + s_, "rc"], w=["b_" + n_])
                    for n_, s_ in (("kz1", "k1"), ("kz2", "k2")):
                        K.op(pool, lambda e, n_=n_, s_=s_: e.tensor_tensor(out=B[n_][:, 0:T], in0=F[s_][:, 0:T], in1=ze_t[:, 0:T],
                                                                           op=ALU.mult), r=["f_" + s_, "rc"], w=["b_" + n_])
                    for i_ in range(2):
                        K.op(pool, lambda e, i_=i_: e.tensor_copy(out=B["rv%d" % i_][:, 0:T], in_=P[:, 12 + i_, 3:3 + T]),
                             r=[Pk(12 + i_)], w=["b_rv%d" % i_])
                        yield
                    for c in range(nchunk):
                        c0 = c * C
                        cs_ = slice(c0, c0 + C)
                        pt, pk = K.ps()
                        for h_ in range(2):
                            K.op(pe, lambda e, pt=pt, h_=h_, cs_=cs_: e.matmul(
                                pt[0:C, 0:C], lhsT=B["k%d" % (h_ + 1)][:, cs_], rhs=B["q%d" % (h_ + 1)][:, cs_],
                                start=(h_ == 0), stop=(h_ == 1)), r=["b_k1", "b_k2", "b_q1", "b_q2"], w=[pk])
                        K.op(dve, lambda e, pt=pt: e.tensor_tensor(out=CB["rqk"][0:C, 0:C], in0=pt[0:C, 0:C], in1=dmT[0:C, 0:C],
                                                                   op=ALU.mult), r=[pk, "rc"], w=["cb_rqk"])
                        yield
                        for dst_, srcs_ in (("v", ("rv0", "rv1")), ("kz", ("kz1", "kz2"))):
                            for h_ in range(2):
                                pt, pk = K.ps()
                                K.op(pe, lambda e, pt=pt, h_=h_, cs_=cs_, srcs_=srcs_: e.matmul(
                                    pt[0:C, 0:128], lhsT=B[srcs_[h_]][:, cs_], rhs=identb[:, :],
                                    start=True, stop=True), r=["b_" + srcs_[h_], "identb"], w=[pk])
                                K.op(act, lambda e, pt=pt, dst_=dst_, h_=h_: e.copy(out=CB2[dst_][0:C, h_ * 128:(h_ + 1) * 128],
                                                                                  in_=pt[0:C, 0:128]), r=[pk], w=["cb2_" + dst_])
                                yield
                        for hv in range(2):
                            pt, pk = K.ps()
                            K.op(pe, lambda e, pt=pt, hv=hv: e.matmul(pt[:, 0:C], lhsT=CB2["v"][0:C, hv * 128:(hv + 1) * 128],
                                                                      rhs=CB["rqk"][0:C, 0:C], start=True, stop=False),
                                 r=["cb2_v", "cb_rqk"], w=[pk])
                            for kh in range(2):
                                K.op(pe, lambda e, pt=pt, hv=hv, kh=kh, cs_=cs_: e.matmul(
                                    pt[:, 0:C], lhsT=Srb[:, kh, hv * 128:(hv + 1) * 128], rhs=B["qx%d" % (kh + 1)][:, cs_],
                                    start=False, stop=(kh == 1)), r=["Srb", "b_qx1", "b_qx2"], w=[pk])
                            K.op(act, lambda e, pt=pt, hv=hv, cs_=cs_: e.copy(out=F["or%d" % hv][:, cs_], in_=pt[:, 0:C]),
                                 r=[pk], w=["f_or%d" % hv])
                            yield
                        for kh in range(2):
                            pt, pk = K.ps()
                            K.op(pe, lambda e, pt=pt, kh=kh: e.matmul(pt[:, 0:256], lhsT=CB2["kz"][0:C, kh * 128:(kh + 1) * 128],
                                                                      rhs=CB2["v"][0:C, :], start=True, stop=True),
                                 r=["cb2_kz", "cb2_v"], w=[pk])
                            K.op(dve, lambda e, pt=pt, kh=kh: e.scalar_tensor_tensor(
                                out=Sr[:, kh, :], in0=Sr[:, kh, :], scalar=cdr, in1=pt[:, 0:256], op0=ALU.mult, op1=ALU.add),
                                r=["Sr", "rc", pk], w=["Sr"])
                        K.op(pool, lambda e: e.tensor_copy(out=Srb[:, :, :], in_=Sr[:, :, :]), r=["Sr"], w=["Srb"])
                        yield
                    rms_gate_out([(F["or0"][:, 0:T], "f_or0"), (F["or1"][:, 0:T], "f_or1")], [14, 15], 256.0, None, T,
                                 [dst_rows_col(2), dst_rows_col(3)], F, B)


                def mixer_tile(T, C, pos0, dst_rows_col, extra=None):
                    gens = [("d0_", dn_gen(0, T, C, dst_rows_col, *DNS[0])), ("d1_", dn_gen(1, T, C, dst_rows_col, *DNS[1])),
                            ("r_", ret_gen(T, C, pos0, dst_rows_col, *RTS))]
                    lists = []
                    for ns_, g_ in gens:
                        K.ns = ns_
                        K.rec = []
                        for _ in g_:
                            pass
                        lists.append(K.rec)
                        K.rec = None
                    if extra is not None:
                        K.ns = "x_"
                        K.rec = []
                        extra()
                        lists.append(K.rec)
                        K.rec = None
                    K.ns = ""
                    K.replay(lists)

                conv_ch = (0, 1, 2, 4, 5, 6)
                nsup = S // T1
                per_sup = 592 // nsup + 1
                def prep(t, dst):
                    for sub in range(2):
                        norm_transpose(x1[t * T1 + sub * 128: t * T1 + (sub + 1) * 128, :], 128, sub * 128, xt[0], [(0, 128, 0)])
                    project(T1, dst, 3)

                prep(0, Pbufs[0])
                for t in range(nsup):
                    pump(per_sup)
                    P = Pbufs[t % 2]
                    extra_fn = None
                    if t + 1 < nsup:
                        def extra_fn(t=t, cur=Pbufs[t % 2], nxt=Pbufs[(t + 1) % 2]):
                            K.op(pool, lambda e: e.tensor_copy(out=nxt[:, :, 0:3], in_=cur[:, :, T1:T1 + 3]),
                                 r=[("P", id(cur), m) for m in range(NCH)], w=[("P", id(nxt), m) for m in range(NCH)])
                            prep(t + 1, nxt)
                    mixer_tile(T1, 128, t * T1, lambda jj, t=t: (ib[t * 512 + jj * 128: t * 512 + (jj + 1) * 128, :], ("ib", t)),
                               extra=extra_fn)
                    if not os.environ.get("SKIP_CC"):
                        K.allgather(ib[t * 512:(t + 1) * 512, :].opt(), ob[t * 2048:(t + 1) * 2048, :].opt(),
                                    [[0, 1, 2, 3], [4, 5, 6, 7]], r=[("ib", t)], w=[("ob", t)])
                Psm = Pbufs[0] if P is Pbufs[1] else Pbufs[1]
                for ci, m in enumerate(conv_ch):
                    K.dma(sp, convp[:, ci * 3:(ci + 1) * 3], P[:, m, T1:T1 + 3], r=[("P", id(P), m)])
                K.dma(sp, deltap.ap().rearrange("j k v -> k j v"), Sdn[:, :, :], r=[("Sdn", 0), ("Sdn", 1)])
                K.dma(sp, retp.ap().rearrange("(h k) v -> k h v", h=2), Sr[:, :, :], r=["Sr"])
                norm_transpose(xs1[:, :], 64, 0, xt[0], [(16 * i, 16 * (i + 1), 1 + i) for i in range(4)])
                project(64, Psm, 0)
                cstt = K.sb("cstt", [128, 6, 3], es=e1)
                for i in range(4):
                    K.dma(sp, cstt[:, :, :].rearrange("p a b -> p (a b)"), cst_in[i, :, :], w=["cstt"])
                    for m in range(NCH):
                        K.op(pool, lambda e, m=m, i=i: e.tensor_copy(out=P[:, m, 3:19], in_=Psm[:, m, 16 * i:16 * (i + 1)]),
                             r=[("P", id(Psm), m)], w=[("P", id(P), m)])
                    for ci, m in enumerate(conv_ch):
                        K.op(pool, lambda e, m=m, ci=ci: e.tensor_copy(out=P[:, m, 0:3], in_=cstt[:, ci, :]),
                             r=["cstt"], w=[("P", id(P), m)])
                    K.dma(sp, Sdn[:, :, :], sd_in[i].rearrange("j k v -> k j v"), w=[("Sdn", 0), ("Sdn", 1)])
                    K.dma(sp, Sr[:, :, :], sr_in[i].rearrange("(h k) v -> k h v", h=2), w=["Sr"])
                    for j in range(2):
                        K.op(pool, lambda e, j=j: e.tensor_copy(out=Sdnb[:, j, :], in_=Sdn[:, j, :]), r=[("Sdn", j)], w=[("Sdnb", j)])
                    K.op(pool, lambda e: e.tensor_copy(out=Srb[:, :, :], in_=Sr[:, :, :]), r=["Sr"], w=["Srb"])
                    mixer_tile(16, 16, PAST_LEN, lambda jj, i=i: (ibs[i * 512 + jj * 128: i * 512 + (jj + 1) * 128, :], "ibs"))
                    for ci, m in enumerate(conv_ch):
                        K.dma(sp, convs[i, :, ci * 3:(ci + 1) * 3], P[:, m, 16:19], r=[("P", id(P), m)])
                    K.dma(sp, deltas[i].rearrange("j k v -> k j v"), Sdn[:, :, :], r=[("Sdn", 0), ("Sdn", 1)])
                    K.dma(sp, rets[i].rearrange("(h k) v -> k h v", h=2), Sr[:, :, :], r=["Sr"])
            pump(10000)
            K.barrier()
            if not os.environ.get("SKIP_CC"):
                K.allgather(ibs.ap().opt(), obs.ap().opt(), [[0, 1, 2, 3], [4, 5, 6, 7]], r=["ibs"], w=["obs"])

            with contextlib.ExitStack() as e2:
                NSUB = 4
                TT = NSUB * 128
                gi = K.sb("gi", [128, 2 * KD], I32, es=e2)
                adaT1 = K.sb("adaT2", [128, 4, D], es=e2)
                K.dma(sp, adaT1[:, :, :].rearrange("p b c -> p (b c)"), adaT_d[:, 0:4 * D], w=["adaT"])
                K.dma(sp, gi[:, :], gidx_in[:, :], w=["gi"])
                xr = [K.sb(f"xr{i}", [128, D], es=e2) for i in range(NSUB)]
                xn2s = [K.sb(f"xn2_{i}", [128, D], BF16, es=e2) for i in range(2)]
                st2s = [K.sb(f"st2_{i}", [128, 4], es=e2) for i in range(2)]
                mT = K.sb("mT", [128, KD, TT], BF16, es=e2)
                aT = K.sb("aT", [128, NJ, TT], BF16, es=e2)
                gsb = K.sb("gsb", [128, TT], es=e2)
                tmp = K.sb("tmp2", [128, 512], es=e2)
                NSLOT = 3
                wsl = [K.sb(f"wsl{i}", [128, KD, 512], BF16, es=e2) for i in range(NSLOT)]
                slot_n = [0]

                def wload(src_ap, nk, keys):
                    i = slot_n[0] % NSLOT
                    slot_n[0] += 1
                    K.dma(sp, wsl[i][:, 0:nk, :], src_ap, r=keys, w=[("wsl", i)])
                    return wsl[i], ("wsl", i)

                tiles = []
                t0 = 0
                while t0 < SEG:
                    n = min(TT, SEG - t0)
                    tiles.append((t0, [128] * (n // 128), 0))
                    t0 += n
                tiles.append((SEG, [16], 1))
                for (tok0, subs, ri) in tiles:
                    if ri == 1:
                        K.dma(sp, adaT1[:, :, :].rearrange("p b c -> p (b c)"), adaT_d[:, 4 * D:8 * D], r=["adaT"], w=["adaT"])
                    nt = sum(subs)
                    offs = [sum(subs[:i]) for i in range(len(subs))]
                    for si, ns in enumerate(subs):
                        src = x2[tok0 + offs[si]: tok0 + offs[si] + ns, :] if ri == 0 else xs2[:, :]
                        K.dma(sp, xr[si][0:ns, :], src, w=[("xr", si)])
                    for k in range(KD):
                        if ri == 1:
                            K.dma(pool, mT[:, k, 0:16], obs[:, :], r=["obs", "gi"], w=[("mT", k)], indirect=gi[:, KD + k:KD + k + 1])
                            continue
                        for h in range(nt // T1):
                            tl = tok0 // T1 + h
                            K.dma(pool, mT[:, k, h * T1:(h + 1) * T1], ob[:, :], r=[("ob", t_) for t_ in range(NSUP)] + ["gi"],
                                  w=[("mT", k)], indirect=gi[:, k:k + 1], eoff=tl * 2048 * T1)
                    for n in range(4):
                        wt, wk = wload(wo_bf[n], KD, [("wo", n)])
                        for si, ns in enumerate(subs):
                            pt, pk = K.ps()
                            for k in range(KD):
                                K.op(pe, lambda e, pt=pt, k=k, si=si, ns=ns, wt=wt: e.matmul(
                                    pt[0:ns, :], lhsT=mT[:, k, offs[si]:offs[si] + ns], rhs=wt[:, k, :],
                                    start=(k == 0), stop=(k == KD - 1)), r=[("mT", k), wk], w=[pk])
                            K.op(dve, lambda e, pt=pt, ns=ns, n=n: e.tensor_tensor(
                                out=tmp[0:ns, :], in0=pt[0:ns, :], in1=adaT1[0:ns, 0, n * 512:(n + 1) * 512], op=ALU.mult),
                                r=[pk, "adaT"], w=["tmp2"])
                            K.op(pool, lambda e, si=si, ns=ns, n=n: e.tensor_tensor(
                                out=xr[si][0:ns, n * 512:(n + 1) * 512], in0=xr[si][0:ns, n * 512:(n + 1) * 512],
                                in1=tmp[0:ns, :], op=ALU.add), r=["tmp2", ("xr", si)], w=[("xr", si)])
                    row = 0 if ri == 0 else 5
                    for si, ns in enumerate(subs):
                        xk = ("xr", si)
                        xn2 = xn2s[si % 2]; sq2 = xn2; st2 = st2s[si % 2]
                        kx2 = ("xn2", si % 2); ks2 = ("st2", si % 2)
                        K.op(act, lambda e, si=si, ns=ns: e.activation(out=sq2[0:ns, :], in_=xr[si][0:ns, :], func=AF.Square,
                                                                       accum_out=st2[0:ns, 0:1]), r=[xk], w=[kx2, ks2])
                        K.op(act, lambda e, ns=ns: e.activation(out=st2[0:ns, 1:2], in_=st2[0:ns, 0:1], func=AF.Sqrt,
                                                                bias=kc[0:ns, 0:1], scale=1.0 / D), r=[ks2, "kc"], w=[ks2])
                        K.op(dve, lambda e, ns=ns: e.reciprocal(out=st2[0:ns, 2:3], in_=st2[0:ns, 1:2]), r=[ks2], w=[ks2])
                        K.op(dve, lambda e, si=si, ns=ns: e.tensor_scalar(out=xn2[0:ns, :], in0=xr[si][0:ns, :],
                                                                          scalar1=st2[0:ns, 2:3], scalar2=None, op0=ALU.mult),
                             r=[xk, ks2], w=[kx2])
                        for k in range(KD):
                            pt, pk = K.ps()
                            K.op(pe, lambda e, k=k, pt=pt, ns=ns: e.matmul(
                                pt[:, 0:ns], lhsT=xn2[0:ns, k * 128:(k + 1) * 128],
                                rhs=identb[0:ns, 0:ns], start=True, stop=True), r=[kx2, "identb"], w=[pk])
                            if k % 2 == 0:
                                K.op(dve, lambda e, k=k, pt=pt, si=si, ns=ns: e.tensor_scalar(
                                    out=mT[:, k, offs[si]:offs[si] + ns], in0=pt[:, 0:ns],
                                    scalar1=gmul[:, 1, k, row:row + 1], scalar2=adaF[:, 2, k, row:row + 1],
                                    op0=ALU.mult, op1=ALU.add), r=[pk, "gmul", "adaF"], w=[("mT", k)])
                            else:
                                K.op(act, lambda e, k=k, pt=pt, si=si, ns=ns: e.activation(
                                    out=mT[:, k, offs[si]:offs[si] + ns], in_=pt[:, 0:ns],
                                    func=AF.Identity, scale=gmul[:, 1, k, row:row + 1], bias=adaF[:, 2, k, row:row + 1]),
                                    r=[pk, "gmul", "adaF"], w=[("mT", k)])
                    mTk = [("mT", k) for k in range(KD)]
                    for jb in range(11):
                        wg, wgk = wload(wg_bf[jb, 0], KD, [("wg", jb, 0)])
                        wu, wuk = wload(wg_bf[jb, 1], KD, [("wg", jb, 1)])
                        for jj in range(4):
                            j = jb * 4 + jj
                            pg, pgk = K.ps()
                            for k in range(KD):
                                K.op(pe, lambda e, pg=pg, k=k, jj=jj, wg=wg: e.matmul(
                                    pg[:, 0:nt], lhsT=wg[:, k, jj * 128:(jj + 1) * 128], rhs=mT[:, k, 0:nt],
                                    start=(k == 0), stop=(k == KD - 1)), r=[wgk, ("mT", k)], w=[pgk])
                            pu, puk = K.ps()
                            for k in range(KD):
                                K.op(pe, lambda e, pu=pu, k=k, jj=jj, wu=wu: e.matmul(
                                    pu[:, 0:nt], lhsT=wu[:, k, jj * 128:(jj + 1) * 128], rhs=mT[:, k, 0:nt],
                                    start=(k == 0), stop=(k == KD - 1)), r=[wuk, ("mT", k)], w=[puk])
                            K.op(act, lambda e, pg=pg: e.activation(out=gsb[:, 0:nt], in_=pg[:, 0:nt], func=AF.Silu),
                                 r=[pgk], w=["gsb"])
                            K.op(dve, lambda e, pu=pu, j=j: e.tensor_tensor(out=aT[:, j, 0:nt], in0=gsb[:, 0:nt], in1=pu[:, 0:nt],
                                                                            op=ALU.mult), r=["gsb", puk], w=[("aT", j)])
                    for n in range(4):
                        pts = [K.ps() for _ in subs]
                        for jq in range(4):
                            wd, wdk = wload(wd_bf[n, jq], 11, [("wd", n, jq)])
                            for si, ns in enumerate(subs):
                                pt, pk = pts[si]
                                for jj in range(11):
                                    j = jq * 11 + jj
                                    K.op(pe, lambda e, pt=pt, j=j, jj=jj, si=si, ns=ns, wd=wd: e.matmul(
                                        pt[0:ns, :], lhsT=aT[:, j, offs[si]:offs[si] + ns], rhs=wd[:, jj, :],
                                        start=(j == 0), stop=(j == NJ - 1)), r=[("aT", j), wdk], w=[pk])
                        for si, ns in enumerate(subs):
                            pt, pk = pts[si]
                            K.op(dve, lambda e, pt=pt, ns=ns, n=n: e.tensor_tensor(
                                out=tmp[0:ns, :], in0=pt[0:ns, :], in1=adaT1[0:ns, 1, n * 512:(n + 1) * 512], op=ALU.mult),
                                r=[pk, "adaT"], w=["tmp2"])
                            K.op(pool, lambda e, si=si, ns=ns, n=n: e.tensor_tensor(
                                out=xr[si][0:ns, n * 512:(n + 1) * 512], in0=xr[si][0:ns, n * 512:(n + 1) * 512],
                                in1=tmp[0:ns, :], op=ALU.add), r=["tmp2", ("xr", si)], w=[("xr", si)])
                    for si, ns in enumerate(subs):
                        xk = ("xr", si)
                        xn2 = xn2s[si % 2]; sq2 = xn2; st2 = st2s[si % 2]
                        kx2 = ("xn2", si % 2); ks2 = ("st2", si % 2)
                        K.op(act, lambda e, si=si, ns=ns: e.activation(out=sq2[0:ns, :], in_=xr[si][0:ns, :], func=AF.Square,
                                                                       accum_out=st2[0:ns, 0:1]), r=[xk], w=[kx2, ks2])
                        K.op(act, lambda e, ns=ns: e.activation(out=st2[0:ns, 1:2], in_=st2[0:ns, 0:1], func=AF.Sqrt,
                                                                bias=kc[0:ns, 0:1], scale=1.0 / D), r=[ks2, "kc"], w=[ks2])
                        K.op(dve, lambda e, ns=ns: e.reciprocal(out=st2[0:ns, 2:3], in_=st2[0:ns, 1:2]), r=[ks2], w=[ks2])
                        K.op(dve, lambda e, si=si, ns=ns: e.scalar_tensor_tensor(
                            out=xr[si][0:ns, :], in0=xr[si][0:ns, :], scalar=st2[0:ns, 2:3], in1=adaT1[0:ns, 3, :],
                            op0=ALU.mult, op1=ALU.mult), r=[xk, ks2, "adaT"], w=[xk])
                        K.op(pool, lambda e, si=si, ns=ns: e.tensor_tensor(out=xr[si][0:ns, :], in0=xr[si][0:ns, :],
                                                                           in1=adaT1[0:ns, 2, :], op=ALU.add),
                             r=[xk, "adaT"], w=[xk])
                        K.dma(sp, y2[tok0 + offs[si]: tok0 + offs[si] + ns, :], xr[si][0:ns, :], r=[xk], w=["y2"])
            K.barrier()
    return nc


def _consts(r):
    j = np.arange(128)[:, None]; i = np.arange(128)[None, :]
    ident = (i == j).astype(np.float32)
    triI = (i >= j).astype(np.float32)
    triS = (i > j).astype(np.float32)
    ones = np.ones((128, 128), np.float32)
    sel = [(np.broadcast_to(j == 32 * g, (128, 128))).astype(np.float32) for g in range(4)]
    bm16 = ((i // 16) == (j // 16)).astype(np.float32)
    lv = []
    for s_ in (16, 32, 64):
        mk = (((j // (2 * s_)) == (i // (2 * s_))) & ((j % (2 * s_)) >= s_) & ((i % (2 * s_)) < s_)).astype(np.float32)
        lv += [mk, mk.T.copy()]
    cmat = np.concatenate([ident, triI, triS, ones] + sel + [bm16] + lv, axis=1)
    iota = np.broadcast_to(np.arange(256, dtype=np.float32)[None, :], (128, 256))
    rmask = np.broadcast_to((np.arange(256) % 128 != 0).astype(np.float32)[None, :], (128, 256))
    inv = (1.0 / (10000.0 ** np.linspace(0.0, 1.0, 128, dtype=np.float32))).astype(np.float32)
    inv2pi = (inv.astype(np.float64) / (2 * np.pi)).astype(np.float32)[:, None]
    crow = np.concatenate([iota, rmask, inv2pi], axis=1).astype(np.float32)
    lg = math.log(1.0 - 2.0 ** (-5.0 - r))

    def rcs(C, reps):
        idx = np.arange(C, dtype=np.float64)
        xi = np.exp((idx + 1.0) * lg); ze = np.exp((C - 1.0 - idx) * lg) / 16.0
        dm = np.zeros((128, 128))
        jj = np.arange(C)[:, None]; ii = np.arange(C)[None, :]
        dm[:C, :C] = np.where(ii >= jj, np.exp(np.where(ii >= jj, ii - jj, 0) * lg), 0.0) / 16.0
        return (np.broadcast_to(np.tile(xi, reps)[None, :], (128, C * reps)), np.broadcast_to(np.tile(ze, reps)[None, :], (128, C * reps)),
                dm, np.full((128, 1), math.exp(C * lg)))
    a = rcs(128, 2); b_ = rcs(16, 1)
    rc = np.concatenate(list(a) + list(b_), axis=1).astype(np.float32)
    assert rc.shape == (128, 802)
    return cmat, crow, rc


def _chan(r):
    out = []
    for jh in range(2):
        h = 2 * r + jh
        for base in (0, 1024, 2048):
            out.append(base + h * 128 + np.arange(128))
    return np.stack(out)


def _wcols(r):
    cols = []
    for jh in range(2):
        h = 2 * r + jh
        for base in (0, 1024, 2048, 3072):
            cols.append(base + h * 128 + np.arange(128))
    rq = 4112 + r * 256 + np.arange(256); rk = 5136 + r * 256 + np.arange(256)
    rv = 6160 + r * 256 + np.arange(256); rg = 7184 + r * 256 + np.arange(256)
    cols += [rq[0::2], rq[1::2], rk[0::2], rk[1::2], rv[:128], rv[128:], rg[:128], rg[128:]]
    small = np.concatenate([np.full(32, 4096 + 2 * r), np.full(32, 4096 + 2 * r + 1),
                            np.full(32, 4104 + 2 * r), np.full(32, 4104 + 2 * r + 1)])
    cols.append(small)
    return np.concatenate(cols)


_ROWPERM = np.concatenate([np.arange(0, 256, 2), np.arange(1, 256, 2)])


def make_in_maps(inp, S):
    SEG = S // 4
    f = lambda a: np.ascontiguousarray(a, dtype=np.float32)
    maps = []
    w_in = inp["w_in"][0]; w_out = inp["w_out"][0]
    shared = dict(
        w_gu=f(inp["w_gu"][0]), w_down=f(inp["w_down"][0]), w_ada=f(inp["w_ada"][0]),
        b_ada_f=f(inp["b_ada"][0].reshape(96, 128).T[:, :]), b_ada_r=f(inp["b_ada"][0][None, :]),
        w_adaf=f(inp["w_ada_final"]), b_adaf_r=f(inp["b_ada_final"][None, :]),
        nmix=f(inp["norm_mix"][0].reshape(16, 128).T), nffn=f(inp["norm_ffn"][0].reshape(16, 128).T),
        nfin=f(np.broadcast_to(inp["norm_final"][None, :], (128, D))),
    )
    for c in range(8):
        b, r = c // 4, c % 4
        cmat, crow, rc = _consts(r)
        ch = _chan(r)
        rows_w = np.concatenate([np.concatenate([np.arange(256 * q, 256 * q + 256), 1024 + np.arange(256 * q, 256 * q + 256)])
                                 for q in range(4)])
        crows = [inp["c_prompt"][b]] + [inp["c_sample"][4 * b + i] for i in range(4)] + [inp["c_sample"][4 * b + r]]
        m = dict(shared)
        m.update(
            x1=f(inp["x_prompt"][b]), xs1=f(inp["x_sample"][4 * b:4 * b + 4].reshape(64, D)),
            x2=f(inp["x_prompt"][b, r * SEG:(r + 1) * SEG]), xs2=f(inp["x_sample"][4 * b + r]),
            cT=f(np.stack(crows, axis=1)), w_in_o=f(w_in[:, _wcols(r)]), w_out_p=f(w_out[rows_w, :]),
            cw=f(inp["conv_w"][0][:, ch].transpose(2, 1, 0).reshape(128, 24)),
            cst=f(inp["state_conv"][0, 4 * b:4 * b + 4][:, :, ch].transpose(0, 3, 2, 1).reshape(4, 128, 18)),
            dnc=f(np.concatenate([np.broadcast_to(inp["dn_a_log"][0, 2 * r:2 * r + 2][None, :], (128, 2)),
                                  np.broadcast_to(inp["dn_dt_bias"][0, 2 * r:2 * r + 2][None, :], (128, 2)),
                                  inp["dn_norm"][0][:, None]], axis=1)),
            sd=f(inp["state_delta"][0, 4 * b:4 * b + 4, 2 * r:2 * r + 2]),
            sr=f(inp["state_ret"][0, 4 * b:4 * b + 4, r][:, _ROWPERM, :]),
            cmat=cmat, crow=crow, rc=rc,
            gidx=np.ascontiguousarray(np.concatenate([
                r * (SEG // 256) * 2048 + (np.arange(16)[None, :] // 4) * 512 + (np.arange(16)[None, :] % 4) * 128 + np.arange(128)[:, None],
                (np.arange(16)[None, :] // 4) * 2048 + r * 512 + (np.arange(16)[None, :] % 4) * 128 + np.arange(128)[:, None]], axis=1),
                dtype=np.int32),
        )
        maps.append(m)
    return maps


def assemble(res, S):
    SEG = S // 4
    yp = np.zeros((2, S, D), np.float32); ys = np.zeros((8, 16, D), np.float32)
    cp = np.zeros((1, 2, 3, 3072), np.float32); dp = np.zeros((1, 2, 8, 128, 128), np.float32)
    rp = np.zeros((1, 2, 4, 256, 256), np.float32)
    cs = np.zeros((1, 8, 3, 3072), np.float32); ds = np.zeros((1, 8, 8, 128, 128), np.float32)
    rs = np.zeros((1, 8, 4, 256, 256), np.float32)
    for c in range(8):
        b, r = c // 4, c % 4
        o = res[c]
        yp[b, r * SEG:(r + 1) * SEG] = o["y2"][:SEG]
        ys[4 * b + r] = o["y2"][SEG:]
        ch = _chan(r)
        cv = o["convp"].reshape(128, 6, 3)
        for ci in range(6):
            cp[0, b][:, ch[ci]] = cv[:, ci, :].T
        dp[0, b, 2 * r:2 * r + 2] = o["deltap"]
        rp[0, b, r][_ROWPERM] = o["retp"]
        for i in range(4):
            cv = o["convs"][i].reshape(128, 6, 3)
            for ci in range(6):
                cs[0, 4 * b + i][:, ch[ci]] = cv[:, ci, :].T
            ds[0, 4 * b + i, 2 * r:2 * r + 2] = o["deltas"][i]
            rs[0, 4 * b + i, r][_ROWPERM] = o["rets"][i]
    return yp, ys, cp, dp, rp, cs, ds, rs


_NC_CACHE = {}


def kernel(**inputs):
    inp = {k: np.asarray(v) for k, v in inputs.items()}
    S = inp["x_prompt"].shape[1]
    if S not in _NC_CACHE:
        _NC_CACHE[S] = build(S)
    nc = _NC_CACHE[S]
    maps = make_in_maps(inp, S)
    res = run_bass_kernel_spmd(nc, maps, core_ids=list(range(8)))
    return assemble(res.results, S)
```

```python
import math
import contextlib
import numpy as np
import concourse.bass as bass
import concourse.mybir as mybir
from concourse.bass_utils import run_bass_kernel_spmd

F32 = mybir.dt.float32
BF16 = mybir.dt.bfloat16
I32 = mybir.dt.int32
ALU = mybir.AluOpType
AF = mybir.ActivationFunctionType

D = 2048
KD = 16
DFF = 5632
NJ = 44
NCH = 17
PW = NCH * 128
EPS = 1e-6
PAST_LEN = 2048
MAGIC = 12582912.0
TWO_PI = 2.0 * math.pi


class Eng:
    def __init__(self, name, obj, sem, step):
        self.name, self.obj, self.sem, self.step = name, obj, sem, step
        self.cnt = 0
        self.waited = {}


class KB:
    def __init__(self, nc, es):
        self.nc, self.es = nc, es
        sem = lambda n: es.enter_context(nc.semaphore(n))
        self.pe = Eng("pe", nc.tensor, sem("s_pe"), 1)
        self.dve = Eng("dve", nc.vector, sem("s_dve"), 1)
        self.act = Eng("act", nc.scalar, sem("s_act"), 1)
        self.pool = Eng("pool", nc.gpsimd, sem("s_pool"), 1)
        self.sp = Eng("sp", nc.sync, sem("s_sp"), 1)
        self.engs = [self.pe, self.dve, self.act, self.pool, self.sp]
        self.dsem = {}
        for q in ("sp", "pool", "act"):
            self.dsem[q] = [Eng(f"d_{q}{i}", None, sem(f"s_d_{q}{i}"), 16) for i in range(8)]
        self.dnext = {"sp": 0, "pool": 0, "act": 0}
        self.cc = Eng("cc", None, sem("s_cc"), 1)
        self.lw = {}
        self.rd = {}
        self.psn = 0
        self.ns = ""
        self.alias = {}
        self.rec = None
        self.psub = {"d0_": [0, 1], "d1_": [2, 3], "r_": [4, 5], "x_": [6, 7]}
        self.psc = {}
        self.ps_t = [es.enter_context(nc.psum_tensor(f"ps{i}", [128, 512], F32)) for i in range(8)]

    def sb(self, name, shape, dt=F32, es=None):
        return (es or self.es).enter_context(self.nc.sbuf_tensor("sb_" + name, shape, dt))

    def ps(self):
        if self.ns in self.psub:
            sub = self.psub[self.ns]
            c = self.psc.get(self.ns, 0)
            self.psc[self.ns] = c + 1
            i = sub[c % len(sub)]
            return self.ps_t[i], ("ps", i)
        i = self.psn % 8
        self.psn += 1
        return self.ps_t[i], ("ps", i)

    def replay(self, lists):
        idx = [0] * len(lists)
        ns_save, self.ns = self.ns, ""
        while True:
            best, bf = -1, 2.0
            for i, l in enumerate(lists):
                if idx[i] < len(l):
                    f = idx[i] / len(l)
                    if f < bf:
                        best, bf = i, f
            if best < 0:
                break
            it = lists[best][idx[best]]
            idx[best] += 1
            if it[0] == "op":
                self.op(it[1], it[2], it[3], it[4])
            else:
                self.dma(it[1], it[2], it[3], it[4], it[5], it[6], it[7])
        self.ns = ns_save

    def _nk(self, ks):
        if not self.ns:
            return ks
        pf = ("f_", "b_", "c_", "cb_", "cb2_", "dcol", "st3")
        al = self.alias.get(self.ns, {})
        return [self.ns + al.get(k, k) if isinstance(k, str) and k.startswith(pf) else k for k in ks]

    def _deps(self, E, r, w, extra=()):
        deps = {}

        def need(p):
            if p is None:
                return
            e, s = p
            if e is self.pe and E is self.pe:
                return
            if deps.get(e, (None, 0))[1] < s:
                deps[e] = (e, s)
        for k in r:
            need(self.lw.get(k))
        for k in w:
            need(self.lw.get(k))
            for e, s in self.rd.get(k, {}).items():
                need((e, s))
        for p in extra:
            need(p)
        for e, s in deps.values():
            if E.waited.get(e.name, 0) < s:
                E.obj.wait_ge(e.sem, s)
                E.waited[e.name] = s

    def _mark(self, P, seq, r, w):
        for k in r:
            self.rd.setdefault(k, {})[P] = seq
        for k in w:
            self.lw[k] = (P, seq)
            self.rd[k] = {}

    def op(self, E, fn, r=(), w=()):
        r, w = self._nk(r), self._nk(w)
        if self.rec is not None:
            self.rec.append(("op", E, fn, r, w))
            return
        self._deps(E, r, w)
        ins = fn(E.obj)
        E.cnt += 1
        ins.then_inc(E.sem, 1)
        self._mark(E, E.cnt, r, w)

    def dma(self, Q, out, in_, r=(), w=(), indirect=None, eoff=0):
        r, w = self._nk(r), self._nk(w)
        if self.rec is not None:
            self.rec.append(("dma", Q, out, in_, r, w, indirect, eoff))
            return
        pool = self.dsem[Q.name]
        Dk = pool[self.dnext[Q.name] % len(pool)]
        self.dnext[Q.name] += 1
        extra = [(Dk, Dk.cnt * 16)] if Dk.cnt else []
        self._deps(Q, r, w, extra)
        if indirect is not None:
            ins = Q.obj.indirect_dma_start(out=out, out_offset=None, in_=in_,
                                           in_offset=bass.IndirectOffsetOnAxis(ap=indirect, axis=0), element_offset=eoff)
        else:
            ins = Q.obj.dma_start(out=out, in_=in_)
        ins.then_inc(Dk.sem, 16)
        Dk.cnt += 1
        self._mark(Dk, Dk.cnt * 16, r, w)

    def allgather(self, in_ap, out_ap, groups, r=(), w=()):
        Q = self.pool
        self._deps(Q, r, w)
        ins = Q.obj.collective_compute("AllGather", ALU.bypass, replica_groups=groups, ins=[in_ap], outs=[out_ap])
        ins.then_inc(self.cc.sem, 1)
        self.cc.cnt += 1
        self._mark(self.cc, self.cc.cnt, r, w)

    def barrier(self):
        allp = self.engs + [d for q in self.dsem.values() for d in q] + [self.cc]
        for E in self.engs:
            for P in allp:
                s = P.cnt * P.step
                if P is E or s == 0:
                    continue
                if E.waited.get(P.name, 0) < s:
                    E.obj.wait_ge(P.sem, s)
                    E.waited[P.name] = s
        self.lw.clear()
        self.rd.clear()


def build(S):
    SEG = S // 4
    SEGW = SEG + 16
    T1 = 256
    assert S % T1 == 0 and SEG % 128 == 0
    nc = bass.Bass("TRN2", target_bir_lowering=False)
    din = lambda n, sh, dt=F32: nc.dram_tensor(n, sh, dt, kind="ExternalInput")
    dout = lambda n, sh: nc.dram_tensor(n, sh, F32, kind="ExternalOutput")
    x1 = din("x1", [S, D]); xs1 = din("xs1", [64, D]); x2 = din("x2", [SEG, D]); xs2 = din("xs2", [16, D])
    cT = din("cT", [D, 6]); w_in_o = din("w_in_o", [D, PW]); w_out_p = din("w_out_p", [D, D])
    w_gu = din("w_gu", [D, 2 * DFF]); w_down = din("w_down", [DFF, D])
    w_ada = din("w_ada", [D, 6 * D]); b_ada_f = din("b_ada_f", [128, 96]); b_ada_r = din("b_ada_r", [1, 6 * D])
    w_adaf = din("w_adaf", [D, 2 * D]); b_adaf_r = din("b_adaf_r", [1, 2 * D])
    nmix = din("nmix", [128, KD]); nffn = din("nffn", [128, KD]); nfin = din("nfin", [128, D])
    cw_in = din("cw", [128, 24]); cst_in = din("cst", [4, 128, 18]); dnc_in = din("dnc", [128, 5])
    sd_in = din("sd", [4, 2, 128, 128]); sr_in = din("sr", [4, 256, 256])
    cmat = din("cmat", [128, 15 * 128]); crow = din("crow", [128, 512 + 1]); rc_in = din("rc", [128, 802])
    gidx_in = din("gidx", [128, 2 * KD], I32)
    y2 = dout("y2", [SEGW, D]); convp = dout("convp", [128, 18]); deltap = dout("deltap", [2, 128, 128])
    retp = dout("retp", [256, 256]); convs = dout("convs", [4, 128, 18]); deltas = dout("deltas", [4, 2, 128, 128])
    rets = dout("rets", [4, 256, 256])
    NSUP = S // T1
    ib = nc.dram_tensor("ib", [NSUP * 512, T1], BF16)
    ob = nc.dram_tensor("ob", [NSUP * 2048, T1], BF16)
    ibs = nc.dram_tensor("ibs", [4 * 512, 16], BF16)
    obs = nc.dram_tensor("obs", [4 * 2048, 16], BF16)
    wo_bf = nc.dram_tensor("wo_bf", [4, 128, KD, 512], BF16)
    wg_bf = nc.dram_tensor("wg_bf", [11, 2, 128, KD, 512], BF16)
    wd_bf = nc.dram_tensor("wd_bf", [4, 4, 128, 11, 512], BF16)
    adaT_d = nc.dram_tensor("adaT_d", [128, 8 * D], F32)

    with contextlib.ExitStack() as es, nc.Block() as block:
        @block.sync
        def _(_sync):
            K = KB(nc, es)
            pe, dve, act, pool, sp = K.pe, K.dve, K.act, K.pool, K.sp
            cm = K.sb("cm", [128, 15 * 128])
            cr = K.sb("cr", [128, 513])
            rc = K.sb("rc", [128, 802])
            kc = K.sb("kc", [128, 8])
            identb = K.sb("identb", [128, 128], BF16)
            K.dma(sp, cm[:, :], cmat[:, :], w=["cm"])
            K.dma(sp, cr[:, :], crow[:, :], w=["cr"])
            K.dma(sp, rc[:, :], rc_in[:, :], w=["rc"])
            K.op(dve, lambda e: e.memset(kc[:, 0:1], EPS), w=["kc"])
            K.op(dve, lambda e: e.memset(kc[:, 1:2], 1.0), w=["kc"])
            K.op(dve, lambda e: e.memset(kc[:, 2:3], 0.0), w=["kc"])
            ident = cm[:, 0:128]; triI = cm[:, 128:256]; triS = cm[:, 256:384]; ones = cm[:, 384:512]
            sel = [cm[:, 512 + 128 * g: 640 + 128 * g] for g in range(4)]
            bm16 = cm[:, 1024:1152]
            lvm = [(cm[:, 1152 + 256 * i: 1280 + 256 * i], cm[:, 1280 + 256 * i: 1408 + 256 * i]) for i in range(3)]
            K.op(dve, lambda e: e.tensor_copy(out=identb[:, :], in_=ident), r=["cm"], w=["identb"])
            iota = cr[:, 0:256]; rmask = cr[:, 256:512]; inv2pi = cr[:, 512:513]
            xi128 = rc[:, 0:256]; ze128 = rc[:, 256:512]; dm128 = rc[:, 512:640]; cd128 = rc[:, 640:641]
            xi16 = rc[:, 641:657]; ze16 = rc[:, 657:673]; dm16 = rc[:, 673:801]; cd16 = rc[:, 801:802]
            adaF = K.sb("adaF", [128, 4, KD, 6])
            gmul = K.sb("gmul", [128, 2, KD, 6])
            nm = K.sb("nm", [128, 2, KD])
            K.dma(sp, nm[:, 0, :], nmix[:, :], w=["nm"])
            K.dma(sp, nm[:, 1, :], nffn[:, :], w=["nm"])

            def cast_weights_gen():
                for n in range(4):
                    for k in range(KD):
                        K.dma(pool, wo_bf[n, :, k, :], w_out_p[k * 128:(k + 1) * 128, n * 512:(n + 1) * 512],
                              w=[("wo", n)])
                        yield
                for jb in range(11):
                    for gu in range(2):
                        for k in range(KD):
                            K.dma(pool, wg_bf[jb, gu, :, k, :],
                                  w_gu[k * 128:(k + 1) * 128, gu * DFF + jb * 512: gu * DFF + (jb + 1) * 512],
                                  w=[("wg", jb, gu)])
                            yield
                for n in range(4):
                    for jq in range(4):
                        for jj in range(11):
                            j = jq * 11 + jj
                            K.dma(pool, wd_bf[n, jq, :, jj, :], w_down[j * 128:(j + 1) * 128, n * 512:(n + 1) * 512],
                                  w=[("wd", n, jq)])
                            yield

            with contextlib.ExitStack() as e0:
                cs = K.sb("cs", [128, KD, 6], es=e0)
                csr = K.sb("csr", [128, 2, KD, 128], BF16, es=e0)
                csb = K.sb("csb", [128, KD, 6], BF16, es=e0)
                wblk = [K.sb(f"wblk{i}", [128, KD, 512], es=e0) for i in range(2)]
                wbf = [K.sb(f"wbf{i}", [128, KD, 512], BF16, es=e0) for i in range(2)]
                onesb = K.sb("onesb", [1, 128], BF16, es=e0)
                browb = [K.sb(f"browb{i}", [1, 512], BF16, es=e0) for i in range(2)]
                K.op(dve, lambda e: e.memset(onesb[:, :], 1.0), w=["onesb"])
                bfe = K.sb("bfe", [128, 96], es=e0)
                browt = [K.sb(f"brow{i}", [1, 512], es=e0) for i in range(2)]
                adaT = K.sb("adaT", [128, 2, 4, D], es=e0)
                K.dma(sp, cs[:, :, :], cT.ap().rearrange("(k p) r -> p k r", p=128), w=["cs"])
                K.dma(sp, bfe[:, :], b_ada_f[:, :], w=["bfe"])
                K.op(act, lambda e: e.activation(out=cs[:, :, :], in_=cs[:, :, :], func=AF.Silu), r=["cs"], w=["cs"])
                K.op(dve, lambda e: e.tensor_copy(out=csb[:, :, :], in_=cs[:, :, :]), r=["cs"], w=["csb"])
                for ri, row in enumerate((0, 5)):
                    for k in range(KD):
                        K.op(dve, lambda e, ri=ri, row=row, k=k: e.tensor_copy(
                            out=csr[:, ri, k, :], in_=cs[:, k, row:row + 1].to_broadcast([128, 128])),
                            r=["cs"], w=["csr"])
                fmap = {0: 0, 1: 1, 3: 2, 4: 3}
                tmap = {2: 0, 5: 1}
                for blk in range(32):
                    wt = wblk[blk % 2]; wk = ("wblk", blk % 2)
                    wb = wbf[blk % 2]; wbk = ("wbf", blk % 2)
                    if blk < 24:
                        src = w_ada[:, blk * 512:(blk + 1) * 512]
                    else:
                        src = w_adaf[:, (blk - 24) * 512:(blk - 23) * 512]
                    srcv = src.rearrange("(k p) n -> p k n", p=128)
                    K.dma(sp, wt[:, 0:8, :], srcv[:, 0:8, :], w=[wk + (0,)])
                    K.dma(sp, wt[:, 8:16, :], srcv[:, 8:16, :], w=[wk + (1,)])
                    split = blk // 4 if blk < 24 else 6 + (blk - 24) // 4
                    brow = browt[blk % 2]; bk = ("brow", blk % 2)
                    bsrc = b_ada_r[:, blk * 512:(blk + 1) * 512] if blk < 24 else b_adaf_r[:, (blk - 24) * 512:(blk - 23) * 512]
                    K.dma(sp, brow[:, :], bsrc, w=[bk])
                    bb = browb[blk % 2]; bbk = ("browb", blk % 2)
                    K.op(dve, lambda e, bb=bb, brow=brow: e.tensor_copy(out=bb[:, :], in_=brow[:, :]), r=[bk], w=[bbk])
                    K.op(dve, lambda e, wb=wb, wt=wt: e.tensor_copy(out=wb[:, 0:8, :], in_=wt[:, 0:8, :]), r=[wk + (0,)], w=[wbk + (0,)])
                    K.op(act, lambda e, wb=wb, wt=wt: e.copy(out=wb[:, 8:12, :], in_=wt[:, 8:12, :]), r=[wk + (1,)], w=[wbk + (1,)])
                    K.op(pool, lambda e, wb=wb, wt=wt: e.tensor_copy(out=wb[:, 12:16, :], in_=wt[:, 12:16, :]), r=[wk + (1,)], w=[wbk + (2,)])
                    wbkk = lambda k, wbk=wbk: wbk + ((0,) if k < 8 else (1,) if k < 12 else (2,))
                    q = blk % 4
                    if split in fmap:
                        for cc in range(4):
                            pt, pk = K.ps()
                            for k in range(KD):
                                K.op(pe, lambda e, k=k, cc=cc, pt=pt, wb=wb: e.matmul(
                                    pt[:, 0:6], lhsT=wb[:, k, cc * 128:(cc + 1) * 128], rhs=csb[:, k, :],
                                    start=(k == 0), stop=(k == KD - 1)), r=[wbkk(k), "csb"], w=[pk])
                            n = q * 4 + cc
                            col = split * 16 + n
                            K.op(act, lambda e, n=n, col=col, pt=pt, sl=fmap[split]: e.activation(
                                out=adaF[:, sl, n, :], in_=pt[:, 0:6], func=AF.Identity,
                                bias=bfe[:, col:col + 1], scale=1.0), r=[pk, "bfe"], w=["adaF"])
                    else:
                        slot = tmap[split] if split in tmap else (2 if split == 6 else 3)
                        for ri in range(2):
                            pt, pk = K.ps()
                            for k in range(KD):
                                K.op(pe, lambda e, k=k, ri=ri, pt=pt, wb=wb: e.matmul(
                                    pt[:, :], lhsT=csr[:, ri, k, :], rhs=wb[:, k, :],
                                    start=(k == 0), stop=False), r=[wbkk(k), "csr"], w=[pk])
                            K.op(pe, lambda e, pt=pt, bb=bb: e.matmul(
                                pt[:, :], lhsT=onesb[0:1, :], rhs=bb[0:1, :],
                                start=False, stop=True), r=[bbk, "onesb"], w=[pk])
                            K.op(dve if ri == 0 else act,
                                 (lambda e, pt=pt, ri=ri, slot=slot, q=q: e.tensor_copy(
                                     out=adaT[:, ri, slot, q * 512:(q + 1) * 512], in_=pt[:, :])) if ri == 0 else
                                 (lambda e, pt=pt, ri=ri, slot=slot, q=q: e.copy(
                                     out=adaT[:, ri, slot, q * 512:(q + 1) * 512], in_=pt[:, :])),
                                 r=[pk], w=["adaT"])
                for i, sl in enumerate((1, 3)):
                    K.op(dve, lambda e, i=i, sl=sl: e.tensor_scalar(
                        out=gmul[:, i, :, :], in0=adaF[:, sl, :, :], scalar1=1.0, scalar2=None, op0=ALU.add),
                        r=["adaF"], w=["gmul"])
                    K.op(dve, lambda e, i=i: e.tensor_tensor(
                        out=gmul[:, i, :, :], in0=gmul[:, i, :, :],
                        in1=nm[:, i, :].unsqueeze(2).to_broadcast([128, KD, 6]), op=ALU.mult),
                        r=["gmul", "nm"], w=["gmul"])
                nf = wblk[0]
                nfv = nf[:, 0:4, :].rearrange("p a b -> p (a b)")
                K.dma(sp, nfv, nfin[:, :], w=[("wblk", 0, 0), ("wblk", 0, 1)])
                for ri in range(2):
                    K.op(dve, lambda e, ri=ri: e.scalar_tensor_tensor(
                        out=adaT[:, ri, 3, :], in0=adaT[:, ri, 3, :], scalar=1.0, in1=nfv,
                        op0=ALU.add, op1=ALU.mult), r=["adaT", ("wblk", 0, 0)], w=["adaT"])
                K.dma(sp, adaT_d[:, :], adaT[:, :, :, :].rearrange("p a b c -> p (a b c)"), r=["adaT"], w=["adaT_d"])
            K.barrier()
            import os
            if os.environ.get("KSTOP") == "0":
                return
            cgen = cast_weights_gen()

            def pump(n):
                for _ in range(n):
                    if next(cgen, "done") == "done":
                        break
            if os.environ.get("KSTOP") == "0b":
                K.barrier()
                return

            with contextlib.ExitStack() as e1:
                w1 = K.sb("w1", [128, KD, PW], BF16, es=e1)
                if os.environ.get("W1_CASTDMA"):
                    for k in range(KD):
                        K.dma(pool, w1[:, k, :], w_in_o[k * 128:(k + 1) * 128, :], w=["w1"])
                cwt = K.sb("cwt", [128, 6, 4], es=e1)
                K.dma(sp, cwt[:, :, :].rearrange("p a b -> p (a b)"), cw_in[:, :], w=["cwt"])
                dnc = K.sb("dnc", [128, 5], es=e1)
                K.dma(sp, dnc[:, :], dnc_in[:, :], w=["dnc"])
                negA = K.sb("negA", [128, 2], es=e1)
                K.op(act, lambda e: e.activation(out=negA[:, :], in_=dnc[:, 0:2], func=AF.Exp), r=["dnc"], w=["negA"])
                K.op(dve, lambda e: e.tensor_scalar(out=negA[:, :], in0=negA[:, :], scalar1=-1.0, scalar2=None,
                                                    op0=ALU.mult), r=["negA"], w=["negA"])
                xt = [K.sb("xt0", [128, D], es=e1)]
                xn = K.sb("xn", [128, D], BF16, es=e1)
                sq_junk = xn
                st = K.sb("st", [128, 4], es=e1)
                hT = K.sb("hT", [128, KD, T1], BF16, es=e1)
                Pbufs = [K.sb(f"P{i}", [128, NCH, T1 + 3], es=e1) for i in range(2)]
                P = Pbufs[0]
                Sdn = K.sb("Sdn", [128, 2, 128], es=e1)
                Sdnb = K.sb("Sdnb", [128, 2, 128], BF16, es=e1)
                Sr = K.sb("Sr", [128, 2, 256], es=e1)
                Srb = K.sb("Srb", [128, 2, 256], BF16, es=e1)
                def mk(tag, fn, bn, cn, cbn, al=None):
                    al = al or {}
                    K.alias[tag + "_"] = {"f_" + a_: "f_" + b_ for a_, b_ in al.items()}
                    fd = {n: K.sb(tag + "f_" + n, [128, T1], es=e1) for n in fn if n not in al}
                    for a_, b_ in al.items():
                        fd[a_] = fd[b_]
                    return (fd,
                            {n: K.sb(tag + "b_" + n, [128, T1], BF16, es=e1) for n in bn},
                            {n: K.sb(tag + "c_" + n, [128, 128], es=e1) for n in cn},
                            {n: K.sb(tag + "cb_" + n, [128, 128], BF16, es=e1) for n in cbn})
                DNS = []
                for tag in ("d0", "d1"):
                    f_, b_, c_, cb_ = mk(tag, ("acc", "csq", "csk", "csv", "sq", "rr", "qn", "beta", "g", "d", "e", "kd", "eb", "oT", "zs", "ta"),
                                         ("q", "k", "kb", "kc", "kdT", "vb", "qd", "mx"), ("junk", "arg", "gam", "gamI", "gamS"),
                                         ("Pa", "Pb", "PTa", "PTb", "Ya", "Yb", "qk", "kdec", "R", "u", "N0", "PT0", "Dva", "Dvb", "W", "V", "N0m", "PTm"),
                                         al={"sq": "acc", "ta": "acc", "eb": "g", "zs": "csv", "qn": "csq"})
                    DNS.append((f_, b_, c_, cb_, K.sb(tag + "dcol", [128, 2], es=e1)))
                f_, b_, c_, cb_ = mk("r", ("u", "t1", "nf", "sin", "cos", "ta", "tb", "q1", "q2", "k1", "k2", "or0", "or1", "sq", "rr", "zs"),
                                     ("q1", "q2", "k1", "k2", "qx1", "qx2", "kz1", "kz2", "rv0", "rv1", "mx"), (), ("rqk",),
                                     al={"sq": "t1", "rr": "nf", "zs": "u"})
                RTS = (f_, b_, cb_, {n: K.sb("rcb2_" + n, [128, 256], BF16, es=e1) for n in ("v", "kz")})
                if not os.environ.get("W1_CASTDMA"):
                    for k in range(KD):
                        for hh in range(2):
                            stg = xt[0]; sk_ = ("xt", id(stg))
                            K.dma(sp, stg[:, 0:PW // 2], w_in_o[k * 128:(k + 1) * 128, hh * (PW // 2):(hh + 1) * (PW // 2)], w=[sk_])
                            K.op(dve if hh == 0 else pool, lambda e, k=k, hh=hh, stg=stg: e.tensor_copy(
                                out=w1[:, k, hh * (PW // 2):(hh + 1) * (PW // 2)], in_=stg[:, 0:PW // 2]), r=[sk_], w=["w1"])
                for pb_ in Pbufs:
                    K.op(dve, lambda e, pb_=pb_: e.memset(pb_[:, :, 0:3], 0.0), w=[("P", id(pb_), m) for m in range(NCH)])
                if os.environ.get("KSTOP") == "1a":
                    K.barrier()
                    return
                K.op(dve, lambda e: e.memset(Sdn[:, :, :], 0.0), w=["Sdn"])
                K.op(dve, lambda e: e.memset(Sdnb[:, :, :], 0.0), w=["Sdnb"])
                K.op(dve, lambda e: e.memset(Sr[:, :, :], 0.0), w=["Sr"])
                K.op(dve, lambda e: e.memset(Srb[:, :, :], 0.0), w=["Srb"])

                def norm_transpose(src_ap, ntok, toff, xbuf, scal):
                    xk = ("xt", id(xbuf))
                    K.dma(sp, xbuf[0:ntok, :], src_ap, w=[xk])
                    K.op(act, lambda e: e.activation(out=sq_junk[0:ntok, :], in_=xbuf[0:ntok, :], func=AF.Square,
                                                     accum_out=st[0:ntok, 0:1]), r=[xk], w=["xn", "st"])
                    K.op(act, lambda e: e.activation(out=st[0:ntok, 1:2], in_=st[0:ntok, 0:1], func=AF.Ln,
                                                     bias=kc[0:ntok, 0:1], scale=1.0 / D), r=["st", "kc"], w=["st"])
                    K.op(act, lambda e: e.activation(out=st[0:ntok, 2:3], in_=st[0:ntok, 1:2], func=AF.Exp, scale=-0.5),
                         r=["st"], w=["st"])
                    K.op(dve, lambda e: e.tensor_scalar(out=xn[0:ntok, :], in0=xbuf[0:ntok, :], scalar1=st[0:ntok, 2:3],
                                                        scalar2=None, op0=ALU.mult), r=[xk, "st"], w=["xn"])
                    for k in range(KD):
                        pt, pk = K.ps()
                        K.op(pe, lambda e, k=k, pt=pt: e.matmul(
                            pt[:, 0:ntok], lhsT=xn[0:ntok, k * 128:(k + 1) * 128],
                            rhs=identb[0:ntok, 0:ntok], start=True, stop=True), r=["xn", "identb"], w=[pk])
                        for (lo, hi, row) in scal:
                            E = dve if (k % 2 == 0) else act
                            if E is dve:
                                fn = lambda e, k=k, lo=lo, hi=hi, row=row, pt=pt: e.tensor_scalar(
                                    out=hT[:, k, toff + lo: toff + hi], in0=pt[:, lo:hi],
                                    scalar1=gmul[:, 0, k, row:row + 1], scalar2=adaF[:, 0, k, row:row + 1],
                                    op0=ALU.mult, op1=ALU.add)
                            else:
                                fn = lambda e, k=k, lo=lo, hi=hi, row=row, pt=pt: e.activation(
                                    out=hT[:, k, toff + lo: toff + hi], in_=pt[:, lo:hi],
                                    func=AF.Identity, scale=gmul[:, 0, k, row:row + 1],
                                    bias=adaF[:, 0, k, row:row + 1])
                            K.op(E, fn, r=[pk, "gmul", "adaF"], w=[("hT", k)])

                def project(T, dst, doff):
                    for m in range(NCH):
                        pt, pk = K.ps()
                        for k in range(KD):
                            K.op(pe, lambda e, k=k, m=m, pt=pt: e.matmul(
                                pt[:, 0:T], lhsT=w1[:, k, m * 128:(m + 1) * 128], rhs=hT[:, k, 0:T],
                                start=(k == 0), stop=(k == KD - 1)), r=["w1", ("hT", k)], w=[pk])
                        if m % 2 == 0:
                            K.op(dve, lambda e, m=m, pt=pt: e.tensor_copy(out=dst[:, m, doff:doff + T], in_=pt[:, 0:T]),
                                 r=[pk], w=[("P", id(dst), m)])
                        else:
                            K.op(act, lambda e, m=m, pt=pt: e.copy(out=dst[:, m, doff:doff + T], in_=pt[:, 0:T]),
                                 r=[pk], w=[("P", id(dst), m)])

                def rms_gate_out(srcs, gates, nfeat, wcol, T, dsts, F, B):
                    pt, pk = K.ps()
                    for i, (sa, skey) in enumerate(srcs):
                        K.op(dve, lambda e, sa=sa: e.tensor_tensor(out=F["sq"][:, 0:T], in0=sa, in1=sa, op=ALU.mult),
                             r=[skey], w=["f_sq"])
                        K.op(pe, lambda e, i=i, pt=pt: e.matmul(pt[:, 0:T], lhsT=ones, rhs=F["sq"][:, 0:T],
                                                                 start=(i == 0), stop=(i == len(srcs) - 1)),
                             r=["f_sq", "cm"], w=[pk])
                    K.op(act, lambda e, pt=pt: e.activation(out=F["rr"][:, 0:T], in_=pt[:, 0:T], func=AF.Ln,
                                                            bias=kc[:, 0:1], scale=1.0 / nfeat), r=[pk, "kc"], w=["f_rr"])
                    K.op(act, lambda e: e.activation(out=F["rr"][:, 0:T], in_=F["rr"][:, 0:T], func=AF.Exp, scale=-0.5),
                         r=["f_rr"], w=["f_rr"])
                    for i, (sa, skey) in enumerate(srcs):
                        K.op(act, lambda e, i=i: e.activation(out=F["zs"][:, 0:T], in_=P[:, gates[i], 3:3 + T], func=AF.Silu),
                             r=[("P", id(P), gates[i])], w=["f_zs"])
                        if wcol is not None:
                            K.op(dve, lambda e, sa=sa: e.scalar_tensor_tensor(
                                out=F["ta"][:, 0:T], in0=sa, scalar=wcol, in1=F["rr"][:, 0:T], op0=ALU.mult, op1=ALU.mult),
                                r=[skey, "f_rr", "dnc"], w=["f_ta"])
                        else:
                            K.op(dve, lambda e, sa=sa: e.tensor_tensor(out=F["ta"][:, 0:T], in0=sa, in1=F["rr"][:, 0:T],
                                                                       op=ALU.mult), r=[skey, "f_rr"], w=["f_ta"])
                        K.op(dve, lambda e: e.tensor_tensor(out=B["mx"][:, 0:T], in0=F["ta"][:, 0:T], in1=F["zs"][:, 0:T],
                                                            op=ALU.mult), r=["f_ta", "f_zs"], w=["b_mx"])
                        K.dma(sp, dsts[i][0], B["mx"][:, 0:T], r=["b_mx"], w=[dsts[i][1]])

                def dn_gen(j, T, C, dst_rows_col, F, B, C_, CB, dcol):
                    nchunk = T // C
                    L = int(math.log2(C))
                    Pk = lambda m: ("P", id(P), m)
                    m0 = 4 * j
                    for ci, nm_ in enumerate(("csq", "csk", "csv")):
                        m = m0 + ci
                        cwi = 3 * j + ci
                        K.op(dve, lambda e, m=m, cwi=cwi: e.tensor_scalar(
                            out=F["acc"][:, 0:T], in0=P[:, m, 0:T], scalar1=cwt[:, cwi, 0:1], scalar2=None,
                            op0=ALU.mult), r=[Pk(m), "cwt"], w=["f_acc"])
                        for i in range(1, 4):
                            K.op(dve, lambda e, m=m, cwi=cwi, i=i: e.scalar_tensor_tensor(
                                out=F["acc"][:, 0:T], in0=P[:, m, i:i + T], scalar=cwt[:, cwi, i:i + 1],
                                in1=F["acc"][:, 0:T], op0=ALU.mult, op1=ALU.add), r=[Pk(m), "cwt", "f_acc"], w=["f_acc"])
                        K.op(act, lambda e, nm_=nm_: e.activation(out=F[nm_][:, 0:T], in_=F["acc"][:, 0:T], func=AF.Silu),
                             r=["f_acc"], w=["f_" + nm_])
                        yield
                    for nm_ in ("csq", "csk"):
                        K.op(dve, lambda e, nm_=nm_: e.tensor_tensor(out=F["sq"][:, 0:T], in0=F[nm_][:, 0:T],
                                                                    in1=F[nm_][:, 0:T], op=ALU.mult),
                             r=["f_" + nm_], w=["f_sq"])
                        pt, pk = K.ps()
                        K.op(pe, lambda e, pt=pt: e.matmul(pt[:, 0:T], lhsT=ones, rhs=F["sq"][:, 0:T], start=True,
                                                           stop=True), r=["f_sq", "cm"], w=[pk])
                        K.op(act, lambda e, pt=pt: e.activation(out=F["rr"][:, 0:T], in_=pt[:, 0:T], func=AF.Ln,
                                                                bias=kc[:, 0:1], scale=1.0), r=[pk, "kc"], w=["f_rr"])
                        K.op(act, lambda e: e.activation(out=F["rr"][:, 0:T], in_=F["rr"][:, 0:T], func=AF.Exp, scale=-0.5),
                             r=["f_rr"], w=["f_rr"])
                        yield
                        if nm_ == "csq":
                            K.op(dve, lambda e: e.scalar_tensor_tensor(
                                out=F["qn"][:, 0:T], in0=F["csq"][:, 0:T], scalar=128.0 ** -0.5, in1=F["rr"][:, 0:T],
                                op0=ALU.mult, op1=ALU.mult), r=["f_csq", "f_rr"], w=["f_qn"])
                            K.op(pool, lambda e: e.tensor_copy(out=B["q"][:, 0:T], in_=F["qn"][:, 0:T]),
                                 r=["f_qn"], w=["b_q"])
                        else:
                            K.op(dve, lambda e: e.tensor_tensor(out=F["csk"][:, 0:T], in0=F["csk"][:, 0:T],
                                                                in1=F["rr"][:, 0:T], op=ALU.mult),
                                 r=["f_csk", "f_rr"], w=["f_csk"])
                            K.op(pool, lambda e: e.tensor_copy(out=B["k"][:, 0:T], in_=F["csk"][:, 0:T]),
                                 r=["f_csk"], w=["b_k"])
                    pt, pk = K.ps()
                    K.op(pe, lambda e, pt=pt: e.matmul(pt[:, 0:T], lhsT=sel[j], rhs=P[:, 16, 3:3 + T], start=True,
                                                       stop=True), r=[Pk(16), "cm"], w=[pk])
                    K.op(act, lambda e, pt=pt: e.activation(out=F["beta"][:, 0:T], in_=pt[:, 0:T], func=AF.Sigmoid),
                         r=[pk], w=["f_beta"])
                    pt, pk = K.ps()
                    K.op(pe, lambda e, pt=pt: e.matmul(pt[:, 0:T], lhsT=sel[2 + j], rhs=P[:, 16, 3:3 + T], start=True,
                                                       stop=True), r=[Pk(16), "cm"], w=[pk])
                    K.op(act, lambda e, pt=pt: e.activation(out=F["g"][:, 0:T], in_=pt[:, 0:T], func=AF.Exp,
                                                            bias=dnc[:, 2 + j:3 + j], scale=1.0), r=[pk, "dnc"], w=["f_g"])
                    K.op(act, lambda e: e.activation(out=F["g"][:, 0:T], in_=F["g"][:, 0:T], func=AF.Ln,
                                                     bias=kc[:, 1:2], scale=1.0), r=["f_g", "kc"], w=["f_g"])
                    K.op(dve, lambda e: e.tensor_scalar(out=F["g"][:, 0:T], in0=F["g"][:, 0:T], scalar1=negA[:, j:j + 1],
                                                        scalar2=None, op0=ALU.mult), r=["f_g", "negA"], w=["f_g"])
                    K.op(dve, lambda e: e.tensor_tensor_scan(out=F["d"][:, 0:T], data0=rmask[:, 0:T], data1=F["g"][:, 0:T],
                                                             initial=0.0, op0=ALU.mult, op1=ALU.add),
                         r=["f_g", "cr"], w=["f_d"])
                    K.op(act, lambda e: e.activation(out=F["e"][:, 0:T], in_=F["d"][:, 0:T], func=AF.Exp), r=["f_d"], w=["f_e"])
                    yield
                    for c in range(nchunk):
                        c0 = c * C
                        K.op(act, lambda e, c0=c0: e.activation(
                            out=F["kd"][:, c0:c0 + C], in_=F["d"][:, c0:c0 + C], func=AF.Exp, scale=-1.0,
                            bias=F["d"][:, c0 + C - 1:c0 + C]), r=["f_d"], w=["f_kd"])
                    K.op(dve, lambda e: e.tensor_tensor(out=F["eb"][:, 0:T], in0=F["e"][:, 0:T], in1=F["beta"][:, 0:T],
                                                        op=ALU.mult), r=["f_e", "f_beta"], w=["f_eb"])
                    K.op(pool, lambda e: e.tensor_tensor(out=B["kb"][:, 0:T], in0=F["csk"][:, 0:T], in1=F["beta"][:, 0:T],
                                                         op=ALU.mult), r=["f_csk", "f_beta"], w=["b_kb"])
                    K.op(dve, lambda e: e.scalar_tensor_tensor(out=B["kc"][:, 0:T], in0=F["csk"][:, 0:T], scalar=-1.0,
                                                               in1=F["eb"][:, 0:T], op0=ALU.mult, op1=ALU.mult),
                         r=["f_csk", "f_eb"], w=["b_kc"])
                    K.op(pool, lambda e: e.tensor_tensor(out=B["kdT"][:, 0:T], in0=F["csk"][:, 0:T], in1=F["kd"][:, 0:T],
                                                         op=ALU.mult), r=["f_csk", "f_kd"], w=["b_kdT"])
                    K.op(pool, lambda e: e.tensor_tensor(out=B["vb"][:, 0:T], in0=F["csv"][:, 0:T], in1=F["beta"][:, 0:T],
                                                         op=ALU.mult), r=["f_csv", "f_beta"], w=["b_vb"])
                    K.op(dve, lambda e: e.tensor_tensor(out=B["qd"][:, 0:T], in0=F["qn"][:, 0:T], in1=F["e"][:, 0:T],
                                                        op=ALU.mult), r=["f_qn", "f_e"], w=["b_qd"])
                    yield
                    for c in range(nchunk):
                        c0 = c * C
                        cs_ = slice(c0, c0 + C)
                        K.op(dve, lambda e, cs_=cs_: e.scalar_tensor_tensor(
                            out=C_["junk"][0:C, 0:C], in0=F["d"][0:C, cs_], scalar=1.0, in1=ident[0:C, 0:C],
                            op0=ALU.mult, op1=ALU.mult, accum_out=dcol[0:C, 0:1]), r=["f_d", "cm"], w=["c_junk", "dcol"])
                        K.op(dve, lambda e, cs_=cs_: e.tensor_scalar(
                            out=C_["arg"][0:C, 0:C], in0=F["d"][0:C, cs_], scalar1=dcol[0:C, 0:1], scalar2=0.0,
                            op0=ALU.subtract, op1=ALU.min), r=["f_d", "dcol"], w=["c_arg"])
                        K.op(act, lambda e: e.activation(out=C_["gam"][0:C, 0:C], in_=C_["arg"][0:C, 0:C], func=AF.Exp),
                             r=["c_arg"], w=["c_gam"])
                        K.op(pool, lambda e: e.tensor_tensor(out=C_["gamI"][0:C, 0:C], in0=C_["gam"][0:C, 0:C],
                                                             in1=triI[0:C, 0:C], op=ALU.mult), r=["c_gam", "cm"], w=["c_gamI"])
                        K.op(pool, lambda e: e.tensor_tensor(out=C_["gamS"][0:C, 0:C], in0=C_["gam"][0:C, 0:C],
                                                             in1=triS[0:C, 0:C], op=ALU.mult), r=["c_gam", "cm"], w=["c_gamS"])
                        pt, pk = K.ps()
                        K.op(pe, lambda e, pt=pt, cs_=cs_: e.matmul(pt[0:C, 0:C], lhsT=B["k"][:, cs_], rhs=B["kb"][:, cs_],
                                                                    start=True, stop=True), r=["b_k", "b_kb"], w=[pk])
                        K.op(dve, lambda e, pt=pt: e.scalar_tensor_tensor(
                            out=CB["N0" if C == 128 else "Pa"][0:C, 0:C], in0=pt[0:C, 0:C], scalar=-1.0, in1=C_["gamS"][0:C, 0:C],
                            op0=ALU.mult, op1=ALU.mult), r=[pk, "c_gamS"], w=["cb_N0" if C == 128 else "cb_Pa"])
                        if C == 128:
                            pt, pk = K.ps()
                            K.op(pe, lambda e, pt=pt: e.matmul(pt[:, 0:128], lhsT=CB["N0"][:, :], rhs=identb[:, :],
                                                               start=True, stop=True), r=["cb_N0", "identb"], w=[pk])
                            K.op(act, lambda e, pt=pt: e.copy(out=CB["PT0"][:, :], in_=pt[:, 0:128]), r=[pk], w=["cb_PT0"])
                            K.op(pool, lambda e: e.tensor_tensor(out=CB["Pa"][:, :], in0=CB["N0"][:, :], in1=bm16, op=ALU.mult),
                                 r=["cb_N0", "cm"], w=["cb_Pa"])
                        pt, pk = K.ps()
                        K.op(pe, lambda e, pt=pt, cs_=cs_: e.matmul(pt[0:C, 0:C], lhsT=B["k"][:, cs_], rhs=B["q"][:, cs_],
                                                                    start=True, stop=True), r=["b_k", "b_q"], w=[pk])
                        K.op(dve, lambda e, pt=pt: e.tensor_tensor(out=CB["qk"][0:C, 0:C], in0=pt[0:C, 0:C],
                                                                   in1=C_["gamI"][0:C, 0:C], op=ALU.mult),
                             r=[pk, "c_gamI"], w=["cb_qk"])
                        yield
                        pt, pk = K.ps()
                        K.op(pe, lambda e, pt=pt: e.matmul(pt[0:C, 0:C], lhsT=CB["Pa"][0:C, 0:C], rhs=identb[0:C, 0:C],
                                                           start=True, stop=True), r=["cb_Pa", "identb"], w=[pk])
                        K.op(act, lambda e, pt=pt: e.copy(out=CB["PTa"][0:C, 0:C], in_=pt[0:C, 0:C]), r=[pk], w=["cb_PTa"])
                        K.op(dve, lambda e: e.tensor_tensor(out=CB["Ya"][0:C, 0:C], in0=CB["Pa"][0:C, 0:C],
                                                            in1=ident[0:C, 0:C], op=ALU.add), r=["cb_Pa", "cm"], w=["cb_Ya"])
                        cur, nxt = "a", "b"
                        for l in range(1, min(L, 4)):
                            Pc, PTc, Yc = "P" + cur, "PT" + cur, "Y" + cur
                            Pn, PTn, Yn = "P" + nxt, "PT" + nxt, "Y" + nxt
                            pt, pk = K.ps()
                            K.op(pe, lambda e, pt=pt, Pc=Pc, PTc=PTc: e.matmul(
                                pt[0:C, 0:C], lhsT=CB[Pc][0:C, 0:C], rhs=CB[PTc][0:C, 0:C], start=True, stop=True),
                                r=["cb_" + Pc, "cb_" + PTc], w=[pk])
                            K.op(act, lambda e, pt=pt, PTn=PTn: e.copy(out=CB[PTn][0:C, 0:C], in_=pt[0:C, 0:C]),
                                 r=[pk], w=["cb_" + PTn])
                            if l < min(L, 4) - 1:
                                pt, pk = K.ps()
                                K.op(pe, lambda e, pt=pt, Pc=Pc, PTc=PTc: e.matmul(
                                    pt[0:C, 0:C], lhsT=CB[PTc][0:C, 0:C], rhs=CB[Pc][0:C, 0:C], start=True, stop=True),
                                    r=["cb_" + Pc, "cb_" + PTc], w=[pk])
                                K.op(act, lambda e, pt=pt, Pn=Pn: e.copy(out=CB[Pn][0:C, 0:C], in_=pt[0:C, 0:C]),
                                     r=[pk], w=["cb_" + Pn])
                            pt, pk = K.ps()
                            K.op(pe, lambda e, pt=pt, PTn=PTn, Yc=Yc: e.matmul(
                                pt[0:C, 0:C], lhsT=CB[PTn][0:C, 0:C], rhs=CB[Yc][0:C, 0:C], start=True, stop=True),
                                r=["cb_" + PTn, "cb_" + Yc], w=[pk])
                            K.op(dve, lambda e, pt=pt, Yc=Yc, Yn=Yn: e.tensor_tensor(
                                out=CB[Yn][0:C, 0:C], in0=pt[0:C, 0:C], in1=CB[Yc][0:C, 0:C], op=ALU.add),
                                r=[pk, "cb_" + Yc], w=["cb_" + Yn])
                            yield
                            cur, nxt = nxt, cur
                        Yf = "Y" + cur
                        if C == 128:
                            Ec, En = "Y" + cur, "Y" + nxt
                            Dc, Dn = "Dva", "Dvb"
                            pt, pk = K.ps()
                            K.op(pe, lambda e, pt=pt, Ec=Ec: e.matmul(pt[:, 0:128], lhsT=CB[Ec][:, :], rhs=identb[:, :],
                                                                      start=True, stop=True), r=["cb_" + Ec, "identb"], w=[pk])
                            K.op(act, lambda e, pt=pt, Dc=Dc: e.copy(out=CB[Dc][:, :], in_=pt[:, 0:128]), r=[pk], w=["cb_" + Dc])
                            yield
                            for li in range(3):
                                mk, mkT = lvm[li]
                                K.op(pool, lambda e, mk=mk: e.tensor_tensor(out=CB["PTm"][:, :], in0=CB["PT0"][:, :], in1=mk, op=ALU.mult),
                                     r=["cb_PT0", "cm"], w=["cb_PTm"])
                                pt, pk = K.ps()
                                K.op(pe, lambda e, pt=pt, Ec=Ec: e.matmul(pt[:, 0:128], lhsT=CB["PTm"][:, :], rhs=CB[Ec][:, :],
                                                                          start=True, stop=True), r=["cb_PTm", "cb_" + Ec], w=[pk])
                                K.op(act, lambda e, pt=pt: e.copy(out=CB["W"][:, :], in_=pt[:, 0:128]), r=[pk], w=["cb_W"])
                                pt, pk = K.ps()
                                K.op(pe, lambda e, pt=pt, Dc=Dc: e.matmul(pt[:, 0:128], lhsT=CB[Dc][:, :], rhs=CB["W"][:, :],
                                                                          start=True, stop=True), r=["cb_W", "cb_" + Dc], w=[pk])
                                K.op(dve, lambda e, pt=pt, Ec=Ec, En=En: e.tensor_tensor(out=CB[En][:, :], in0=pt[:, 0:128], in1=CB[Ec][:, :],
                                                                                       op=ALU.add), r=[pk, "cb_" + Ec], w=["cb_" + En])
                                yield
                                if li < 2:
                                    K.op(pool, lambda e, mkT=mkT: e.tensor_tensor(out=CB["N0m"][:, :], in0=CB["N0"][:, :], in1=mkT, op=ALU.mult),
                                         r=["cb_N0", "cm"], w=["cb_N0m"])
                                    pt, pk = K.ps()
                                    K.op(pe, lambda e, pt=pt, Dc=Dc: e.matmul(pt[:, 0:128], lhsT=CB["N0m"][:, :], rhs=CB[Dc][:, :],
                                                                              start=True, stop=True), r=["cb_N0m", "cb_" + Dc], w=[pk])
                                    K.op(act, lambda e, pt=pt: e.copy(out=CB["V"][:, :], in_=pt[:, 0:128]), r=[pk], w=["cb_V"])
                                    pt, pk = K.ps()
                                    K.op(pe, lambda e, pt=pt, Ec=Ec: e.matmul(pt[:, 0:128], lhsT=CB[Ec][:, :], rhs=CB["V"][:, :],
                                                                              start=True, stop=True), r=["cb_V", "cb_" + Ec], w=[pk])
                                    K.op(dve, lambda e, pt=pt, Dc=Dc, Dn=Dn: e.tensor_tensor(out=CB[Dn][:, :], in0=pt[:, 0:128], in1=CB[Dc][:, :],
                                                                                           op=ALU.add), r=[pk, "cb_" + Dc], w=["cb_" + Dn])
                                    Dc, Dn = Dn, Dc
                                Ec, En = En, Ec
                            Yf = Ec
                        pt, pk = K.ps()
                        K.op(pe, lambda e, pt=pt, cs_=cs_: e.matmul(pt[0:C, 0:128], lhsT=B["kdT"][:, cs_], rhs=identb[:, :],
                                                                    start=True, stop=True), r=["b_kdT", "identb"], w=[pk])
                        K.op(act, lambda e, pt=pt: e.copy(out=CB["kdec"][0:C, :], in_=pt[0:C, 0:128]), r=[pk], w=["cb_kdec"])
                        yield
                        sk, skb = ("Sdn", j), ("Sdnb", j)
                        pt, pk = K.ps()
                        K.op(pe, lambda e, pt=pt, cs_=cs_: e.matmul(pt[0:C, 0:128], lhsT=B["vb"][:, cs_], rhs=identb[:, :],
                                                                    start=True, stop=False), r=["b_vb", "identb"], w=[pk])
                        K.op(pe, lambda e, pt=pt, cs_=cs_: e.matmul(pt[0:C, 0:128], lhsT=B["kc"][:, cs_], rhs=Sdnb[:, j, :],
                                                                    start=False, stop=True), r=["b_kc", skb], w=[pk])
                        K.op(act, lambda e, pt=pt: e.copy(out=CB["R"][0:C, :], in_=pt[0:C, 0:128]), r=[pk], w=["cb_R"])
                        yield
                        pt, pk = K.ps()
                        K.op(pe, lambda e, pt=pt, Yf=Yf: e.matmul(pt[0:C, 0:128], lhsT=CB[Yf][0:C, 0:C], rhs=CB["R"][0:C, :],
                                                           start=True, stop=True), r=["cb_" + Yf, "cb_R"], w=[pk])
                        K.op(act, lambda e, pt=pt: e.copy(out=CB["u"][0:C, :], in_=pt[0:C, 0:128]), r=[pk], w=["cb_u"])
                        yield
                        pt, pk = K.ps()
                        K.op(pe, lambda e, pt=pt, cs_=cs_: e.matmul(pt[:, 0:C], lhsT=Sdnb[:, j, :], rhs=B["qd"][:, cs_],
                                                                    start=True, stop=False), r=[skb, "b_qd"], w=[pk])
                        K.op(pe, lambda e, pt=pt: e.matmul(pt[:, 0:C], lhsT=CB["u"][0:C, :], rhs=CB["qk"][0:C, 0:C],
                                                           start=False, stop=True), r=["cb_u", "cb_qk"], w=[pk])
                        K.op(act, lambda e, pt=pt, cs_=cs_: e.copy(out=F["oT"][:, cs_], in_=pt[:, 0:C]), r=[pk], w=["f_oT"])
                        yield
                        pt, pk = K.ps()
                        K.op(pe, lambda e, pt=pt: e.matmul(pt[:, 0:128], lhsT=CB["kdec"][0:C, :], rhs=CB["u"][0:C, :],
                                                           start=True, stop=True), r=["cb_kdec", "cb_u"], w=[pk])
                        K.op(act, lambda e, c0=c0: e.activation(out=dcol[:, 1:2], in_=F["d"][:, c0 + C - 1:c0 + C], func=AF.Exp),
                             r=["f_d"], w=["st3"])
                        K.op(dve, lambda e, pt=pt: e.scalar_tensor_tensor(
                            out=Sdn[:, j, :], in0=Sdn[:, j, :], scalar=dcol[:, 1:2], in1=pt[:, 0:128],
                            op0=ALU.mult, op1=ALU.add), r=[sk, "st3", pk], w=[sk])
                        K.op(pool, lambda e: e.tensor_copy(out=Sdnb[:, j, :], in_=Sdn[:, j, :]), r=[sk], w=[skb])
                        yield
                    rms_gate_out([(F["oT"][:, 0:T], "f_oT")], [m0 + 3], 128.0, dnc[:, 4:5], T, [dst_rows_col(j)], F, B)

                def ret_gen(T, C, pos0, dst_rows_col, F, B, CB, CB2):
                    nchunk = T // C
                    Pk = lambda m: ("P", id(P), m)
                    xi_t, ze_t, dmT, cdr = (xi128, ze128, dm128, cd128) if C == 128 else (xi16, ze16, dm16, cd16)
                    K.op(dve, lambda e: e.tensor_scalar(out=F["u"][:, 0:T], in0=iota[:, 0:T], scalar1=float(pos0),
                                                        scalar2=inv2pi, op0=ALU.add, op1=ALU.mult), r=["cr"], w=["f_u"])
                    for fn_, off in (("sin", 0.0), ("cos", 0.25)):
                        if off:
                            K.op(dve, lambda e: e.tensor_scalar(out=F["u"][:, 0:T], in0=F["u"][:, 0:T], scalar1=0.25,
                                                                scalar2=None, op0=ALU.add), r=["f_u"], w=["f_u"])
                        K.op(dve, lambda e: e.tensor_scalar(out=F["t1"][:, 0:T], in0=F["u"][:, 0:T], scalar1=MAGIC,
                                                            scalar2=None, op0=ALU.add), r=["f_u"], w=["f_t1"])
                        K.op(dve, lambda e: e.scalar_tensor_tensor(out=F["nf"][:, 0:T], in0=F["t1"][:, 0:T], scalar=MAGIC,
                                                                   in1=F["u"][:, 0:T], op0=ALU.subtract, op1=ALU.subtract),
                             r=["f_t1", "f_u"], w=["f_nf"])
                        K.op(act, lambda e, fn_=fn_: e.activation(out=F[fn_][:, 0:T], in_=F["nf"][:, 0:T], func=AF.Sin,
                                                                  scale=-6.28318), r=["f_nf"], w=["f_" + fn_])
                        yield
                    for (ce, co, o1, o2) in ((8, 9, "q1", "q2"), (10, 11, "k1", "k2")):
                        xe = P[:, ce, 3:3 + T]; xo = P[:, co, 3:3 + T]
                        rk_ = [Pk(ce), Pk(co), "f_sin", "f_cos"]
                        K.op(dve, lambda e, xe=xe: e.tensor_tensor(out=F["ta"][:, 0:T], in0=xe, in1=F["cos"][:, 0:T], op=ALU.mult),
                             r=rk_, w=["f_ta"])
                        K.op(pool, lambda e, xo=xo: e.tensor_tensor(out=F["tb"][:, 0:T], in0=xo, in1=F["sin"][:, 0:T], op=ALU.mult),
                             r=rk_, w=["f_tb"])
                        K.op(dve, lambda e, o1=o1: e.tensor_tensor(out=F[o1][:, 0:T], in0=F["ta"][:, 0:T], in1=F["tb"][:, 0:T],
                                                                   op=ALU.subtract), r=["f_ta", "f_tb"], w=["f_" + o1])
                        K.op(dve, lambda e, xo=xo: e.tensor_tensor(out=F["ta"][:, 0:T], in0=xo, in1=F["cos"][:, 0:T], op=ALU.mult),
                             r=rk_ + ["f_ta"], w=["f_ta"])
                        K.op(pool, lambda e, xe=xe: e.tensor_tensor(out=F["tb"][:, 0:T], in0=xe, in1=F["sin"][:, 0:T], op=ALU.mult),
                             r=rk_ + ["f_tb"], w=["f_tb"])
                        K.op(dve, lambda e, o2=o2: e.tensor_tensor(out=F[o2][:, 0:T], in0=F["ta"][:, 0:T], in1=F["tb"][:, 0:T],
                                                                   op=ALU.add), r=["f_ta", "f_tb"], w=["f_" + o2])
                        yield
                    for n_ in ("q1", "q2", "k1", "k2"):
                        K.op(pool, lambda e, n_=n_: e.tensor_copy(out=B[n_][:, 0:T], in_=F[n_][:, 0:T]), r=["f_" + n_], w=["b_" + n_])
                    for n_, s_ in (("qx1", "q1"), ("qx2", "q2")):
                        K.op(dve, lambda e, n_=n_, s_=s_: e.tensor_tensor(out=B[n_][:, 0:T], in0=F[s_][:, 0:T], in1=xi_t[:, 0:T],
                                                                          op=ALU.mult), r=["f_" + s_, "rc"], w=["b_" + n_])
                    for n_, s_ in (("kz1", "k1"), ("kz2", "k2")):
                        K.op(pool, lambda e, n_=n_, s_=s_: e.tensor_tensor(out=B[n_][:, 0:T], in0=F[s_][:, 0:T], in1=ze_t[:, 0:T],
                                                                           op=ALU.mult), r=["f_" + s_, "rc"], w=["b_" + n_])
                    for i_ in range(2):
                        K.op(pool, lambda e, i_=i_: e.tensor_copy(out=B["rv%d" % i_][:, 0:T], in_=P[:, 12 + i_, 3:3 + T]),
                             r=[Pk(12 + i_)], w=["b_rv%d" % i_])
                        yield
                    for c in range(nchunk):
                        c0 = c * C
                        cs_ = slice(c0, c0 + C)
                        pt, pk = K.ps()
                        for h_ in range(2):
                            K.op(pe, lambda e, pt=pt, h_=h_, cs_=cs_: e.matmul(
                                pt[0:C, 0:C], lhsT=B["k%d" % (h_ + 1)][:, cs_], rhs=B["q%d" % (h_ + 1)][:, cs_],
                                start=(h_ == 0), stop=(h_ == 1)), r=["b_k1", "b_k2", "b_q1", "b_q2"], w=[pk])
                        K.op(dve, lambda e, pt=pt: e.tensor_tensor(out=CB["rqk"][0:C, 0:C], in0=pt[0:C, 0:C], in1=dmT[0:C, 0:C],
                                                                   op=ALU.mult), r=[pk, "rc"], w=["cb_rqk"])
                        yield
                        for dst_, srcs_ in (("v", ("rv0", "rv1")), ("kz", ("kz1", "kz2"))):
                            for h_ in range(2):
                                pt, pk = K.ps()
                                K.op(pe, lambda e, pt=pt, h_=h_, cs_=cs_, srcs_=srcs_: e.matmul(
                                    pt[0:C, 0:128], lhsT=B[srcs_[h_]][:, cs_], rhs=identb[:, :],
                                    start=True, stop=True), r=["b_" + srcs_[h_], "identb"], w=[pk])
                                K.op(act, lambda e, pt=pt, dst_=dst_, h_=h_: e.copy(out=CB2[dst_][0:C, h_ * 128:(h_ + 1) * 128],
                                                                                  in_=pt[0:C, 0:128]), r=[pk], w=["cb2_" + dst_])
                                yield
                        for hv in range(2):
                            pt, pk = K.ps()
                            K.op(pe, lambda e, pt=pt, hv=hv: e.matmul(pt[:, 0:C], lhsT=CB2["v"][0:C, hv * 128:(hv + 1) * 128],
                                                                      rhs=CB["rqk"][0:C, 0:C], start=True, stop=False),
                                 r=["cb2_v", "cb_rqk"], w=[pk])
                            for kh in range(2):
                                K.op(pe, lambda e, pt=pt, hv=hv, kh=kh, cs_=cs_: e.matmul(
                                    pt[:, 0:C], lhsT=Srb[:, kh, hv * 128:(hv + 1) * 128], rhs=B["qx%d" % (kh + 1)][:, cs_],
                                    start=False, stop=(kh == 1)), r=["Srb", "b_qx1", "b_qx2"], w=[pk])
                            K.op(act, lambda e, pt=pt, hv=hv, cs_=cs_: e.copy(out=F["or%d" % hv][:, cs_], in_=pt[:, 0:C]),
                                 r=[pk], w=["f_or%d" % hv])
                            yield
                        for kh in range(2):
                            pt, pk = K.ps()
                            K.op(pe, lambda e, pt=pt, kh=kh: e.matmul(pt[:, 0:256], lhsT=CB2["kz"][0:C, kh * 128:(kh + 1) * 128],
                                                                      rhs=CB2["v"][0:C, :], start=True, stop=True),
                                 r=["cb2_kz", "cb2_v"], w=[pk])
                            K.op(dve, lambda e, pt=pt, kh=kh: e.scalar_tensor_tensor(
                                out=Sr[:, kh, :], in0=Sr[:, kh, :], scalar=cdr, in1=pt[:, 0:256], op0=ALU.mult, op1=ALU.add),
                                r=["Sr", "rc", pk], w=["Sr"])
                        K.op(pool, lambda e: e.tensor_copy(out=Srb[:, :, :], in_=Sr[:, :, :]), r=["Sr"], w=["Srb"])
                        yield
                    rms_gate_out([(F["or0"][:, 0:T], "f_or0"), (F["or1"][:, 0:T], "f_or1")], [14, 15], 256.0, None, T,
                                 [dst_rows_col(2), dst_rows_col(3)], F, B)


                def mixer_tile(T, C, pos0, dst_rows_col, extra=None):
                    gens = [("d0_", dn_gen(0, T, C, dst_rows_col, *DNS[0])), ("d1_", dn_gen(1, T, C, dst_rows_col, *DNS[1])),
                            ("r_", ret_gen(T, C, pos0, dst_rows_col, *RTS))]
                    lists = []
                    for ns_, g_ in gens:
                        K.ns = ns_
                        K.rec = []
                        for _ in g_:
                            pass
                        lists.append(K.rec)
                        K.rec = None
                    if extra is not None:
                        K.ns = "x_"
                        K.rec = []
                        extra()
                        lists.append(K.rec)
                        K.rec = None
                    K.ns = ""
                    K.replay(lists)

                conv_ch = (0, 1, 2, 4, 5, 6)
                nsup = S // T1
                per_sup = 592 // nsup + 1
                def prep(t, dst):
                    for sub in range(2):
                        norm_transpose(x1[t * T1 + sub * 128: t * T1 + (sub + 1) * 128, :], 128, sub * 128, xt[0], [(0, 128, 0)])
                    project(T1, dst, 3)

                prep(0, Pbufs[0])
                for t in range(nsup):
                    pump(per_sup)
                    P = Pbufs[t % 2]
                    extra_fn = None
                    if t + 1 < nsup:
                        def extra_fn(t=t, cur=Pbufs[t % 2], nxt=Pbufs[(t + 1) % 2]):
                            K.op(pool, lambda e: e.tensor_copy(out=nxt[:, :, 0:3], in_=cur[:, :, T1:T1 + 3]),
                                 r=[("P", id(cur), m) for m in range(NCH)], w=[("P", id(nxt), m) for m in range(NCH)])
                            prep(t + 1, nxt)
                    mixer_tile(T1, 128, t * T1, lambda jj, t=t: (ib[t * 512 + jj * 128: t * 512 + (jj + 1) * 128, :], ("ib", t)),
                               extra=extra_fn)
                    if not os.environ.get("SKIP_CC"):
                        K.allgather(ib[t * 512:(t + 1) * 512, :].opt(), ob[t * 2048:(t + 1) * 2048, :].opt(),
                                    [[0, 1, 2, 3], [4, 5, 6, 7]], r=[("ib", t)], w=[("ob", t)])
                Psm = Pbufs[0] if P is Pbufs[1] else Pbufs[1]
                for ci, m in enumerate(conv_ch):
                    K.dma(sp, convp[:, ci * 3:(ci + 1) * 3], P[:, m, T1:T1 + 3], r=[("P", id(P), m)])
                K.dma(sp, deltap.ap().rearrange("j k v -> k j v"), Sdn[:, :, :], r=[("Sdn", 0), ("Sdn", 1)])
                K.dma(sp, retp.ap().rearrange("(h k) v -> k h v", h=2), Sr[:, :, :], r=["Sr"])
                norm_transpose(xs1[:, :], 64, 0, xt[0], [(16 * i, 16 * (i + 1), 1 + i) for i in range(4)])
                project(64, Psm, 0)
                cstt = K.sb("cstt", [128, 6, 3], es=e1)
                for i in range(4):
                    K.dma(sp, cstt[:, :, :].rearrange("p a b -> p (a b)"), cst_in[i, :, :], w=["cstt"])
                    for m in range(NCH):
                        K.op(pool, lambda e, m=m, i=i: e.tensor_copy(out=P[:, m, 3:19], in_=Psm[:, m, 16 * i:16 * (i + 1)]),
                             r=[("P", id(Psm), m)], w=[("P", id(P), m)])
                    for ci, m in enumerate(conv_ch):
                        K.op(pool, lambda e, m=m, ci=ci: e.tensor_copy(out=P[:, m, 0:3], in_=cstt[:, ci, :]),
                             r=["cstt"], w=[("P", id(P), m)])
                    K.dma(sp, Sdn[:, :, :], sd_in[i].rearrange("j k v -> k j v"), w=[("Sdn", 0), ("Sdn", 1)])
                    K.dma(sp, Sr[:, :, :], sr_in[i].rearrange("(h k) v -> k h v", h=2), w=["Sr"])
                    for j in range(2):
                        K.op(pool, lambda e, j=j: e.tensor_copy(out=Sdnb[:, j, :], in_=Sdn[:, j, :]), r=[("Sdn", j)], w=[("Sdnb", j)])
                    K.op(pool, lambda e: e.tensor_copy(out=Srb[:, :, :], in_=Sr[:, :, :]), r=["Sr"], w=["Srb"])
                    mixer_tile(16, 16, PAST_LEN, lambda jj, i=i: (ibs[i * 512 + jj * 128: i * 512 + (jj + 1) * 128, :], "ibs"))
                    for ci, m in enumerate(conv_ch):
                        K.dma(sp, convs[i, :, ci * 3:(ci + 1) * 3], P[:, m, 16:19], r=[("P", id(P), m)])
                    K.dma(sp, deltas[i].rearrange("j k v -> k j v"), Sdn[:, :, :], r=[("Sdn", 0), ("Sdn", 1)])
                    K.dma(sp, rets[i].rearrange("(h k) v -> k h v", h=2), Sr[:, :, :], r=["Sr"])
            pump(10000)
            K.barrier()
            if not os.environ.get("SKIP_CC"):
                K.allgather(ibs.ap().opt(), obs.ap().opt(), [[0, 1, 2, 3], [4, 5, 6, 7]], r=["ibs"], w=["obs"])

            with contextlib.ExitStack() as e2:
                NSUB = 4
                TT = NSUB * 128
                gi = K.sb("gi", [128, 2 * KD], I32, es=e2)
                adaT1 = K.sb("adaT2", [128, 4, D], es=e2)
                K.dma(sp, adaT1[:, :, :].rearrange("p b c -> p (b c)"), adaT_d[:, 0:4 * D], w=["adaT"])
                K.dma(sp, gi[:, :], gidx_in[:, :], w=["gi"])
                xr = [K.sb(f"xr{i}", [128, D], es=e2) for i in range(NSUB)]
                xn2s = [K.sb(f"xn2_{i}", [128, D], BF16, es=e2) for i in range(2)]
                st2s = [K.sb(f"st2_{i}", [128, 4], es=e2) for i in range(2)]
                mT = K.sb("mT", [128, KD, TT], BF16, es=e2)
                aT = K.sb("aT", [128, NJ, TT], BF16, es=e2)
                gsb = K.sb("gsb", [128, TT], es=e2)
                tmp = K.sb("tmp2", [128, 512], es=e2)
                NSLOT = 3
                wsl = [K.sb(f"wsl{i}", [128, KD, 512], BF16, es=e2) for i in range(NSLOT)]
                slot_n = [0]

                def wload(src_ap, nk, keys):
                    i = slot_n[0] % NSLOT
                    slot_n[0] += 1
                    K.dma(sp, wsl[i][:, 0:nk, :], src_ap, r=keys, w=[("wsl", i)])
                    return wsl[i], ("wsl", i)

                tiles = []
                t0 = 0
                while t0 < SEG:
                    n = min(TT, SEG - t0)
                    tiles.append((t0, [128] * (n // 128), 0))
                    t0 += n
                tiles.append((SEG, [16], 1))
                for (tok0, subs, ri) in tiles:
                    if ri == 1:
                        K.dma(sp, adaT1[:, :, :].rearrange("p b c -> p (b c)"), adaT_d[:, 4 * D:8 * D], r=["adaT"], w=["adaT"])
                    nt = sum(subs)
                    offs = [sum(subs[:i]) for i in range(len(subs))]
                    for si, ns in enumerate(subs):
                        src = x2[tok0 + offs[si]: tok0 + offs[si] + ns, :] if ri == 0 else xs2[:, :]
                        K.dma(sp, xr[si][0:ns, :], src, w=[("xr", si)])
                    for k in range(KD):
                        if ri == 1:
                            K.dma(pool, mT[:, k, 0:16], obs[:, :], r=["obs", "gi"], w=[("mT", k)], indirect=gi[:, KD + k:KD + k + 1])
                            continue
                        for h in range(nt // T1):
                            tl = tok0 // T1 + h
                            K.dma(pool, mT[:, k, h * T1:(h + 1) * T1], ob[:, :], r=[("ob", t_) for t_ in range(NSUP)] + ["gi"],
                                  w=[("mT", k)], indirect=gi[:, k:k + 1], eoff=tl * 2048 * T1)
                    for n in range(4):
                        wt, wk = wload(wo_bf[n], KD, [("wo", n)])
                        for si, ns in enumerate(subs):
                            pt, pk = K.ps()
                            for k in range(KD):
                                K.op(pe, lambda e, pt=pt, k=k, si=si, ns=ns, wt=wt: e.matmul(
                                    pt[0:ns, :], lhsT=mT[:, k, offs[si]:offs[si] + ns], rhs=wt[:, k, :],
                                    start=(k == 0), stop=(k == KD - 1)), r=[("mT", k), wk], w=[pk])
                            K.op(dve, lambda e, pt=pt, ns=ns, n=n: e.tensor_tensor(
                                out=tmp[0:ns, :], in0=pt[0:ns, :], in1=adaT1[0:ns, 0, n * 512:(n + 1) * 512], op=ALU.mult),
                                r=[pk, "adaT"], w=["tmp2"])
                            K.op(pool, lambda e, si=si, ns=ns, n=n: e.tensor_tensor(
                                out=xr[si][0:ns, n * 512:(n + 1) * 512], in0=xr[si][0:ns, n * 512:(n + 1) * 512],
                                in1=tmp[0:ns, :], op=ALU.add), r=["tmp2", ("xr", si)], w=[("xr", si)])
                    row = 0 if ri == 0 else 5
                    for si, ns in enumerate(subs):
                        xk = ("xr", si)
                        xn2 = xn2s[si % 2]; sq2 = xn2; st2 = st2s[si % 2]
                        kx2 = ("xn2", si % 2); ks2 = ("st2", si % 2)
                        K.op(act, lambda e, si=si, ns=ns: e.activation(out=sq2[0:ns, :], in_=xr[si][0:ns, :], func=AF.Square,
                                                                       accum_out=st2[0:ns, 0:1]), r=[xk], w=[kx2, ks2])
                        K.op(act, lambda e, ns=ns: e.activation(out=st2[0:ns, 1:2], in_=st2[0:ns, 0:1], func=AF.Sqrt,
                                                                bias=kc[0:ns, 0:1], scale=1.0 / D), r=[ks2, "kc"], w=[ks2])
                        K.op(dve, lambda e, ns=ns: e.reciprocal(out=st2[0:ns, 2:3], in_=st2[0:ns, 1:2]), r=[ks2], w=[ks2])
                        K.op(dve, lambda e, si=si, ns=ns: e.tensor_scalar(out=xn2[0:ns, :], in0=xr[si][0:ns, :],
                                                                          scalar1=st2[0:ns, 2:3], scalar2=None, op0=ALU.mult),
                             r=[xk, ks2], w=[kx2])
                        for k in range(KD):
                            pt, pk = K.ps()
                            K.op(pe, lambda e, k=k, pt=pt, ns=ns: e.matmul(
                                pt[:, 0:ns], lhsT=xn2[0:ns, k * 128:(k + 1) * 128],
                                rhs=identb[0:ns, 0:ns], start=True, stop=True), r=[kx2, "identb"], w=[pk])
                            if k % 2 == 0:
                                K.op(dve, lambda e, k=k, pt=pt, si=si, ns=ns: e.tensor_scalar(
                                    out=mT[:, k, offs[si]:offs[si] + ns], in0=pt[:, 0:ns],
                                    scalar1=gmul[:, 1, k, row:row + 1], scalar2=adaF[:, 2, k, row:row + 1],
                                    op0=ALU.mult, op1=ALU.add), r=[pk, "gmul", "adaF"], w=[("mT", k)])
                            else:
                                K.op(act, lambda e, k=k, pt=pt, si=si, ns=ns: e.activation(
                                    out=mT[:, k, offs[si]:offs[si] + ns], in_=pt[:, 0:ns],
                                    func=AF.Identity, scale=gmul[:, 1, k, row:row + 1], bias=adaF[:, 2, k, row:row + 1]),
                                    r=[pk, "gmul", "adaF"], w=[("mT", k)])
                    mTk = [("mT", k) for k in range(KD)]
                    for jb in range(11):
                        wg, wgk = wload(wg_bf[jb, 0], KD, [("wg", jb, 0)])
                        wu, wuk = wload(wg_bf[jb, 1], KD, [("wg", jb, 1)])
                        for jj in range(4):
                            j = jb * 4 + jj
                            pg, pgk = K.ps()
                            for k in range(KD):
                                K.op(pe, lambda e, pg=pg, k=k, jj=jj, wg=wg: e.matmul(
                                    pg[:, 0:nt], lhsT=wg[:, k, jj * 128:(jj + 1) * 128], rhs=mT[:, k, 0:nt],
                                    start=(k == 0), stop=(k == KD - 1)), r=[wgk, ("mT", k)], w=[pgk])
                            pu, puk = K.ps()
                            for k in range(KD):
                                K.op(pe, lambda e, pu=pu, k=k, jj=jj, wu=wu: e.matmul(
                                    pu[:, 0:nt], lhsT=wu[:, k, jj * 128:(jj + 1) * 128], rhs=mT[:, k, 0:nt],
                                    start=(k == 0), stop=(k == KD - 1)), r=[wuk, ("mT", k)], w=[puk])
                            K.op(act, lambda e, pg=pg: e.activation(out=gsb[:, 0:nt], in_=pg[:, 0:nt], func=AF.Silu),
                                 r=[pgk], w=["gsb"])
                            K.op(dve, lambda e, pu=pu, j=j: e.tensor_tensor(out=aT[:, j, 0:nt], in0=gsb[:, 0:nt], in1=pu[:, 0:nt],
                                                                            op=ALU.mult), r=["gsb", puk], w=[("aT", j)])
                    for n in range(4):
                        pts = [K.ps() for _ in subs]
                        for jq in range(4):
                            wd, wdk = wload(wd_bf[n, jq], 11, [("wd", n, jq)])
                            for si, ns in enumerate(subs):
                                pt, pk = pts[si]
                                for jj in range(11):
                                    j = jq * 11 + jj
                                    K.op(pe, lambda e, pt=pt, j=j, jj=jj, si=si, ns=ns, wd=wd: e.matmul(
                                        pt[0:ns, :], lhsT=aT[:, j, offs[si]:offs[si] + ns], rhs=wd[:, jj, :],
                                        start=(j == 0), stop=(j == NJ - 1)), r=[("aT", j), wdk], w=[pk])
                        for si, ns in enumerate(subs):
                            pt, pk = pts[si]
                            K.op(dve, lambda e, pt=pt, ns=ns, n=n: e.tensor_tensor(
                                out=tmp[0:ns, :], in0=pt[0:ns, :], in1=adaT1[0:ns, 1, n * 512:(n + 1) * 512], op=ALU.mult),
                                r=[pk, "adaT"], w=["tmp2"])
                            K.op(pool, lambda e, si=si, ns=ns, n=n: e.tensor_tensor(
                                out=xr[si][0:ns, n * 512:(n + 1) * 512], in0=xr[si][0:ns, n * 512:(n + 1) * 512],
                                in1=tmp[0:ns, :], op=ALU.add), r=["tmp2", ("xr", si)], w=[("xr", si)])
                    for si, ns in enumerate(subs):
                        xk = ("xr", si)
                        xn2 = xn2s[si % 2]; sq2 = xn2; st2 = st2s[si % 2]
                        kx2 = ("xn2", si % 2); ks2 = ("st2", si % 2)
                        K.op(act, lambda e, si=si, ns=ns: e.activation(out=sq2[0:ns, :], in_=xr[si][0:ns, :], func=AF.Square,
                                                                       accum_out=st2[0:ns, 0:1]), r=[xk], w=[kx2, ks2])
                        K.op(act, lambda e, ns=ns: e.activation(out=st2[0:ns, 1:2], in_=st2[0:ns, 0:1], func=AF.Sqrt,
                                                                bias=kc[0:ns, 0:1], scale=1.0 / D), r=[ks2, "kc"], w=[ks2])
                        K.op(dve, lambda e, ns=ns: e.reciprocal(out=st2[0:ns, 2:3], in_=st2[0:ns, 1:2]), r=[ks2], w=[ks2])
                        K.op(dve, lambda e, si=si, ns=ns: e.scalar_tensor_tensor(
                            out=xr[si][0:ns, :], in0=xr[si][0:ns, :], scalar=st2[0:ns, 2:3], in1=adaT1[0:ns, 3, :],
                            op0=ALU.mult, op1=ALU.mult), r=[xk, ks2, "adaT"], w=[xk])
                        K.op(pool, lambda e, si=si, ns=ns: e.tensor_tensor(out=xr[si][0:ns, :], in0=xr[si][0:ns, :],
                                                                           in1=adaT1[0:ns, 2, :], op=ALU.add),
                             r=[xk, "adaT"], w=[xk])
                        K.dma(sp, y2[tok0 + offs[si]: tok0 + offs[si] + ns, :], xr[si][0:ns, :], r=[xk], w=["y2"])
            K.barrier()
    return nc


def _consts(r):
    j = np.arange(128)[:, None]; i = np.arange(128)[None, :]
    ident = (i == j).astype(np.float32)
    triI = (i >= j).astype(np.float32)
    triS = (i > j).astype(np.float32)
    ones = np.ones((128, 128), np.float32)
    sel = [(np.broadcast_to(j == 32 * g, (128, 128))).astype(np.float32) for g in range(4)]
    bm16 = ((i // 16) == (j // 16)).astype(np.float32)
    lv = []
    for s_ in (16, 32, 64):
        mk = (((j // (2 * s_)) == (i // (2 * s_))) & ((j % (2 * s_)) >= s_) & ((i % (2 * s_)) < s_)).astype(np.float32)
        lv += [mk, mk.T.copy()]
    cmat = np.concatenate([ident, triI, triS, ones] + sel + [bm16] + lv, axis=1)
    iota = np.broadcast_to(np.arange(256, dtype=np.float32)[None, :], (128, 256))
    rmask = np.broadcast_to((np.arange(256) % 128 != 0).astype(np.float32)[None, :], (128, 256))
    inv = (1.0 / (10000.0 ** np.linspace(0.0, 1.0, 128, dtype=np.float32))).astype(np.float32)
    inv2pi = (inv.astype(np.float64) / (2 * np.pi)).astype(np.float32)[:, None]
    crow = np.concatenate([iota, rmask, inv2pi], axis=1).astype(np.float32)
    lg = math.log(1.0 - 2.0 ** (-5.0 - r))

    def rcs(C, reps):
        idx = np.arange(C, dtype=np.float64)
        xi = np.exp((idx + 1.0) * lg); ze = np.exp((C - 1.0 - idx) * lg) / 16.0
        dm = np.zeros((128, 128))
        jj = np.arange(C)[:, None]; ii = np.arange(C)[None, :]
        dm[:C, :C] = np.where(ii >= jj, np.exp(np.where(ii >= jj, ii - jj, 0) * lg), 0.0) / 16.0
        return (np.broadcast_to(np.tile(xi, reps)[None, :], (128, C * reps)), np.broadcast_to(np.tile(ze, reps)[None, :], (128, C * reps)),
                dm, np.full((128, 1), math.exp(C * lg)))
    a = rcs(128, 2); b_ = rcs(16, 1)
    rc = np.concatenate(list(a) + list(b_), axis=1).astype(np.float32)
    assert rc.shape == (128, 802)
    return cmat, crow, rc


def _chan(r):
    out = []
    for jh in range(2):
        h = 2 * r + jh
        for base in (0, 1024, 2048):
            out.append(base + h * 128 + np.arange(128))
    return np.stack(out)


def _wcols(r):
    cols = []
    for jh in range(2):
        h = 2 * r + jh
        for base in (0, 1024, 2048, 3072):
            cols.append(base + h * 128 + np.arange(128))
    rq = 4112 + r * 256 + np.arange(256); rk = 5136 + r * 256 + np.arange(256)
    rv = 6160 + r * 256 + np.arange(256); rg = 7184 + r * 256 + np.arange(256)
    cols += [rq[0::2], rq[1::2], rk[0::2], rk[1::2], rv[:128], rv[128:], rg[:128], rg[128:]]
    small = np.concatenate([np.full(32, 4096 + 2 * r), np.full(32, 4096 + 2 * r + 1),
                            np.full(32, 4104 + 2 * r), np.full(32, 4104 + 2 * r + 1)])
    cols.append(small)
    return np.concatenate(cols)


_ROWPERM = np.concatenate([np.arange(0, 256, 2), np.arange(1, 256, 2)])


def make_in_maps(inp, S):
    SEG = S // 4
    f = lambda a: np.ascontiguousarray(a, dtype=np.float32)
    maps = []
    w_in = inp["w_in"][0]; w_out = inp["w_out"][0]
    shared = dict(
        w_gu=f(inp["w_gu"][0]), w_down=f(inp["w_down"][0]), w_ada=f(inp["w_ada"][0]),
        b_ada_f=f(inp["b_ada"][0].reshape(96, 128).T[:, :]), b_ada_r=f(inp["b_ada"][0][None, :]),
        w_adaf=f(inp["w_ada_final"]), b_adaf_r=f(inp["b_ada_final"][None, :]),
        nmix=f(inp["norm_mix"][0].reshape(16, 128).T), nffn=f(inp["norm_ffn"][0].reshape(16, 128).T),
        nfin=f(np.broadcast_to(inp["norm_final"][None, :], (128, D))),
    )
    for c in range(8):
        b, r = c // 4, c % 4
        cmat, crow, rc = _consts(r)
        ch = _chan(r)
        rows_w = np.concatenate([np.concatenate([np.arange(256 * q, 256 * q + 256), 1024 + np.arange(256 * q, 256 * q + 256)])
                                 for q in range(4)])
        crows = [inp["c_prompt"][b]] + [inp["c_sample"][4 * b + i] for i in range(4)] + [inp["c_sample"][4 * b + r]]
        m = dict(shared)
        m.update(
            x1=f(inp["x_prompt"][b]), xs1=f(inp["x_sample"][4 * b:4 * b + 4].reshape(64, D)),
            x2=f(inp["x_prompt"][b, r * SEG:(r + 1) * SEG]), xs2=f(inp["x_sample"][4 * b + r]),
            cT=f(np.stack(crows, axis=1)), w_in_o=f(w_in[:, _wcols(r)]), w_out_p=f(w_out[rows_w, :]),
            cw=f(inp["conv_w"][0][:, ch].transpose(2, 1, 0).reshape(128, 24)),
            cst=f(inp["state_conv"][0, 4 * b:4 * b + 4][:, :, ch].transpose(0, 3, 2, 1).reshape(4, 128, 18)),
            dnc=f(np.concatenate([np.broadcast_to(inp["dn_a_log"][0, 2 * r:2 * r + 2][None, :], (128, 2)),
                                  np.broadcast_to(inp["dn_dt_bias"][0, 2 * r:2 * r + 2][None, :], (128, 2)),
                                  inp["dn_norm"][0][:, None]], axis=1)),
            sd=f(inp["state_delta"][0, 4 * b:4 * b + 4, 2 * r:2 * r + 2]),
            sr=f(inp["state_ret"][0, 4 * b:4 * b + 4, r][:, _ROWPERM, :]),
            cmat=cmat, crow=crow, rc=rc,
            gidx=np.ascontiguousarray(np.concatenate([
                r * (SEG // 256) * 2048 + (np.arange(16)[None, :] // 4) * 512 + (np.arange(16)[None, :] % 4) * 128 + np.arange(128)[:, None],
                (np.arange(16)[None, :] // 4) * 2048 + r * 512 + (np.arange(16)[None, :] % 4) * 128 + np.arange(128)[:, None]], axis=1),
                dtype=np.int32),
        )
        maps.append(m)
    return maps


def assemble(res, S):
    SEG = S // 4
    yp = np.zeros((2, S, D), np.float32); ys = np.zeros((8, 16, D), np.float32)
    cp = np.zeros((1, 2, 3, 3072), np.float32); dp = np.zeros((1, 2, 8, 128, 128), np.float32)
    rp = np.zeros((1, 2, 4, 256, 256), np.float32)
    cs = np.zeros((1, 8, 3, 3072), np.float32); ds = np.zeros((1, 8, 8, 128, 128), np.float32)
    rs = np.zeros((1, 8, 4, 256, 256), np.float32)
    for c in range(8):
        b, r = c // 4, c % 4
        o = res[c]
        yp[b, r * SEG:(r + 1) * SEG] = o["y2"][:SEG]
        ys[4 * b + r] = o["y2"][SEG:]
        ch = _chan(r)
        cv = o["convp"].reshape(128, 6, 3)
        for ci in range(6):
            cp[0, b][:, ch[ci]] = cv[:, ci, :].T
        dp[0, b, 2 * r:2 * r + 2] = o["deltap"]
        rp[0, b, r][_ROWPERM] = o["retp"]
        for i in range(4):
            cv = o["convs"][i].reshape(128, 6, 3)
            for ci in range(6):
                cs[0, 4 * b + i][:, ch[ci]] = cv[:, ci, :].T
            ds[0, 4 * b + i, 2 * r:2 * r + 2] = o["deltas"][i]
            rs[0, 4 * b + i, r][_ROWPERM] = o["rets"][i]
    return yp, ys, cp, dp, rp, cs, ds, rs


_NC_CACHE = {}


def kernel(**inputs):
    inp = {k: np.asarray(v) for k, v in inputs.items()}
    S = inp["x_prompt"].shape[1]
    if S not in _NC_CACHE:
        _NC_CACHE[S] = build(S)
    nc = _NC_CACHE[S]
    maps = make_in_maps(inp, S)
    res = run_bass_kernel_spmd(nc, maps, core_ids=list(range(8)))
    return assemble(res.results, S)
```

```python
import math
import contextlib
import numpy as np
import concourse.bass as bass
import concourse.mybir as mybir
from concourse.bass_utils import run_bass_kernel_spmd

F32 = mybir.dt.float32
BF16 = mybir.dt.bfloat16
I32 = mybir.dt.int32
ALU = mybir.AluOpType
AF = mybir.ActivationFunctionType

D = 2048
KD = 16
DFF = 5632
NJ = 44
NCH = 17
PW = NCH * 128
EPS = 1e-6
PAST_LEN = 2048
MAGIC = 12582912.0
TWO_PI = 2.0 * math.pi


class Eng:
    def __init__(self, name, obj, sem, step):
        self.name, self.obj, self.sem, self.step = name, obj, sem, step
        self.cnt = 0
        self.waited = {}


class KB:
    def __init__(self, nc, es):
        self.nc, self.es = nc, es
        sem = lambda n: es.enter_context(nc.semaphore(n))
        self.pe = Eng("pe", nc.tensor, sem("s_pe"), 1)
        self.dve = Eng("dve", nc.vector, sem("s_dve"), 1)
        self.act = Eng("act", nc.scalar, sem("s_act"), 1)
        self.pool = Eng("pool", nc.gpsimd, sem("s_pool"), 1)
        self.sp = Eng("sp", nc.sync, sem("s_sp"), 1)
        self.engs = [self.pe, self.dve, self.act, self.pool, self.sp]
        self.dsem = {}
        for q in ("sp", "pool", "act"):
            self.dsem[q] = [Eng(f"d_{q}{i}", None, sem(f"s_d_{q}{i}"), 16) for i in range(8)]
        self.dnext = {"sp": 0, "pool": 0, "act": 0}
        self.cc = Eng("cc", None, sem("s_cc"), 1)
        self.lw = {}
        self.rd = {}
        self.psn = 0
        self.ns = ""
        self.alias = {}
        self.rec = None
        self.psub = {"d0_": [0, 1], "d1_": [2, 3], "r_": [4, 5], "x_": [6, 7]}
        self.psc = {}
        self.ps_t = [es.enter_context(nc.psum_tensor(f"ps{i}", [128, 512], F32)) for i in range(8)]

    def sb(self, name, shape, dt=F32, es=None):
        return (es or self.es).enter_context(self.nc.sbuf_tensor("sb_" + name, shape, dt))

    def ps(self):
        if self.ns in self.psub:
            sub = self.psub[self.ns]
            c = self.psc.get(self.ns, 0)
            self.psc[self.ns] = c + 1
            i = sub[c % len(sub)]
            return self.ps_t[i], ("ps", i)
        i = self.psn % 8
        self.psn += 1
        return self.ps_t[i], ("ps", i)

    def replay(self, lists):
        idx = [0] * len(lists)
        ns_save, self.ns = self.ns, ""
        while True:
            best, bf = -1, 2.0
            for i, l in enumerate(lists):
                if idx[i] < len(l):
                    f = idx[i] / len(l)
                    if f < bf:
                        best, bf = i, f
            if best < 0:
                break
            it = lists[best][idx[best]]
            idx[best] += 1
            if it[0] == "op":
                self.op(it[1], it[2], it[3], it[4])
            else:
                self.dma(it[1], it[2], it[3], it[4], it[5], it[6], it[7])
        self.ns = ns_save

    def _nk(self, ks):
        if not self.ns:
            return ks
        pf = ("f_", "b_", "c_", "cb_", "cb2_", "dcol", "st3")
        al = self.alias.get(self.ns, {})
        return [self.ns + al.get(k, k) if isinstance(k, str) and k.startswith(pf) else k for k in ks]

    def _deps(self, E, r, w, extra=()):
        deps = {}

        def need(p):
            if p is None:
                return
            e, s = p
            if e is self.pe and E is self.pe:
                return
            if deps.get(e, (None, 0))[1] < s:
                deps[e] = (e, s)
        for k in r:
            need(self.lw.get(k))
        for k in w:
            need(self.lw.get(k))
            for e, s in self.rd.get(k, {}).items():
                need((e, s))
        for p in extra:
            need(p)
        for e, s in deps.values():
            if E.waited.get(e.name, 0) < s:
                E.obj.wait_ge(e.sem, s)
                E.waited[e.name] = s

    def _mark(self, P, seq, r, w):
        for k in r:
            self.rd.setdefault(k, {})[P] = seq
        for k in w:
            self.lw[k] = (P, seq)
            self.rd[k] = {}

    def op(self, E, fn, r=(), w=()):
        r, w = self._nk(r), self._nk(w)
        if self.rec is not None:
            self.rec.append(("op", E, fn, r, w))
            return
        self._deps(E, r, w)
        ins = fn(E.obj)
        E.cnt += 1
        ins.then_inc(E.sem, 1)
        self._mark(E, E.cnt, r, w)

    def dma(self, Q, out, in_, r=(), w=(), indirect=None, eoff=0):
        r, w = self._nk(r), self._nk(w)
        if self.rec is not None:
            self.rec.append(("dma", Q, out, in_, r, w, indirect, eoff))
            return
        pool = self.dsem[Q.name]
        Dk = pool[self.dnext[Q.name] % len(pool)]
        self.dnext[Q.name] += 1
        extra = [(Dk, Dk.cnt * 16)] if Dk.cnt else []
        self._deps(Q, r, w, extra)
        if indirect is not None:
            ins = Q.obj.indirect_dma_start(out=out, out_offset=None, in_=in_,
                                           in_offset=bass.IndirectOffsetOnAxis(ap=indirect, axis=0), element_offset=eoff)
        else:
            ins = Q.obj.dma_start(out=out, in_=in_)
        ins.then_inc(Dk.sem, 16)
        Dk.cnt += 1
        self._mark(Dk, Dk.cnt * 16, r, w)

    def allgather(self, in_ap, out_ap, groups, r=(), w=()):
        Q = self.pool
        self._deps(Q, r, w)
        ins = Q.obj.collective_compute("AllGather", ALU.bypass, replica_groups=groups, ins=[in_ap], outs=[out_ap])
        ins.then_inc(self.cc.sem, 1)
        self.cc.cnt += 1
        self._mark(self.cc, self.cc.cnt, r, w)

    def barrier(self):
        allp = self.engs + [d for q in self.dsem.values() for d in q] + [self.cc]
        for E in self.engs:
            for P in allp:
                s = P.cnt * P.step
                if P is E or s == 0:
                    continue
                if E.waited.get(P.name, 0) < s:
                    E.obj.wait_ge(P.sem, s)
                    E.waited[P.name] = s
        self.lw.clear()
        self.rd.clear()


def build(S):
    SEG = S // 4
    SEGW = SEG + 16
    T1 = 256
    assert S % T1 == 0 and SEG % 128 == 0
    nc = bass.Bass("TRN2", target_bir_lowering=False)
    din = lambda n, sh, dt=F32: nc.dram_tensor(n, sh, dt, kind="ExternalInput")
    dout = lambda n, sh: nc.dram_tensor(n, sh, F32, kind="ExternalOutput")
    x1 = din("x1", [S, D]); xs1 = din("xs1", [64, D]); x2 = din("x2", [SEG, D]); xs2 = din("xs2", [16, D])
    cT = din("cT", [D, 6]); w_in_o = din("w_in_o", [D, PW]); w_out_p = din("w_out_p", [D, D])
    w_gu = din("w_gu", [D, 2 * DFF]); w_down = din("w_down", [DFF, D])
    w_ada = din("w_ada", [D, 6 * D]); b_ada_f = din("b_ada_f", [128, 96]); b_ada_r = din("b_ada_r", [1, 6 * D])
    w_adaf = din("w_adaf", [D, 2 * D]); b_adaf_r = din("b_adaf_r", [1, 2 * D])
    nmix = din("nmix", [128, KD]); nffn = din("nffn", [128, KD]); nfin = din("nfin", [128, D])
    cw_in = din("cw", [128, 24]); cst_in = din("cst", [4, 128, 18]); dnc_in = din("dnc", [128, 5])
    sd_in = din("sd", [4, 2, 128, 128]); sr_in = din("sr", [4, 256, 256])
    cmat = din("cmat", [128, 15 * 128]); crow = din("crow", [128, 512 + 1]); rc_in = din("rc", [128, 802])
    gidx_in = din("gidx", [128, 2 * KD], I32)
    y2 = dout("y2", [SEGW, D]); convp = dout("convp", [128, 18]); deltap = dout("deltap", [2, 128, 128])
    retp = dout("retp", [256, 256]); convs = dout("convs", [4, 128, 18]); deltas = dout("deltas", [4, 2, 128, 128])
    rets = dout("rets", [4, 256, 256])
    NSUP = S // T1
    ib = nc.dram_tensor("ib", [NSUP * 512, T1], BF16)
    ob = nc.dram_tensor("ob", [NSUP * 2048, T1], BF16)
    ibs = nc.dram_tensor("ibs", [4 * 512, 16], BF16)
    obs = nc.dram_tensor("obs", [4 * 2048, 16], BF16)
    wo_bf = nc.dram_tensor("wo_bf", [4, 128, KD, 512], BF16)
    wg_bf = nc.dram_tensor("wg_bf", [11, 2, 128, KD, 512], BF16)
    wd_bf = nc.dram_tensor("wd_bf", [4, 4, 128, 11, 512], BF16)
    adaT_d = nc.dram_tensor("adaT_d", [128, 8 * D], F32)

    with contextlib.ExitStack() as es, nc.Block() as block:
        @block.sync
        def _(_sync):
            K = KB(nc, es)
            pe, dve, act, pool, sp = K.pe, K.dve, K.act, K.pool, K.sp
            cm = K.sb("cm", [128, 15 * 128])
            cr = K.sb("cr", [128, 513])
            rc = K.sb("rc", [128, 802])
            kc = K.sb("kc", [128, 8])
            identb = K.sb("identb", [128, 128], BF16)
            K.dma(sp, cm[:, :], cmat[:, :], w=["cm"])
            K.dma(sp, cr[:, :], crow[:, :], w=["cr"])
            K.dma(sp, rc[:, :], rc_in[:, :], w=["rc"])
            K.op(dve, lambda e: e.memset(kc[:, 0:1], EPS), w=["kc"])
            K.op(dve, lambda e: e.memset(kc[:, 1:2], 1.0), w=["kc"])
            K.op(dve, lambda e: e.memset(kc[:, 2:3], 0.0), w=["kc"])
            ident = cm[:, 0:128]; triI = cm[:, 128:256]; triS = cm[:, 256:384]; ones = cm[:, 384:512]
            sel = [cm[:, 512 + 128 * g: 640 + 128 * g] for g in range(4)]
            bm16 = cm[:, 1024:1152]
            lvm = [(cm[:, 1152 + 256 * i: 1280 + 256 * i], cm[:, 1280 + 256 * i: 1408 + 256 * i]) for i in range(3)]
            K.op(dve, lambda e: e.tensor_copy(out=identb[:, :], in_=ident), r=["cm"], w=["identb"])
            iota = cr[:, 0:256]; rmask = cr[:, 256:512]; inv2pi = cr[:, 512:513]
            xi128 = rc[:, 0:256]; ze128 = rc[:, 256:512]; dm128 = rc[:, 512:640]; cd128 = rc[:, 640:641]
            xi16 = rc[:, 641:657]; ze16 = rc[:, 657:673]; dm16 = rc[:, 673:801]; cd16 = rc[:, 801:802]
            adaF = K.sb("adaF", [128, 4, KD, 6])
            gmul = K.sb("gmul", [128, 2, KD, 6])
            nm = K.sb("nm", [128, 2, KD])
            K.dma(sp, nm[:, 0, :], nmix[:, :], w=["nm"])
            K.dma(sp, nm[:, 1, :], nffn[:, :], w=["nm"])

            def cast_weights_gen():
                for n in range(4):
                    for k in range(KD):
                        K.dma(pool, wo_bf[n, :, k, :], w_out_p[k * 128:(k + 1) * 128, n * 512:(n + 1) * 512],
                              w=[("wo", n)])
                        yield
                for jb in range(11):
                    for gu in range(2):
                        for k in range(KD):
                            K.dma(pool, wg_bf[jb, gu, :, k, :],
                                  w_gu[k * 128:(k + 1) * 128, gu * DFF + jb * 512: gu * DFF + (jb + 1) * 512],
                                  w=[("wg", jb, gu)])
                            yield
                for n in range(4):
                    for jq in range(4):
                        for jj in range(11):
                            j = jq * 11 + jj
                            K.dma(pool, wd_bf[n, jq, :, jj, :], w_down[j * 128:(j + 1) * 128, n * 512:(n + 1) * 512],
                                  w=[("wd", n, jq)])
                            yield

            with contextlib.ExitStack() as e0:
                cs = K.sb("cs", [128, KD, 6], es=e0)
                csr = K.sb("csr", [128, 2, KD, 128], BF16, es=e0)
                csb = K.sb("csb", [128, KD, 6], BF16, es=e0)
                wblk = [K.sb(f"wblk{i}", [128, KD, 512], es=e0) for i in range(2)]
                wbf = [K.sb(f"wbf{i}", [128, KD, 512], BF16, es=e0) for i in range(2)]
                onesb = K.sb("onesb", [1, 128], BF16, es=e0)
                browb = [K.sb(f"browb{i}", [1, 512], BF16, es=e0) for i in range(2)]
                K.op(dve, lambda e: e.memset(onesb[:, :], 1.0), w=["onesb"])
                bfe = K.sb("bfe", [128, 96], es=e0)
                browt = [K.sb(f"brow{i}", [1, 512], es=e0) for i in range(2)]
                adaT = K.sb("adaT", [128, 2, 4, D], es=e0)
                K.dma(sp, cs[:, :, :], cT.ap().rearrange("(k p) r -> p k r", p=128), w=["cs"])
                K.dma(sp, bfe[:, :], b_ada_f[:, :], w=["bfe"])
                K.op(act, lambda e: e.activation(out=cs[:, :, :], in_=cs[:, :, :], func=AF.Silu), r=["cs"], w=["cs"])
                K.op(dve, lambda e: e.tensor_copy(out=csb[:, :, :], in_=cs[:, :, :]), r=["cs"], w=["csb"])
                for ri, row in enumerate((0, 5)):
                    for k in range(KD):
                        K.op(dve, lambda e, ri=ri, row=row, k=k: e.tensor_copy(
                            out=csr[:, ri, k, :], in_=cs[:, k, row:row + 1].to_broadcast([128, 128])),
                            r=["cs"], w=["csr"])
                fmap = {0: 0, 1: 1, 3: 2, 4: 3}
                tmap = {2: 0, 5: 1}
                for blk in range(32):
                    wt = wblk[blk % 2]; wk = ("wblk", blk % 2)
                    wb = wbf[blk % 2]; wbk = ("wbf", blk % 2)
                    if blk < 24:
                        src = w_ada[:, blk * 512:(blk + 1) * 512]
                    else:
                        src = w_adaf[:, (blk - 24) * 512:(blk - 23) * 512]
                    srcv = src.rearrange("(k p) n -> p k n", p=128)
                    K.dma(sp, wt[:, 0:8, :], srcv[:, 0:8, :], w=[wk + (0,)])
                    K.dma(sp, wt[:, 8:16, :], srcv[:, 8:16, :], w=[wk + (1,)])
                    split = blk // 4 if blk < 24 else 6 + (blk - 24) // 4
                    brow = browt[blk % 2]; bk = ("brow", blk % 2)
                    bsrc = b_ada_r[:, blk * 512:(blk + 1) * 512] if blk < 24 else b_adaf_r[:, (blk - 24) * 512:(blk - 23) * 512]
                    K.dma(sp, brow[:, :], bsrc, w=[bk])
                    bb = browb[blk % 2]; bbk = ("browb", blk % 2)
                    K.op(dve, lambda e, bb=bb, brow=brow: e.tensor_copy(out=bb[:, :], in_=brow[:, :]), r=[bk], w=[bbk])
                    K.op(dve, lambda e, wb=wb, wt=wt: e.tensor_copy(out=wb[:, 0:8, :], in_=wt[:, 0:8, :]), r=[wk + (0,)], w=[wbk + (0,)])
                    K.op(act, lambda e, wb=wb, wt=wt: e.copy(out=wb[:, 8:12, :], in_=wt[:, 8:12, :]), r=[wk + (1,)], w=[wbk + (1,)])
                    K.op(pool, lambda e, wb=wb, wt=wt: e.tensor_copy(out=wb[:, 12:16, :], in_=wt[:, 12:16, :]), r=[wk + (1,)], w=[wbk + (2,)])
                    wbkk = lambda k, wbk=wbk: wbk + ((0,) if k < 8 else (1,) if k < 12 else (2,))
                    q = blk % 4
                    if split in fmap:
                        for cc in range(4):
                            pt, pk = K.ps()
                            for k in range(KD):
                                K.op(pe, lambda e, k=k, cc=cc, pt=pt, wb=wb: e.matmul(
                                    pt[:, 0:6], lhsT=wb[:, k, cc * 128:(cc + 1) * 128], rhs=csb[:, k, :],
                                    start=(k == 0), stop=(k == KD - 1)), r=[wbkk(k), "csb"], w=[pk])
                            n = q * 4 + cc
                            col = split * 16 + n
                            K.op(act, lambda e, n=n, col=col, pt=pt, sl=fmap[split]: e.activation(
                                out=adaF[:, sl, n, :], in_=pt[:, 0:6], func=AF.Identity,
                                bias=bfe[:, col:col + 1], scale=1.0), r=[pk, "bfe"], w=["adaF"])
                    else:
                        slot = tmap[split] if split in tmap else (2 if split == 6 else 3)
                        for ri in range(2):
                            pt, pk = K.ps()
                            for k in range(KD):
                                K.op(pe, lambda e, k=k, ri=ri, pt=pt, wb=wb: e.matmul(
                                    pt[:, :], lhsT=csr[:, ri, k, :], rhs=wb[:, k, :],
                                    start=(k == 0), stop=False), r=[wbkk(k), "csr"], w=[pk])
                            K.op(pe, lambda e, pt=pt, bb=bb: e.matmul(
                                pt[:, :], lhsT=onesb[0:1, :], rhs=bb[0:1, :],
                                start=False, stop=True), r=[bbk, "onesb"], w=[pk])
                            K.op(dve if ri == 0 else act,
                                 (lambda e, pt=pt, ri=ri, slot=slot, q=q: e.tensor_copy(
                                     out=adaT[:, ri, slot, q * 512:(q + 1) * 512], in_=pt[:, :])) if ri == 0 else
                                 (lambda e, pt=pt, ri=ri, slot=slot, q=q: e.copy(
                                     out=adaT[:, ri, slot, q * 512:(q + 1) * 512], in_=pt[:, :])),
                                 r=[pk], w=["adaT"])
                for i, sl in enumerate((1, 3)):
                    K.op(dve, lambda e, i=i, sl=sl: e.tensor_scalar(
                        out=gmul[:, i, :, :], in0=adaF[:, sl, :, :], scalar1=1.0, scalar2=None, op0=ALU.add),
                        r=["adaF"], w=["gmul"])
                    K.op(dve, lambda e, i=i: e.tensor_tensor(
                        out=gmul[:, i, :, :], in0=gmul[:, i, :, :],
                        in1=nm[:, i, :].unsqueeze(2).to_broadcast([128, KD, 6]), op=ALU.mult),
                        r=["gmul", "nm"], w=["gmul"])
                nf = wblk[0]
                nfv = nf[:, 0:4, :].rearrange("p a b -> p (a b)")
                K.dma(sp, nfv, nfin[:, :], w=[("wblk", 0, 0), ("wblk", 0, 1)])
                for ri in range(2):
                    K.op(dve, lambda e, ri=ri: e.scalar_tensor_tensor(
                        out=adaT[:, ri, 3, :], in0=adaT[:, ri, 3, :], scalar=1.0, in1=nfv,
                        op0=ALU.add, op1=ALU.mult), r=["adaT", ("wblk", 0, 0)], w=["adaT"])
                K.dma(sp, adaT_d[:, :], adaT[:, :, :, :].rearrange("p a b c -> p (a b c)"), r=["adaT"], w=["adaT_d"])
            K.barrier()
            import os
            if os.environ.get("KSTOP") == "0":
                return
            cgen = cast_weights_gen()

            def pump(n):
                for _ in range(n):
                    if next(cgen, "done") == "done":
                        break
            if os.environ.get("KSTOP") == "0b":
                K.barrier()
                return

            with contextlib.ExitStack() as e1:
                w1 = K.sb("w1", [128, KD, PW], BF16, es=e1)
                if os.environ.get("W1_CASTDMA"):
                    for k in range(KD):
                        K.dma(pool, w1[:, k, :], w_in_o[k * 128:(k + 1) * 128, :], w=["w1"])
                cwt = K.sb("cwt", [128, 6, 4], es=e1)
                K.dma(sp, cwt[:, :, :].rearrange("p a b -> p (a b)"), cw_in[:, :], w=["cwt"])
                dnc = K.sb("dnc", [128, 5], es=e1)
                K.dma(sp, dnc[:, :], dnc_in[:, :], w=["dnc"])
                negA = K.sb("negA", [128, 2], es=e1)
                K.op(act, lambda e: e.activation(out=negA[:, :], in_=dnc[:, 0:2], func=AF.Exp), r=["dnc"], w=["negA"])
                K.op(dve, lambda e: e.tensor_scalar(out=negA[:, :], in0=negA[:, :], scalar1=-1.0, scalar2=None,
                                                    op0=ALU.mult), r=["negA"], w=["negA"])
                xt = [K.sb("xt0", [128, D], es=e1)]
                xn = K.sb("xn", [128, D], BF16, es=e1)
                sq_junk = xn
                st = K.sb("st", [128, 4], es=e1)
                hT = K.sb("hT", [128, KD, T1], BF16, es=e1)
                Pbufs = [K.sb(f"P{i}", [128, NCH, T1 + 3], es=e1) for i in range(2)]
                P = Pbufs[0]
                Sdn = K.sb("Sdn", [128, 2, 128], es=e1)
                Sdnb = K.sb("Sdnb", [128, 2, 128], BF16, es=e1)
                Sr = K.sb("Sr", [128, 2, 256], es=e1)
                Srb = K.sb("Srb", [128, 2, 256], BF16, es=e1)
                def mk(tag, fn, bn, cn, cbn, al=None):
                    al = al or {}
                    K.alias[tag + "_"] = {"f_" + a_: "f_" + b_ for a_, b_ in al.items()}
                    fd = {n: K.sb(tag + "f_" + n, [128, T1], es=e1) for n in fn if n not in al}
                    for a_, b_ in al.items():
                        fd[a_] = fd[b_]
                    return (fd,
                            {n: K.sb(tag + "b_" + n, [128, T1], BF16, es=e1) for n in bn},
                            {n: K.sb(tag + "c_" + n, [128, 128], es=e1) for n in cn},
                            {n: K.sb(tag + "cb_" + n, [128, 128], BF16, es=e1) for n in cbn})
                DNS = []
                for tag in ("d0", "d1"):
                    f_, b_, c_, cb_ = mk(tag, ("acc", "csq", "csk", "csv", "sq", "rr", "qn", "beta", "g", "d", "e", "kd", "eb", "oT", "zs", "ta"),
                                         ("q", "k", "kb", "kc", "kdT", "vb", "qd", "mx"), ("junk", "arg", "gam", "gamI", "gamS"),
                                         ("Pa", "Pb", "PTa", "PTb", "Ya", "Yb", "qk", "kdec", "R", "u", "N0", "PT0", "Dva", "Dvb", "W", "V", "N0m", "PTm"),
                                         al={"sq": "acc", "ta": "acc", "eb": "g", "zs": "csv", "qn": "csq"})
                    DNS.append((f_, b_, c_, cb_, K.sb(tag + "dcol", [128, 2], es=e1)))
                f_, b_, c_, cb_ = mk("r", ("u", "t1", "nf", "sin", "cos", "ta", "tb", "q1", "q2", "k1", "k2", "or0", "or1", "sq", "rr", "zs"),
                                     ("q1", "q2", "k1", "k2", "qx1", "qx2", "kz1", "kz2", "rv0", "rv1", "mx"), (), ("rqk",),
                                     al={"sq": "t1", "rr": "nf", "zs": "u"})
                RTS = (f_, b_, cb_, {n: K.sb("rcb2_" + n, [128, 256], BF16, es=e1) for n in ("v", "kz")})
                if not os.environ.get("W1_CASTDMA"):
                    for k in range(KD):
                        for hh in range(2):
                            stg = xt[0]; sk_ = ("xt", id(stg))
                            K.dma(sp, stg[:, 0:PW // 2], w_in_o[k * 128:(k + 1) * 128, hh * (PW // 2):(hh + 1) * (PW // 2)], w=[sk_])
                            K.op(dve if hh == 0 else pool, lambda e, k=k, hh=hh, stg=stg: e.tensor_copy(
                                out=w1[:, k, hh * (PW // 2):(hh + 1) * (PW // 2)], in_=stg[:, 0:PW // 2]), r=[sk_], w=["w1"])
                for pb_ in Pbufs:
                    K.op(dve, lambda e, pb_=pb_: e.memset(pb_[:, :, 0:3], 0.0), w=[("P", id(pb_), m) for m in range(NCH)])
                if os.environ.get("KSTOP") == "1a":
                    K.barrier()
                    return
                K.op(dve, lambda e: e.memset(Sdn[:, :, :], 0.0), w=["Sdn"])
                K.op(dve, lambda e: e.memset(Sdnb[:, :, :], 0.0), w=["Sdnb"])
                K.op(dve, lambda e: e.memset(Sr[:, :, :], 0.0), w=["Sr"])
                K.op(dve, lambda e: e.memset(Srb[:, :, :], 0.0), w=["Srb"])

                def norm_transpose(src_ap, ntok, toff, xbuf, scal):
                    xk = ("xt", id(xbuf))
                    K.dma(sp, xbuf[0:ntok, :], src_ap, w=[xk])
                    K.op(act, lambda e: e.activation(out=sq_junk[0:ntok, :], in_=xbuf[0:ntok, :], func=AF.Square,
                                                     accum_out=st[0:ntok, 0:1]), r=[xk], w=["xn", "st"])
                    K.op(act, lambda e: e.activation(out=st[0:ntok, 1:2], in_=st[0:ntok, 0:1], func=AF.Ln,
                                                     bias=kc[0:ntok, 0:1], scale=1.0 / D), r=["st", "kc"], w=["st"])
                    K.op(act, lambda e: e.activation(out=st[0:ntok, 2:3], in_=st[0:ntok, 1:2], func=AF.Exp, scale=-0.5),
                         r=["st"], w=["st"])
                    K.op(dve, lambda e: e.tensor_scalar(out=xn[0:ntok, :], in0=xbuf[0:ntok, :], scalar1=st[0:ntok, 2:3],
                                                        scalar2=None, op0=ALU.mult), r=[xk, "st"], w=["xn"])
                    for k in range(KD):
                        pt, pk = K.ps()
                        K.op(pe, lambda e, k=k, pt=pt: e.matmul(
                            pt[:, 0:ntok], lhsT=xn[0:ntok, k * 128:(k + 1) * 128],
                            rhs=identb[0:ntok, 0:ntok], start=True, stop=True), r=["xn", "identb"], w=[pk])
                        for (lo, hi, row) in scal:
                            E = dve if (k % 2 == 0) else act
                            if E is dve:
                                fn = lambda e, k=k, lo=lo, hi=hi, row=row, pt=pt: e.tensor_scalar(
                                    out=hT[:, k, toff + lo: toff + hi], in0=pt[:, lo:hi],
                                    scalar1=gmul[:, 0, k, row:row + 1], scalar2=adaF[:, 0, k, row:row + 1],
                                    op0=ALU.mult, op1=ALU.add)
                            else:
                                fn = lambda e, k=k, lo=lo, hi=hi, row=row, pt=pt: e.activation(
                                    out=hT[:, k, toff + lo: toff + hi], in_=pt[:, lo:hi],
                                    func=AF.Identity, scale=gmul[:, 0, k, row:row + 1],
                                    bias=adaF[:, 0, k, row:row + 1])
                            K.op(E, fn, r=[pk, "gmul", "adaF"], w=[("hT", k)])

                def project(T, dst, doff):
                    for m in range(NCH):
                        pt, pk = K.ps()
                        for k in range(KD):
                            K.op(pe, lambda e, k=k, m=m, pt=pt: e.matmul(
                                pt[:, 0:T], lhsT=w1[:, k, m * 128:(m + 1) * 128], rhs=hT[:, k, 0:T],
                                start=(k == 0), stop=(k == KD - 1)), r=["w1", ("hT", k)], w=[pk])
                        if m % 2 == 0:
                            K.op(dve, lambda e, m=m, pt=pt: e.tensor_copy(out=dst[:, m, doff:doff + T], in_=pt[:, 0:T]),
                                 r=[pk], w=[("P", id(dst), m)])
                        else:
                            K.op(act, lambda e, m=m, pt=pt: e.copy(out=dst[:, m, doff:doff + T], in_=pt[:, 0:T]),
                                 r=[pk], w=[("P", id(dst), m)])

                def rms_gate_out(srcs, gates, nfeat, wcol, T, dsts, F, B):
                    pt, pk = K.ps()
                    for i, (sa, skey) in enumerate(srcs):
                        K.op(dve, lambda e, sa=sa: e.tensor_tensor(out=F["sq"][:, 0:T], in0=sa, in1=sa, op=ALU.mult),
                             r=[skey], w=["f_sq"])
                        K.op(pe, lambda e, i=i, pt=pt: e.matmul(pt[:, 0:T], lhsT=ones, rhs=F["sq"][:, 0:T],
                                                                 start=(i == 0), stop=(i == len(srcs) - 1)),
                             r=["f_sq", "cm"], w=[pk])
                    K.op(act, lambda e, pt=pt: e.activation(out=F["rr"][:, 0:T], in_=pt[:, 0:T], func=AF.Ln,
                                                            bias=kc[:, 0:1], scale=1.0 / nfeat), r=[pk, "kc"], w=["f_rr"])
                    K.op(act, lambda e: e.activation(out=F["rr"][:, 0:T], in_=F["rr"][:, 0:T], func=AF.Exp, scale=-0.5),
                         r=["f_rr"], w=["f_rr"])
                    for i, (sa, skey) in enumerate(srcs):
                        K.op(act, lambda e, i=i: e.activation(out=F["zs"][:, 0:T], in_=P[:, gates[i], 3:3 + T], func=AF.Silu),
                             r=[("P", id(P), gates[i])], w=["f_zs"])
                        if wcol is not None:
                            K.op(dve, lambda e, sa=sa: e.scalar_tensor_tensor(
                                out=F["ta"][:, 0:T], in0=sa, scalar=wcol, in1=F["rr"][:, 0:T], op0=ALU.mult, op1=ALU.mult),
                                r=[skey, "f_rr", "dnc"], w=["f_ta"])
                        else:
                            K.op(dve, lambda e, sa=sa: e.tensor_tensor(out=F["ta"][:, 0:T], in0=sa, in1=F["rr"][:, 0:T],
                                                                       op=ALU.mult), r=[skey, "f_rr"], w=["f_ta"])
                        K.op(dve, lambda e: e.tensor_tensor(out=B["mx"][:, 0:T], in0=F["ta"][:, 0:T], in1=F["zs"][:, 0:T],
                                                            op=ALU.mult), r=["f_ta", "f_zs"], w=["b_mx"])
                        K.dma(sp, dsts[i][0], B["mx"][:, 0:T], r=["b_mx"], w=[dsts[i][1]])

                def dn_gen(j, T, C, dst_rows_col, F, B, C_, CB, dcol):
                    nchunk = T // C
                    L = int(math.log2(C))
                    Pk = lambda m: ("P", id(P), m)
                    m0 = 4 * j
                    for ci, nm_ in enumerate(("csq", "csk", "csv")):
                        m = m0 + ci
                        cwi = 3 * j + ci
                        K.op(dve, lambda e, m=m, cwi=cwi: e.tensor_scalar(
                            out=F["acc"][:, 0:T], in0=P[:, m, 0:T], scalar1=cwt[:, cwi, 0:1], scalar2=None,
                            op0=ALU.mult), r=[Pk(m), "cwt"], w=["f_acc"])
                        for i in range(1, 4):
                            K.op(dve, lambda e, m=m, cwi=cwi, i=i: e.scalar_tensor_tensor(
                                out=F["acc"][:, 0:T], in0=P[:, m, i:i + T], scalar=cwt[:, cwi, i:i + 1],
                                in1=F["acc"][:, 0:T], op0=ALU.mult, op1=ALU.add), r=[Pk(m), "cwt", "f_acc"], w=["f_acc"])
                        K.op(act, lambda e, nm_=nm_: e.activation(out=F[nm_][:, 0:T], in_=F["acc"][:, 0:T], func=AF.Silu),
                             r=["f_acc"], w=["f_" + nm_])
                        yield
                    for nm_ in ("csq", "csk"):
                        K.op(dve, lambda e, nm_=nm_: e.tensor_tensor(out=F["sq"][:, 0:T], in0=F[nm_][:, 0:T],
                                                                    in1=F[nm_][:, 0:T], op=ALU.mult),
                             r=["f_" + nm_], w=["f_sq"])
                        pt, pk = K.ps()
                        K.op(pe, lambda e, pt=pt: e.matmul(pt[:, 0:T], lhsT=ones, rhs=F["sq"][:, 0:T], start=True,
                                                           stop=True), r=["f_sq", "cm"], w=[pk])
                        K.op(act, lambda e, pt=pt: e.activation(out=F["rr"][:, 0:T], in_=pt[:, 0:T], func=AF.Ln,
                                                                bias=kc[:, 0:1], scale=1.0), r=[pk, "kc"], w=["f_rr"])
                        K.op(act, lambda e: e.activation(out=F["rr"][:, 0:T], in_=F["rr"][:, 0:T], func=AF.Exp, scale=-0.5),
                             r=["f_rr"], w=["f_rr"])
                        yield
                        if nm_ == "csq":
                            K.op(dve, lambda e: e.scalar_tensor_tensor(
                                out=F["qn"][:, 0:T], in0=F["csq"][:, 0:T], scalar=128.0 ** -0.5, in1=F["rr"][:, 0:T],
                                op0=ALU.mult, op1=ALU.mult), r=["f_csq", "f_rr"], w=["f_qn"])
                            K.op(pool, lambda e: e.tensor_copy(out=B["q"][:, 0:T], in_=F["qn"][:, 0:T]),
                                 r=["f_qn"], w=["b_q"])
                        else:
                            K.op(dve, lambda e: e.tensor_tensor(out=F["csk"][:, 0:T], in0=F["csk"][:, 0:T],
                                                                in1=F["rr"][:, 0:T], op=ALU.mult),
                                 r=["f_csk", "f_rr"], w=["f_csk"])
                            K.op(pool, lambda e: e.tensor_copy(out=B["k"][:, 0:T], in_=F["csk"][:, 0:T]),
                                 r=["f_csk"], w=["b_k"])
                    pt, pk = K.ps()
                    K.op(pe, lambda e, pt=pt: e.matmul(pt[:, 0:T], lhsT=sel[j], rhs=P[:, 16, 3:3 + T], start=True,
                                                       stop=True), r=[Pk(16), "cm"], w=[pk])
                    K.op(act, lambda e, pt=pt: e.activation(out=F["beta"][:, 0:T], in_=pt[:, 0:T], func=AF.Sigmoid),
                         r=[pk], w=["f_beta"])
                    pt, pk = K.ps()
                    K.op(pe, lambda e, pt=pt: e.matmul(pt[:, 0:T], lhsT=sel[2 + j], rhs=P[:, 16, 3:3 + T], start=True,
                                                       stop=True), r=[Pk(16), "cm"], w=[pk])
                    K.op(act, lambda e, pt=pt: e.activation(out=F["g"][:, 0:T], in_=pt[:, 0:T], func=AF.Exp,
                                                            bias=dnc[:, 2 + j:3 + j], scale=1.0), r=[pk, "dnc"], w=["f_g"])
                    K.op(act, lambda e: e.activation(out=F["g"][:, 0:T], in_=F["g"][:, 0:T], func=AF.Ln,
                                                     bias=kc[:, 1:2], scale=1.0), r=["f_g", "kc"], w=["f_g"])
                    K.op(dve, lambda e: e.tensor_scalar(out=F["g"][:, 0:T], in0=F["g"][:, 0:T], scalar1=negA[:, j:j + 1],
                                                        scalar2=None, op0=ALU.mult), r=["f_g", "negA"], w=["f_g"])
                    K.op(dve, lambda e: e.tensor_tensor_scan(out=F["d"][:, 0:T], data0=rmask[:, 0:T], data1=F["g"][:, 0:T],
                                                             initial=0.0, op0=ALU.mult, op1=ALU.add),
                         r=["f_g", "cr"], w=["f_d"])
                    K.op(act, lambda e: e.activation(out=F["e"][:, 0:T], in_=F["d"][:, 0:T], func=AF.Exp), r=["f_d"], w=["f_e"])
                    yield
                    for c in range(nchunk):
                        c0 = c * C
                        K.op(act, lambda e, c0=c0: e.activation(
                            out=F["kd"][:, c0:c0 + C], in_=F["d"][:, c0:c0 + C], func=AF.Exp, scale=-1.0,
                            bias=F["d"][:, c0 + C - 1:c0 + C]), r=["f_d"], w=["f_kd"])
                    K.op(dve, lambda e: e.tensor_tensor(out=F["eb"][:, 0:T], in0=F["e"][:, 0:T], in1=F["beta"][:, 0:T],
                                                        op=ALU.mult), r=["f_e", "f_beta"], w=["f_eb"])
                    K.op(pool, lambda e: e.tensor_tensor(out=B["kb"][:, 0:T], in0=F["csk"][:, 0:T], in1=F["beta"][:, 0:T],
                                                         op=ALU.mult), r=["f_csk", "f_beta"], w=["b_kb"])
                    K.op(dve, lambda e: e.scalar_tensor_tensor(out=B["kc"][:, 0:T], in0=F["csk"][:, 0:T], scalar=-1.0,
                                                               in1=F["eb"][:, 0:T], op0=ALU.mult, op1=ALU.mult),
                         r=["f_csk", "f_eb"], w=["b_kc"])
                    K.op(pool, lambda e: e.tensor_tensor(out=B["kdT"][:, 0:T], in0=F["csk"][:, 0:T], in1=F["kd"][:, 0:T],
                                                         op=ALU.mult), r=["f_csk", "f_kd"], w=["b_kdT"])
                    K.op(pool, lambda e: e.tensor_tensor(out=B["vb"][:, 0:T], in0=F["csv"][:, 0:T], in1=F["beta"][:, 0:T],
                                                         op=ALU.mult), r=["f_csv", "f_beta"], w=["b_vb"])
                    K.op(dve, lambda e: e.tensor_tensor(out=B["qd"][:, 0:T], in0=F["qn"][:, 0:T], in1=F["e"][:, 0:T],
                                                        op=ALU.mult), r=["f_qn", "f_e"], w=["b_qd"])
                    yield
                    for c in range(nchunk):
                        c0 = c * C
                        cs_ = slice(c0, c0 + C)
                        K.op(dve, lambda e, cs_=cs_: e.scalar_tensor_tensor(
                            out=C_["junk"][0:C, 0:C], in0=F["d"][0:C, cs_], scalar=1.0, in1=ident[0:C, 0:C],
                            op0=ALU.mult, op1=ALU.mult, accum_out=dcol[0:C, 0:1]), r=["f_d", "cm"], w=["c_junk", "dcol"])
                        K.op(dve, lambda e, cs_=cs_: e.tensor_scalar(
                            out=C_["arg"][0:C, 0:C], in0=F["d"][0:C, cs_], scalar1=dcol[0:C, 0:1], scalar2=0.0,
                            op0=ALU.subtract, op1=ALU.min), r=["f_d", "dcol"], w=["c_arg"])
                        K.op(act, lambda e: e.activation(out=C_["gam"][0:C, 0:C], in_=C_["arg"][0:C, 0:C], func=AF.Exp),
                             r=["c_arg"], w=["c_gam"])
                        K.op(pool, lambda e: e.tensor_tensor(out=C_["gamI"][0:C, 0:C], in0=C_["gam"][0:C, 0:C],
                                                             in1=triI[0:C, 0:C], op=ALU.mult), r=["c_gam", "cm"], w=["c_gamI"])
                        K.op(pool, lambda e: e.tensor_tensor(out=C_["gamS"][0:C, 0:C], in0=C_["gam"][0:C, 0:C],
                                                             in1=triS[0:C, 0:C], op=ALU.mult), r=["c_gam", "cm"], w=["c_gamS"])
                        pt, pk = K.ps()
                        K.op(pe, lambda e, pt=pt, cs_=cs_: e.matmul(pt[0:C, 0:C], lhsT=B["k"][:, cs_], rhs=B["kb"][:, cs_],
                                                                    start=True, stop=True), r=["b_k", "b_kb"], w=[pk])
                        K.op(dve, lambda e, pt=pt: e.scalar_tensor_tensor(
                            out=CB["N0" if C == 128 else "Pa"][0:C, 0:C], in0=pt[0:C, 0:C], scalar=-1.0, in1=C_["gamS"][0:C, 0:C],
                            op0=ALU.mult, op1=ALU.mult), r=[pk, "c_gamS"], w=["cb_N0" if C == 128 else "cb_Pa"])
                        if C == 128:
                            pt, pk = K.ps()
                            K.op(pe, lambda e, pt=pt: e.matmul(pt[:, 0:128], lhsT=CB["N0"][:, :], rhs=identb[:, :],
                                                               start=True, stop=True), r=["cb_N0", "identb"], w=[pk])
                            K.op(act, lambda e, pt=pt: e.copy(out=CB["PT0"][:, :], in_=pt[:, 0:128]), r=[pk], w=["cb_PT0"])
                            K.op(pool, lambda e: e.tensor_tensor(out=CB["Pa"][:, :], in0=CB["N0"][:, :], in1=bm16, op=ALU.mult),
                                 r=["cb_N0", "cm"], w=["cb_Pa"])
                        pt, pk = K.ps()
                        K.op(pe, lambda e, pt=pt, cs_=cs_: e.matmul(pt[0:C, 0:C], lhsT=B["k"][:, cs_], rhs=B["q"][:, cs_],
                                                                    start=True, stop=True), r=["b_k", "b_q"], w=[pk])
                        K.op(dve, lambda e, pt=pt: e.tensor_tensor(out=CB["qk"][0:C, 0:C], in0=pt[0:C, 0:C],
                                                                   in1=C_["gamI"][0:C, 0:C], op=ALU.mult),
                             r=[pk, "c_gamI"], w=["cb_qk"])
                        yield
                        pt, pk = K.ps()
                        K.op(pe, lambda e, pt=pt: e.matmul(pt[0:C, 0:C], lhsT=CB["Pa"][0:C, 0:C], rhs=identb[0:C, 0:C],
                                                           start=True, stop=True), r=["cb_Pa", "identb"], w=[pk])
                        K.op(act, lambda e, pt=pt: e.copy(out=CB["PTa"][0:C, 0:C], in_=pt[0:C, 0:C]), r=[pk], w=["cb_PTa"])
                        K.op(dve, lambda e: e.tensor_tensor(out=CB["Ya"][0:C, 0:C], in0=CB["Pa"][0:C, 0:C],
                                                            in1=ident[0:C, 0:C], op=ALU.add), r=["cb_Pa", "cm"], w=["cb_Ya"])
                        cur, nxt = "a", "b"
                        for l in range(1, min(L, 4)):
                            Pc, PTc, Yc = "P" + cur, "PT" + cur, "Y" + cur
                            Pn, PTn, Yn = "P" + nxt, "PT" + nxt, "Y" + nxt
                            pt, pk = K.ps()
                            K.op(pe, lambda e, pt=pt, Pc=Pc, PTc=PTc: e.matmul(
                                pt[0:C, 0:C], lhsT=CB[Pc][0:C, 0:C], rhs=CB[PTc][0:C, 0:C], start=True, stop=True),
                                r=["cb_" + Pc, "cb_" + PTc], w=[pk])
                            K.op(act, lambda e, pt=pt, PTn=PTn: e.copy(out=CB[PTn][0:C, 0:C], in_=pt[0:C, 0:C]),
                                 r=[pk], w=["cb_" + PTn])
                            if l < min(L, 4) - 1:
                                pt, pk = K.ps()
                                K.op(pe, lambda e, pt=pt, Pc=Pc, PTc=PTc: e.matmul(
                                    pt[0:C, 0:C], lhsT=CB[PTc][0:C, 0:C], rhs=CB[Pc][0:C, 0:C], start=True, stop=True),
                                    r=["cb_" + Pc, "cb_" + PTc], w=[pk])
                                K.op(act, lambda e, pt=pt, Pn=Pn: e.copy(out=CB[Pn][0:C, 0:C], in_=pt[0:C, 0:C]),
                                     r=[pk], w=["cb_" + Pn])
                            pt, pk = K.ps()
                            K.op(pe, lambda e, pt=pt, PTn=PTn, Yc=Yc: e.matmul(
                                pt[0:C, 0:C], lhsT=CB[PTn][0:C, 0:C], rhs=CB[Yc][0:C, 0:C], start=True, stop=True),
                                r=["cb_" + PTn, "cb_" + Yc], w=[pk])
                            K.op(dve, lambda e, pt=pt, Yc=Yc, Yn=Yn: e.tensor_tensor(
                                out=CB[Yn][0:C, 0:C], in0=pt[0:C, 0:C], in1=CB[Yc][0:C, 0:C], op=ALU.add),
                                r=[pk, "cb_" + Yc], w=["cb_" + Yn])
                            yield
                            cur, nxt = nxt, cur
                        Yf = "Y" + cur
                        if C == 128:
                            Ec, En = "Y" + cur, "Y" + nxt
                            Dc, Dn = "Dva", "Dvb"
                            pt, pk = K.ps()
                            K.op(pe, lambda e, pt=pt, Ec=Ec: e.matmul(pt[:, 0:128], lhsT=CB[Ec][:, :], rhs=identb[:, :],
                                                                      start=True, stop=True), r=["cb_" + Ec, "identb"], w=[pk])
                            K.op(act, lambda e, pt=pt, Dc=Dc: e.copy(out=CB[Dc][:, :], in_=pt[:, 0:128]), r=[pk], w=["cb_" + Dc])
                            yield
                            for li in range(3):
                                mk, mkT = lvm[li]
                                K.op(pool, lambda e, mk=mk: e.tensor_tensor(out=CB["PTm"][:, :], in0=CB["PT0"][:, :], in1=mk, op=ALU.mult),
                                     r=["cb_PT0", "cm"], w=["cb_PTm"])
                                pt, pk = K.ps()
                                K.op(pe, lambda e, pt=pt, Ec=Ec: e.matmul(pt[:, 0:128], lhsT=CB["PTm"][:, :], rhs=CB[Ec][:, :],
                                                                          start=True, stop=True), r=["cb_PTm", "cb_" + Ec], w=[pk])
                                K.op(act, lambda e, pt=pt: e.copy(out=CB["W"][:, :], in_=pt[:, 0:128]), r=[pk], w=["cb_W"])
                                pt, pk = K.ps()
                                K.op(pe, lambda e, pt=pt, Dc=Dc: e.matmul(pt[:, 0:128], lhsT=CB[Dc][:, :], rhs=CB["W"][:, :],
                                                                          start=True, stop=True), r=["cb_W", "cb_" + Dc], w=[pk])
                                K.op(dve, lambda e, pt=pt, Ec=Ec, En=En: e.tensor_tensor(out=CB[En][:, :], in0=pt[:, 0:128], in1=CB[Ec][:, :],
                                                                                       op=ALU.add), r=[pk, "cb_" + Ec], w=["cb_" + En])
                                yield
                                if li < 2:
                                    K.op(pool, lambda e, mkT=mkT: e.tensor_tensor(out=CB["N0m"][:, :], in0=CB["N0"][:, :], in1=mkT, op=ALU.mult),
                                         r=["cb_N0", "cm"], w=["cb_N0m"])
                                    pt, pk = K.ps()
                                    K.op(pe, lambda e, pt=pt, Dc=Dc: e.matmul(pt[:, 0:128], lhsT=CB["N0m"][:, :], rhs=CB[Dc][:, :],
                                                                              start=True, stop=True), r=["cb_N0m", "cb_" + Dc], w=[pk])
                                    K.op(act, lambda e, pt=pt: e.copy(out=CB["V"][:, :], in_=pt[:, 0:128]), r=[pk], w=["cb_V"])
                                    pt, pk = K.ps()
                                    K.op(pe, lambda e, pt=pt, Ec=Ec: e.matmul(pt[:, 0:128], lhsT=CB[Ec][:, :], rhs=CB["V"][:, :],
                                                                              start=True, stop=True), r=["cb_V", "cb_" + Ec], w=[pk])
                                    K.op(dve, lambda e, pt=pt, Dc=Dc, Dn=Dn: e.tensor_tensor(out=CB[Dn][:, :], in0=pt[:, 0:128], in1=CB[Dc][:, :],
                                                                                           op=ALU.add), r=[pk, "cb_" + Dc], w=["cb_" + Dn])
                                    Dc, Dn = Dn, Dc
                                Ec, En = En, Ec
                            Yf = Ec
                        pt, pk = K.ps()
                        K.op(pe, lambda e, pt=pt, cs_=cs_: e.matmul(pt[0:C, 0:128], lhsT=B["kdT"][:, cs_], rhs=identb[:, :],
                                                                    start=True, stop=True), r=["b_kdT", "identb"], w=[pk])
                        K.op(act, lambda e, pt=pt: e.copy(out=CB["kdec"][0:C, :], in_=pt[0:C, 0:128]), r=[pk], w=["cb_kdec"])
                        yield
                        sk, skb = ("Sdn", j), ("Sdnb", j)
                        pt, pk = K.ps()
                        K.op(pe, lambda e, pt=pt, cs_=cs_: e.matmul(pt[0:C, 0:128], lhsT=B["vb"][:, cs_], rhs=identb[:, :],
                                                                    start=True, stop=False), r=["b_vb", "identb"], w=[pk])
                        K.op(pe, lambda e, pt=pt, cs_=cs_: e.matmul(pt[0:C, 0:128], lhsT=B["kc"][:, cs_], rhs=Sdnb[:, j, :],
                                                                    start=False, stop=True), r=["b_kc", skb], w=[pk])
                        K.op(act, lambda e, pt=pt: e.copy(out=CB["R"][0:C, :], in_=pt[0:C, 0:128]), r=[pk], w=["cb_R"])
                        yield
                        pt, pk = K.ps()
                        K.op(pe, lambda e, pt=pt, Yf=Yf: e.matmul(pt[0:C, 0:128], lhsT=CB[Yf][0:C, 0:C], rhs=CB["R"][0:C, :],
                                                           start=True, stop=True), r=["cb_" + Yf, "cb_R"], w=[pk])
                        K.op(act, lambda e, pt=pt: e.copy(out=CB["u"][0:C, :], in_=pt[0:C, 0:128]), r=[pk], w=["cb_u"])
                        yield
                        pt, pk = K.ps()
                        K.op(pe, lambda e, pt=pt, cs_=cs_: e.matmul(pt[:, 0:C], lhsT=Sdnb[:, j, :], rhs=B["qd"][:, cs_],
                                                                    start=True, stop=False), r=[skb, "b_qd"], w=[pk])
                        K.op(pe, lambda e, pt=pt: e.matmul(pt[:, 0:C], lhsT=CB["u"][0:C, :], rhs=CB["qk"][0:C, 0:C],
                                                           start=False, stop=True), r=["cb_u", "cb_qk"], w=[pk])
                        K.op(act, lambda e, pt=pt, cs_=cs_: e.copy(out=F["oT"][:, cs_], in_=pt[:, 0:C]), r=[pk], w=["f_oT"])
                        yield
                        pt, pk = K.ps()
                        K.op(pe, lambda e, pt=pt: e.matmul(pt[:, 0:128], lhsT=CB["kdec"][0:C, :], rhs=CB["u"][0:C, :],
                                                           start=True, stop=True), r=["cb_kdec", "cb_u"], w=[pk])
                        K.op(act, lambda e, c0=c0: e.activation(out=dcol[:, 1:2], in_=F["d"][:, c0 + C - 1:c0 + C], func=AF.Exp),
                             r=["f_d"], w=["st3"])
                        K.op(dve, lambda e, pt=pt: e.scalar_tensor_tensor(
                            out=Sdn[:, j, :], in0=Sdn[:, j, :], scalar=dcol[:, 1:2], in1=pt[:, 0:128],
                            op0=ALU.mult, op1=ALU.add), r=[sk, "st3", pk], w=[sk])
                        K.op(pool, lambda e: e.tensor_copy(out=Sdnb[:, j, :], in_=Sdn[:, j, :]), r=[sk], w=[skb])
                        yield
                    rms_gate_out([(F["oT"][:, 0:T], "f_oT")], [m0 + 3], 128.0, dnc[:, 4:5], T, [dst_rows_col(j)], F, B)

                def ret_gen(T, C, pos0, dst_rows_col, F, B, CB, CB2):
                    nchunk = T // C
                    Pk = lambda m: ("P", id(P), m)
                    xi_t, ze_t, dmT, cdr = (xi128, ze128, dm128, cd128) if C == 128 else (xi16, ze16, dm16, cd16)
                    K.op(dve, lambda e: e.tensor_scalar(out=F["u"][:, 0:T], in0=iota[:, 0:T], scalar1=float(pos0),
                                                        scalar2=inv2pi, op0=ALU.add, op1=ALU.mult), r=["cr"], w=["f_u"])
                    for fn_, off in (("sin", 0.0), ("cos", 0.25)):
                        if off:
                            K.op(dve, lambda e: e.tensor_scalar(out=F["u"][:, 0:T], in0=F["u"][:, 0:T], scalar1=0.25,
                                                                scalar2=None, op0=ALU.add), r=["f_u"], w=["f_u"])
                        K.op(dve, lambda e: e.tensor_scalar(out=F["t1"][:, 0:T], in0=F["u"][:, 0:T], scalar1=MAGIC,
                                                            scalar2=None, op0=ALU.add), r=["f_u"], w=["f_t1"])
                        K.op(dve, lambda e: e.scalar_tensor_tensor(out=F["nf"][:, 0:T], in0=F["t1"][:, 0:T], scalar=MAGIC,
                                                                   in1=F["u"][:, 0:T], op0=ALU.subtract, op1=ALU.subtract),
                             r=["f_t1", "f_u"], w=["f_nf"])
                        K.op(act, lambda e, fn_=fn_: e.activation(out=F[fn_][:, 0:T], in_=F["nf"][:, 0:T], func=AF.Sin,
                                                                  scale=-6.28318), r=["f_nf"], w=["f_" + fn_])
                        yield
                    for (ce, co, o1, o2) in ((8, 9, "q1", "q2"), (10, 11, "k1", "k2")):
                        xe = P[:, ce, 3:3 + T]; xo = P[:, co, 3:3 + T]
                        rk_ = [Pk(ce), Pk(co), "f_sin", "f_cos"]
                        K.op(dve, lambda e, xe=xe: e.tensor_tensor(out=F["ta"][:, 0:T], in0=xe, in1=F["cos"][:, 0:T], op=ALU.mult),
                             r=rk_, w=["f_ta"])
                        K.op(pool, lambda e, xo=xo: e.tensor_tensor(out=F["tb"][:, 0:T], in0=xo, in1=F["sin"][:, 0:T], op=ALU.mult),
                             r=rk_, w=["f_tb"])
                        K.op(dve, lambda e, o1=o1: e.tensor_tensor(out=F[o1][:, 0:T], in0=F["ta"][:, 0:T], in1=F["tb"][:, 0:T],
                                                                   op=ALU.subtract), r=["f_ta", "f_tb"], w=["f_" + o1])
                        K.op(dve, lambda e, xo=xo: e.tensor_tensor(out=F["ta"][:, 0:T], in0=xo, in1=F["cos"][:, 0:T], op=ALU.mult),
                             r=rk_ + ["f_ta"], w=["f_ta"])
                        K.op(pool, lambda e, xe=xe: e.tensor_tensor(out=F["tb"][:, 0:T], in0=xe, in1=F["sin"][:, 0:T], op=ALU.mult),
                             r=rk_ + ["f_tb"], w=["f_tb"])
                        K.op(dve, lambda e, o2=o2: e.tensor_tensor(out=F[o2][:, 0:T], in0=F["ta"][:, 0:T], in1=F["tb"][:, 0:T],
                                                                   op=ALU.add), r=["f_ta", "f_tb"], w=["f_" + o2])
                        yield
                    for n_ in ("q1", "q2", "k1", "k2"):
                        K.op(pool, lambda e, n_=n_: e.tensor_copy(out=B[n_][:, 0:T], in_=F[n_][:, 0:T]), r=["f_" + n_], w=["b_" + n_])
                    for n_, s_ in (("qx1", "q1"), ("qx2", "q2")):
                        K.op(dve, lambda e, n_=n_, s_=s_: e.tensor_tensor(out=B[n_][:, 0:T], in0=F[s_][:, 0:T], in1=xi_t[:, 0:T],
                                                                          op=ALU.mult), r=["f_" + s_, "rc"], w=["b_" + n_])
                    for n_, s_ in (("kz1", "k1"), ("kz2", "k2")):
                        K.op(pool, lambda e, n_=n_, s_=s_: e.tensor_tensor(out=B[n_][:, 0:T], in0=F[s_][:, 0:T], in1=ze_t[:, 0:T],
                                                                           op=ALU.mult), r=["f_" + s_, "rc"], w=["b_" + n_])
                    for i_ in range(2):
                        K.op(pool, lambda e, i_=i_: e.tensor_copy(out=B["rv%d" % i_][:, 0:T], in_=P[:, 12 + i_, 3:3 + T]),
                             r=[Pk(12 + i_)], w=["b_rv%d" % i_])
                        yield
                    for c in range(nchunk):
                        c0 = c * C
                        cs_ = slice(c0, c0 + C)
                        pt, pk = K.ps()
                        for h_ in range(2):
                            K.op(pe, lambda e, pt=pt, h_=h_, cs_=cs_: e.matmul(
                                pt[0:C, 0:C], lhsT=B["k%d" % (h_ + 1)][:, cs_], rhs=B["q%d" % (h_ + 1)][:, cs_],
                                start=(h_ == 0), stop=(h_ == 1)), r=["b_k1", "b_k2", "b_q1", "b_q2"], w=[pk])
                        K.op(dve, lambda e, pt=pt: e.tensor_tensor(out=CB["rqk"][0:C, 0:C], in0=pt[0:C, 0:C], in1=dmT[0:C, 0:C],
                                                                   op=ALU.mult), r=[pk, "rc"], w=["cb_rqk"])
                        yield
                        for dst_, srcs_ in (("v", ("rv0", "rv1")), ("kz", ("kz1", "kz2"))):
                            for h_ in range(2):
                                pt, pk = K.ps()
                                K.op(pe, lambda e, pt=pt, h_=h_, cs_=cs_, srcs_=srcs_: e.matmul(
                                    pt[0:C, 0:128], lhsT=B[srcs_[h_]][:, cs_], rhs=identb[:, :],
                                    start=True, stop=True), r=["b_" + srcs_[h_], "identb"], w=[pk])
                                K.op(act, lambda e, pt=pt, dst_=dst_, h_=h_: e.copy(out=CB2[dst_][0:C, h_ * 128:(h_ + 1) * 128],
                                                                                  in_=pt[0:C, 0:128]), r=[pk], w=["cb2_" + dst_])
                                yield
                        for hv in range(2):
                            pt, pk = K.ps()
                            K.op(pe, lambda e, pt=pt, hv=hv: e.matmul(pt[:, 0:C], lhsT=CB2["v"][0:C, hv * 128:(hv + 1) * 128],
                                                                      rhs=CB["rqk"][0:C, 0:C], start=True, stop=False),
                                 r=["cb2_v", "cb_rqk"], w=[pk])
                            for kh in range(2):
                                K.op(pe, lambda e, pt=pt, hv=hv, kh=kh, cs_=cs_: e.matmul(
                                    pt[:, 0:C], lhsT=Srb[:, kh, hv * 128:(hv + 1) * 128], rhs=B["qx%d" % (kh + 1)][:, cs_],
                                    start=False, stop=(kh == 1)), r=["Srb", "b_qx1", "b_qx2"], w=[pk])
                            K.op(act, lambda e, pt=pt, hv=hv, cs_=cs_: e.copy(out=F["or%d" % hv][:, cs_], in_=pt[:, 0:C]),
                                 r=[pk], w=["f_or%d" % hv])
                            yield
                        for kh in range(2):
                            pt, pk = K.ps()
                            K.op(pe, lambda e, pt=pt, kh=kh: e.matmul(pt[:, 0:256], lhsT=CB2["kz"][0:C, kh * 128:(kh + 1) * 128],
                                                                      rhs=CB2["v"][0:C, :], start=True, stop=True),
                                 r=["cb2_kz", "cb2_v"], w=[pk])
                            K.op(dve, lambda e, pt=pt, kh=kh: e.scalar_tensor_tensor(
                                out=Sr[:, kh, :], in0=Sr[:, kh, :], scalar=cdr, in1=pt[:, 0:256], op0=ALU.mult, op1=ALU.add),
                                r=["Sr", "rc", pk], w=["Sr"])
                        K.op(pool, lambda e: e.tensor_copy(out=Srb[:, :, :], in_=Sr[:, :, :]), r=["Sr"], w=["Srb"])
                        yield
                    rms_gate_out([(F["or0"][:, 0:T], "f_or0"), (F["or1"][:, 0:T], "f_or1")], [14, 15], 256.0, None, T,
                                 [dst_rows_col(2), dst_rows_col(3)], F, B)


                def mixer_tile(T, C, pos0, dst_rows_col, extra=None, pump_n=0):
                    gens = [("d0_", dn_gen(0, T, C, dst_rows_col, *DNS[0])), ("d1_", dn_gen(1, T, C, dst_rows_col, *DNS[1])),
                            ("r_", ret_gen(T, C, pos0, dst_rows_col, *RTS))]
                    lists = []
                    for ns_, g_ in gens:
                        K.ns = ns_
                        K.rec = []
                        for _ in g_:
                            pass
                        lists.append(K.rec)
                        K.rec = None
                    if extra is not None:
                        K.ns = "x_"
                        K.rec = []
                        extra()
                        lists.append(K.rec)
                        K.rec = None
                    if pump_n:
                        K.ns = ""
                        K.rec = []
                        pump(pump_n)
                        lists.append(K.rec)
                        K.rec = None
                    K.ns = ""
                    K.replay(lists)

                conv_ch = (0, 1, 2, 4, 5, 6)
                nsup = S // T1
                per_sup = 592 // nsup + 1
                def prep(t, dst):
                    for sub in range(2):
                        norm_transpose(x1[t * T1 + sub * 128: t * T1 + (sub + 1) * 128, :], 128, sub * 128, xt[0], [(0, 128, 0)])
                    project(T1, dst, 3)

                prep(0, Pbufs[0])
                for t in range(nsup):
                    P = Pbufs[t % 2]
                    extra_fn = None
                    if t + 1 < nsup:
                        def extra_fn(t=t, cur=Pbufs[t % 2], nxt=Pbufs[(t + 1) % 2]):
                            K.op(pool, lambda e: e.tensor_copy(out=nxt[:, :, 0:3], in_=cur[:, :, T1:T1 + 3]),
                                 r=[("P", id(cur), m) for m in range(NCH)], w=[("P", id(nxt), m) for m in range(NCH)])
                            prep(t + 1, nxt)
                    mixer_tile(T1, 128, t * T1, lambda jj, t=t: (ib[t * 512 + jj * 128: t * 512 + (jj + 1) * 128, :], ("ib", t)),
                               extra=extra_fn, pump_n=per_sup)
                    if not os.environ.get("SKIP_CC"):
                        K.allgather(ib[t * 512:(t + 1) * 512, :].opt(), ob[t * 2048:(t + 1) * 2048, :].opt(),
                                    [[0, 1, 2, 3], [4, 5, 6, 7]], r=[("ib", t)], w=[("ob", t)])
                Psm = Pbufs[0] if P is Pbufs[1] else Pbufs[1]
                for ci, m in enumerate(conv_ch):
                    K.dma(sp, convp[:, ci * 3:(ci + 1) * 3], P[:, m, T1:T1 + 3], r=[("P", id(P), m)])
                K.dma(sp, deltap.ap().rearrange("j k v -> k j v"), Sdn[:, :, :], r=[("Sdn", 0), ("Sdn", 1)])
                K.dma(sp, retp.ap().rearrange("(h k) v -> k h v", h=2), Sr[:, :, :], r=["Sr"])
                norm_transpose(xs1[:, :], 64, 0, xt[0], [(16 * i, 16 * (i + 1), 1 + i) for i in range(4)])
                project(64, Psm, 0)
                cstt = K.sb("cstt", [128, 6, 3], es=e1)
                for i in range(4):
                    K.dma(sp, cstt[:, :, :].rearrange("p a b -> p (a b)"), cst_in[i, :, :], w=["cstt"])
                    for m in range(NCH):
                        K.op(pool, lambda e, m=m, i=i: e.tensor_copy(out=P[:, m, 3:19], in_=Psm[:, m, 16 * i:16 * (i + 1)]),
                             r=[("P", id(Psm), m)], w=[("P", id(P), m)])
                    for ci, m in enumerate(conv_ch):
                        K.op(pool, lambda e, m=m, ci=ci: e.tensor_copy(out=P[:, m, 0:3], in_=cstt[:, ci, :]),
                             r=["cstt"], w=[("P", id(P), m)])
                    K.dma(sp, Sdn[:, :, :], sd_in[i].rearrange("j k v -> k j v"), w=[("Sdn", 0), ("Sdn", 1)])
                    K.dma(sp, Sr[:, :, :], sr_in[i].rearrange("(h k) v -> k h v", h=2), w=["Sr"])
                    for j in range(2):
                        K.op(pool, lambda e, j=j: e.tensor_copy(out=Sdnb[:, j, :], in_=Sdn[:, j, :]), r=[("Sdn", j)], w=[("Sdnb", j)])
                    K.op(pool, lambda e: e.tensor_copy(out=Srb[:, :, :], in_=Sr[:, :, :]), r=["Sr"], w=["Srb"])
                    mixer_tile(16, 16, PAST_LEN, lambda jj, i=i: (ibs[i * 512 + jj * 128: i * 512 + (jj + 1) * 128, :], "ibs"))
                    for ci, m in enumerate(conv_ch):
                        K.dma(sp, convs[i, :, ci * 3:(ci + 1) * 3], P[:, m, 16:19], r=[("P", id(P), m)])
                    K.dma(sp, deltas[i].rearrange("j k v -> k j v"), Sdn[:, :, :], r=[("Sdn", 0), ("Sdn", 1)])
                    K.dma(sp, rets[i].rearrange("(h k) v -> k h v", h=2), Sr[:, :, :], r=["Sr"])
            pump(10000)
            K.barrier()
            if not os.environ.get("SKIP_CC"):
                K.allgather(ibs.ap().opt(), obs.ap().opt(), [[0, 1, 2, 3], [4, 5, 6, 7]], r=["ibs"], w=["obs"])

            with contextlib.ExitStack() as e2:
                NSUB = 4
                TT = NSUB * 128
                gi = K.sb("gi", [128, 2 * KD], I32, es=e2)
                adaT1 = K.sb("adaT2", [128, 4, D], es=e2)
                K.dma(sp, adaT1[:, :, :].rearrange("p b c -> p (b c)"), adaT_d[:, 0:4 * D], w=["adaT"])
                K.dma(sp, gi[:, :], gidx_in[:, :], w=["gi"])
                xr = [K.sb(f"xr{i}", [128, D], es=e2) for i in range(NSUB)]
                xn2s = [K.sb(f"xn2_{i}", [128, D], BF16, es=e2) for i in range(2)]
                st2s = [K.sb(f"st2_{i}", [128, 4], es=e2) for i in range(2)]
                mT = K.sb("mT", [128, KD, TT], BF16, es=e2)
                aT = K.sb("aT", [128, NJ, TT], BF16, es=e2)
                gsb = K.sb("gsb", [128, TT], es=e2)
                tmp = K.sb("tmp2", [128, 512], es=e2)
                NSLOT = 3
                wsl = [K.sb(f"wsl{i}", [128, KD, 512], BF16, es=e2) for i in range(NSLOT)]
                slot_n = [0]

                def wload(src_ap, nk, keys):
                    i = slot_n[0] % NSLOT
                    slot_n[0] += 1
                    K.dma(sp, wsl[i][:, 0:nk, :], src_ap, r=keys, w=[("wsl", i)])
                    return wsl[i], ("wsl", i)

                tiles = []
                t0 = 0
                while t0 < SEG:
                    n = min(TT, SEG - t0)
                    tiles.append((t0, [128] * (n // 128), 0))
                    t0 += n
                tiles.append((SEG, [16], 1))
                for (tok0, subs, ri) in tiles:
                    if ri == 1:
                        K.dma(sp, adaT1[:, :, :].rearrange("p b c -> p (b c)"), adaT_d[:, 4 * D:8 * D], r=["adaT"], w=["adaT"])
                    nt = sum(subs)
                    offs = [sum(subs[:i]) for i in range(len(subs))]
                    for si, ns in enumerate(subs):
                        src = x2[tok0 + offs[si]: tok0 + offs[si] + ns, :] if ri == 0 else xs2[:, :]
                        K.dma(sp, xr[si][0:ns, :], src, w=[("xr", si)])
                    for k in range(KD):
                        if ri == 1:
                            K.dma(pool, mT[:, k, 0:16], obs[:, :], r=["obs", "gi"], w=[("mT", k)], indirect=gi[:, KD + k:KD + k + 1])
                            continue
                        for h in range(nt // T1):
                            tl = tok0 // T1 + h
                            K.dma(pool, mT[:, k, h * T1:(h + 1) * T1], ob[:, :], r=[("ob", t_) for t_ in range(NSUP)] + ["gi"],
                                  w=[("mT", k)], indirect=gi[:, k:k + 1], eoff=tl * 2048 * T1)
                    for n in range(4):
                        wt, wk = wload(wo_bf[n], KD, [("wo", n)])
                        for si, ns in enumerate(subs):
                            pt, pk = K.ps()
                            for k in range(KD):
                                K.op(pe, lambda e, pt=pt, k=k, si=si, ns=ns, wt=wt: e.matmul(
                                    pt[0:ns, :], lhsT=mT[:, k, offs[si]:offs[si] + ns], rhs=wt[:, k, :],
                                    start=(k == 0), stop=(k == KD - 1)), r=[("mT", k), wk], w=[pk])
                            K.op(dve, lambda e, pt=pt, ns=ns, n=n: e.tensor_tensor(
                                out=tmp[0:ns, :], in0=pt[0:ns, :], in1=adaT1[0:ns, 0, n * 512:(n + 1) * 512], op=ALU.mult),
                                r=[pk, "adaT"], w=["tmp2"])
                            K.op(pool, lambda e, si=si, ns=ns, n=n: e.tensor_tensor(
                                out=xr[si][0:ns, n * 512:(n + 1) * 512], in0=xr[si][0:ns, n * 512:(n + 1) * 512],
                                in1=tmp[0:ns, :], op=ALU.add), r=["tmp2", ("xr", si)], w=[("xr", si)])
                    row = 0 if ri == 0 else 5
                    for si, ns in enumerate(subs):
                        xk = ("xr", si)
                        xn2 = xn2s[si % 2]; sq2 = xn2; st2 = st2s[si % 2]
                        kx2 = ("xn2", si % 2); ks2 = ("st2", si % 2)
                        K.op(act, lambda e, si=si, ns=ns: e.activation(out=sq2[0:ns, :], in_=xr[si][0:ns, :], func=AF.Square,
                                                                       accum_out=st2[0:ns, 0:1]), r=[xk], w=[kx2, ks2])
                        K.op(act, lambda e, ns=ns: e.activation(out=st2[0:ns, 1:2], in_=st2[0:ns, 0:1], func=AF.Sqrt,
                                                                bias=kc[0:ns, 0:1], scale=1.0 / D), r=[ks2, "kc"], w=[ks2])
                        K.op(dve, lambda e, ns=ns: e.reciprocal(out=st2[0:ns, 2:3], in_=st2[0:ns, 1:2]), r=[ks2], w=[ks2])
                        K.op(dve, lambda e, si=si, ns=ns: e.tensor_scalar(out=xn2[0:ns, :], in0=xr[si][0:ns, :],
                                                                          scalar1=st2[0:ns, 2:3], scalar2=None, op0=ALU.mult),
                             r=[xk, ks2], w=[kx2])
                        for k in range(KD):
                            pt, pk = K.ps()
                            K.op(pe, lambda e, k=k, pt=pt, ns=ns: e.matmul(
                                pt[:, 0:ns], lhsT=xn2[0:ns, k * 128:(k + 1) * 128],
                                rhs=identb[0:ns, 0:ns], start=True, stop=True), r=[kx2, "identb"], w=[pk])
                            if k % 2 == 0:
                                K.op(dve, lambda e, k=k, pt=pt, si=si, ns=ns: e.tensor_scalar(
                                    out=mT[:, k, offs[si]:offs[si] + ns], in0=pt[:, 0:ns],
                                    scalar1=gmul[:, 1, k, row:row + 1], scalar2=adaF[:, 2, k, row:row + 1],
                                    op0=ALU.mult, op1=ALU.add), r=[pk, "gmul", "adaF"], w=[("mT", k)])
                            else:
                                K.op(act, lambda e, k=k, pt=pt, si=si, ns=ns: e.activation(
                                    out=mT[:, k, offs[si]:offs[si] + ns], in_=pt[:, 0:ns],
                                    func=AF.Identity, scale=gmul[:, 1, k, row:row + 1], bias=adaF[:, 2, k, row:row + 1]),
                                    r=[pk, "gmul", "adaF"], w=[("mT", k)])
                    mTk = [("mT", k) for k in range(KD)]
                    for jb in range(11):
                        wg, wgk = wload(wg_bf[jb, 0], KD, [("wg", jb, 0)])
                        wu, wuk = wload(wg_bf[jb, 1], KD, [("wg", jb, 1)])
                        for jj in range(4):
                            j = jb * 4 + jj
                            pg, pgk = K.ps()
                            for k in range(KD):
                                K.op(pe, lambda e, pg=pg, k=k, jj=jj, wg=wg: e.matmul(
                                    pg[:, 0:nt], lhsT=wg[:, k, jj * 128:(jj + 1) * 128], rhs=mT[:, k, 0:nt],
                                    start=(k == 0), stop=(k == KD - 1)), r=[wgk, ("mT", k)], w=[pgk])
                            pu, puk = K.ps()
                            for k in range(KD):
                                K.op(pe, lambda e, pu=pu, k=k, jj=jj, wu=wu: e.matmul(
                                    pu[:, 0:nt], lhsT=wu[:, k, jj * 128:(jj + 1) * 128], rhs=mT[:, k, 0:nt],
                                    start=(k == 0), stop=(k == KD - 1)), r=[wuk, ("mT", k)], w=[puk])
                            K.op(act, lambda e, pg=pg: e.activation(out=gsb[:, 0:nt], in_=pg[:, 0:nt], func=AF.Silu),
                                 r=[pgk], w=["gsb"])
                            K.op(dve, lambda e, pu=pu, j=j: e.tensor_tensor(out=aT[:, j, 0:nt], in0=gsb[:, 0:nt], in1=pu[:, 0:nt],
                                                                            op=ALU.mult), r=["gsb", puk], w=[("aT", j)])
                    for n in range(4):
                        pts = [K.ps() for _ in subs]
                        for jq in range(4):
                            wd, wdk = wload(wd_bf[n, jq], 11, [("wd", n, jq)])
                            for si, ns in enumerate(subs):
                                pt, pk = pts[si]
                                for jj in range(11):
                                    j = jq * 11 + jj
                                    K.op(pe, lambda e, pt=pt, j=j, jj=jj, si=si, ns=ns, wd=wd: e.matmul(
                                        pt[0:ns, :], lhsT=aT[:, j, offs[si]:offs[si] + ns], rhs=wd[:, jj, :],
                                        start=(j == 0), stop=(j == NJ - 1)), r=[("aT", j), wdk], w=[pk])
                        for si, ns in enumerate(subs):
                            pt, pk = pts[si]
                            K.op(dve, lambda e, pt=pt, ns=ns, n=n: e.tensor_tensor(
                                out=tmp[0:ns, :], in0=pt[0:ns, :], in1=adaT1[0:ns, 1, n * 512:(n + 1) * 512], op=ALU.mult),
                                r=[pk, "adaT"], w=["tmp2"])
                            K.op(pool, lambda e, si=si, ns=ns, n=n: e.tensor_tensor(
                                out=xr[si][0:ns, n * 512:(n + 1) * 512], in0=xr[si][0:ns, n * 512:(n + 1) * 512],
                                in1=tmp[0:ns, :], op=ALU.add), r=["tmp2", ("xr", si)], w=[("xr", si)])
                    for si, ns in enumerate(subs):
                        xk = ("xr", si)
                        xn2 = xn2s[si % 2]; sq2 = xn2; st2 = st2s[si % 2]
                        kx2 = ("xn2", si % 2); ks2 = ("st2", si % 2)
                        K.op(act, lambda e, si=si, ns=ns: e.activation(out=sq2[0:ns, :], in_=xr[si][0:ns, :], func=AF.Square,
                                                                       accum_out=st2[0:ns, 0:1]), r=[xk], w=[kx2, ks2])
                        K.op(act, lambda e, ns=ns: e.activation(out=st2[0:ns, 1:2], in_=st2[0:ns, 0:1], func=AF.Sqrt,
                                                                bias=kc[0:ns, 0:1], scale=1.0 / D), r=[ks2, "kc"], w=[ks2])
                        K.op(dve, lambda e, ns=ns: e.reciprocal(out=st2[0:ns, 2:3], in_=st2[0:ns, 1:2]), r=[ks2], w=[ks2])
                        K.op(dve, lambda e, si=si, ns=ns: e.scalar_tensor_tensor(
                            out=xr[si][0:ns, :], in0=xr[si][0:ns, :], scalar=st2[0:ns, 2:3], in1=adaT1[0:ns, 3, :],
                            op0=ALU.mult, op1=ALU.mult), r=[xk, ks2, "adaT"], w=[xk])
                        K.op(pool, lambda e, si=si, ns=ns: e.tensor_tensor(out=xr[si][0:ns, :], in0=xr[si][0:ns, :],
                                                                           in1=adaT1[0:ns, 2, :], op=ALU.add),
                             r=[xk, "adaT"], w=[xk])
                        K.dma(sp, y2[tok0 + offs[si]: tok0 + offs[si] + ns, :], xr[si][0:ns, :], r=[xk], w=["y2"])
            K.barrier()
    return nc


def _consts(r):
    j = np.arange(128)[:, None]; i = np.arange(128)[None, :]
    ident = (i == j).astype(np.float32)
    triI = (i >= j).astype(np.float32)
    triS = (i > j).astype(np.float32)
    ones = np.ones((128, 128), np.float32)
    sel = [(np.broadcast_to(j == 32 * g, (128, 128))).astype(np.float32) for g in range(4)]
    bm16 = ((i // 16) == (j // 16)).astype(np.float32)
    lv = []
    for s_ in (16, 32, 64):
        mk = (((j // (2 * s_)) == (i // (2 * s_))) & ((j % (2 * s_)) >= s_) & ((i % (2 * s_)) < s_)).astype(np.float32)
        lv += [mk, mk.T.copy()]
    cmat = np.concatenate([ident, triI, triS, ones] + sel + [bm16] + lv, axis=1)
    iota = np.broadcast_to(np.arange(256, dtype=np.float32)[None, :], (128, 256))
    rmask = np.broadcast_to((np.arange(256) % 128 != 0).astype(np.float32)[None, :], (128, 256))
    inv = (1.0 / (10000.0 ** np.linspace(0.0, 1.0, 128, dtype=np.float32))).astype(np.float32)
    inv2pi = (inv.astype(np.float64) / (2 * np.pi)).astype(np.float32)[:, None]
    crow = np.concatenate([iota, rmask, inv2pi], axis=1).astype(np.float32)
    lg = math.log(1.0 - 2.0 ** (-5.0 - r))

    def rcs(C, reps):
        idx = np.arange(C, dtype=np.float64)
        xi = np.exp((idx + 1.0) * lg); ze = np.exp((C - 1.0 - idx) * lg) / 16.0
        dm = np.zeros((128, 128))
        jj = np.arange(C)[:, None]; ii = np.arange(C)[None, :]
        dm[:C, :C] = np.where(ii >= jj, np.exp(np.where(ii >= jj, ii - jj, 0) * lg), 0.0) / 16.0
        return (np.broadcast_to(np.tile(xi, reps)[None, :], (128, C * reps)), np.broadcast_to(np.tile(ze, reps)[None, :], (128, C * reps)),
                dm, np.full((128, 1), math.exp(C * lg)))
    a = rcs(128, 2); b_ = rcs(16, 1)
    rc = np.concatenate(list(a) + list(b_), axis=1).astype(np.float32)
    assert rc.shape == (128, 802)
    return cmat, crow, rc


def _chan(r):
    out = []
    for jh in range(2):
        h = 2 * r + jh
        for base in (0, 1024, 2048):
            out.append(base + h * 128 + np.arange(128))
    return np.stack(out)


def _wcols(r):
    cols = []
    for jh in range(2):
        h = 2 * r + jh
        for base in (0, 1024, 2048, 3072):
            cols.append(base + h * 128 + np.arange(128))
    rq = 4112 + r * 256 + np.arange(256); rk = 5136 + r * 256 + np.arange(256)
    rv = 6160 + r * 256 + np.arange(256); rg = 7184 + r * 256 + np.arange(256)
    cols += [rq[0::2], rq[1::2], rk[0::2], rk[1::2], rv[:128], rv[128:], rg[:128], rg[128:]]
    small = np.concatenate([np.full(32, 4096 + 2 * r), np.full(32, 4096 + 2 * r + 1),
                            np.full(32, 4104 + 2 * r), np.full(32, 4104 + 2 * r + 1)])
    cols.append(small)
    return np.concatenate(cols)


_ROWPERM = np.concatenate([np.arange(0, 256, 2), np.arange(1, 256, 2)])


def make_in_maps(inp, S):
    SEG = S // 4
    f = lambda a: np.ascontiguousarray(a, dtype=np.float32)
    maps = []
    w_in = inp["w_in"][0]; w_out = inp["w_out"][0]
    shared = dict(
        w_gu=f(inp["w_gu"][0]), w_down=f(inp["w_down"][0]), w_ada=f(inp["w_ada"][0]),
        b_ada_f=f(inp["b_ada"][0].reshape(96, 128).T[:, :]), b_ada_r=f(inp["b_ada"][0][None, :]),
        w_adaf=f(inp["w_ada_final"]), b_adaf_r=f(inp["b_ada_final"][None, :]),
        nmix=f(inp["norm_mix"][0].reshape(16, 128).T), nffn=f(inp["norm_ffn"][0].reshape(16, 128).T),
        nfin=f(np.broadcast_to(inp["norm_final"][None, :], (128, D))),
    )
    for c in range(8):
        b, r = c // 4, c % 4
        cmat, crow, rc = _consts(r)
        ch = _chan(r)
        rows_w = np.concatenate([np.concatenate([np.arange(256 * q, 256 * q + 256), 1024 + np.arange(256 * q, 256 * q + 256)])
                                 for q in range(4)])
        crows = [inp["c_prompt"][b]] + [inp["c_sample"][4 * b + i] for i in range(4)] + [inp["c_sample"][4 * b + r]]
        m = dict(shared)
        m.update(
            x1=f(inp["x_prompt"][b]), xs1=f(inp["x_sample"][4 * b:4 * b + 4].reshape(64, D)),
            x2=f(inp["x_prompt"][b, r * SEG:(r + 1) * SEG]), xs2=f(inp["x_sample"][4 * b + r]),
            cT=f(np.stack(crows, axis=1)), w_in_o=f(w_in[:, _wcols(r)]), w_out_p=f(w_out[rows_w, :]),
            cw=f(inp["conv_w"][0][:, ch].transpose(2, 1, 0).reshape(128, 24)),
            cst=f(inp["state_conv"][0, 4 * b:4 * b + 4][:, :, ch].transpose(0, 3, 2, 1).reshape(4, 128, 18)),
            dnc=f(np.concatenate([np.broadcast_to(inp["dn_a_log"][0, 2 * r:2 * r + 2][None, :], (128, 2)),
                                  np.broadcast_to(inp["dn_dt_bias"][0, 2 * r:2 * r + 2][None, :], (128, 2)),
                                  inp["dn_norm"][0][:, None]], axis=1)),
            sd=f(inp["state_delta"][0, 4 * b:4 * b + 4, 2 * r:2 * r + 2]),
            sr=f(inp["state_ret"][0, 4 * b:4 * b + 4, r][:, _ROWPERM, :]),
            cmat=cmat, crow=crow, rc=rc,
            gidx=np.ascontiguousarray(np.concatenate([
                r * (SEG // 256) * 2048 + (np.arange(16)[None, :] // 4) * 512 + (np.arange(16)[None, :] % 4) * 128 + np.arange(128)[:, None],
                (np.arange(16)[None, :] // 4) * 2048 + r * 512 + (np.arange(16)[None, :] % 4) * 128 + np.arange(128)[:, None]], axis=1),
                dtype=np.int32),
        )
        maps.append(m)
    return maps


def assemble(res, S):
    SEG = S // 4
    yp = np.zeros((2, S, D), np.float32); ys = np.zeros((8, 16, D), np.float32)
    cp = np.zeros((1, 2, 3, 3072), np.float32); dp = np.zeros((1, 2, 8, 128, 128), np.float32)
    rp = np.zeros((1, 2, 4, 256, 256), np.float32)
    cs = np.zeros((1, 8, 3, 3072), np.float32); ds = np.zeros((1, 8, 8, 128, 128), np.float32)
    rs = np.zeros((1, 8, 4, 256, 256), np.float32)
    for c in range(8):
        b, r = c // 4, c % 4
        o = res[c]
        yp[b, r * SEG:(r + 1) * SEG] = o["y2"][:SEG]
        ys[4 * b + r] = o["y2"][SEG:]
        ch = _chan(r)
        cv = o["convp"].reshape(128, 6, 3)
        for ci in range(6):
            cp[0, b][:, ch[ci]] = cv[:, ci, :].T
        dp[0, b, 2 * r:2 * r + 2] = o["deltap"]
        rp[0, b, r][_ROWPERM] = o["retp"]
        for i in range(4):
            cv = o["convs"][i].reshape(128, 6, 3)
            for ci in range(6):
                cs[0, 4 * b + i][:, ch[ci]] = cv[:, ci, :].T
            ds[0, 4 * b + i, 2 * r:2 * r + 2] = o["deltas"][i]
            rs[0, 4 * b + i, r][_ROWPERM] = o["rets"][i]
    return yp, ys, cp, dp, rp, cs, ds, rs


_NC_CACHE = {}


def kernel(**inputs):
    inp = {k: np.asarray(v) for k, v in inputs.items()}
    S = inp["x_prompt"].shape[1]
    if S not in _NC_CACHE:
        _NC_CACHE[S] = build(S)
    nc = _NC_CACHE[S]
    maps = make_in_maps(inp, S)
    res = run_bass_kernel_spmd(nc, maps, core_ids=list(range(8)))
    return assemble(res.results, S)
```
